# Optimizing a Trainium2 kernel written in Bass

```python
import math
import jax
import jax.numpy as jnp
from jax import lax
import numpy as np

D_MODEL = 1024
BATCH = 4
SEQ = 4096
DEPTH = 1

GRID_W = 64
CTX_LEN = 256
EPS = 1e-6
POS_BASE = 10000.0
N_MOD = 6

GLA_HEADS = 4
GLA_DK = 64
GLA_DV = 128
GLA_RANK = 16
GLA_GATE_NORM = 16.0
GLA_CHUNK = 64
GLA_QK_W = GLA_HEADS * GLA_DK
GLA_V_W = GLA_HEADS * GLA_DV

HY_WIDTH = D_MODEL - GLA_V_W
HY_ORDER = 2
HY_BANDS = 16
HY_EMB = 2 * HY_BANDS + 1
HY_HIDDEN = 64
HY_SHORT = 3
HY_MIN_DECAY = math.log(1e-2) / 1.5
HY_MAX_DECAY = math.log(1e-2) / 0.3
HY_FILTER_SCALE = 0.02

PROJ_SPLITS = (GLA_QK_W, 2 * GLA_QK_W, 2 * GLA_QK_W + GLA_V_W, 2 * GLA_QK_W + 2 * GLA_V_W,
               2 * GLA_QK_W + 2 * GLA_V_W + GLA_RANK, 2 * GLA_QK_W + 2 * GLA_V_W + 2 * GLA_RANK)
D_IN = 2 * GLA_QK_W + 2 * GLA_V_W + 2 * GLA_RANK + (HY_ORDER + 1) * HY_WIDTH

N_EXPERTS = 16
EC_FACTOR = 2
D_EXPERT = 1024

kernel_name = 'hybrid_gla_hyena_ec_prefix_dit'


def _rmsnorm(x, g):
    xf = x.astype(jnp.float32)
    y = xf * lax.rsqrt(jnp.mean(xf * xf, axis=-1, keepdims=True) + EPS)
    return (y * g.astype(jnp.float32)).astype(x.dtype)


def _modulate(h, shift, scale):
    return h * (1.0 + scale) + shift


def _rev(a):
    return a[:, ::-1]


def _pos_embed_2d(rows):
    r = jnp.repeat(jnp.arange(rows, dtype=jnp.float32), GRID_W)
    col = jnp.tile(jnp.arange(GRID_W, dtype=jnp.float32), rows)
    quarter = D_MODEL // 4
    omega = 1.0 / (POS_BASE ** (jnp.arange(quarter, dtype=jnp.float32) / quarter))
    er = r[:, None] * omega
    ec = col[:, None] * omega
    return jnp.concatenate([jnp.sin(er), jnp.cos(er), jnp.sin(ec), jnp.cos(ec)], axis=-1)


def _chunk(a):
    B, L, H, d = a.shape
    return a.reshape(B, L // GLA_CHUNK, GLA_CHUNK, H, d).transpose(0, 3, 1, 2, 4).astype(jnp.float32)


def _unchunk(o):
    B, H, N, C, d = o.shape
    return o.transpose(0, 2, 3, 1, 4).reshape(B, N * C, H, d)


def _gla_direction(q, k, v, la, s0, with_output):
    kc, vc, lac = _chunk(k), _chunk(v), _chunk(la)
    b = jnp.cumsum(lac, axis=3)
    b_last = b[:, :, :, -1]
    upd = jnp.einsum('bhncd,bhnce->bhnde', kc * jnp.exp(b_last[:, :, :, None] - b), vc)

    def step(s, inp):
        dec, u = inp
        return dec[..., None] * s + u, s

    s_fin, s_prev = lax.scan(step, s0, (jnp.moveaxis(jnp.exp(b_last), 2, 0), jnp.moveaxis(upd, 2, 0)))
    if not with_output:
        return None, s_fin
    s_prev = jnp.moveaxis(s_prev, 0, 2)
    qc = _chunk(q)
    ref = b[:, :, :, GLA_CHUNK // 2:GLA_CHUNK // 2 + 1]
    scores = jnp.einsum('bhncd,bhnsd->bhncs', qc * jnp.exp(b - ref), kc * jnp.exp(ref - b))
    mask = jnp.tril(jnp.ones((GLA_CHUNK, GLA_CHUNK), dtype=bool))
    scores = jnp.where(mask, scores, 0.0)
    o = (jnp.einsum('bhncs,bhnse->bhnce', scores, vc)
         + jnp.einsum('bhncd,bhnde->bhnce', qc * jnp.exp(b), s_prev))
    return _unchunk(o), s_fin


def _gla_bidir(q, k, v, la_f, la_b, s_f, s_b, with_output):
    o_f, sf = _gla_direction(q, k, v, la_f, s_f, with_output)
    o_b, sb = _gla_direction(_rev(q), _rev(k), _rev(v), _rev(la_b), s_b, with_output)
    o = o_f + _rev(o_b) if with_output else None
    return o, sf, sb


def _gla_out(o, g, norm_g):
    B, L = o.shape[:2]
    return _rmsnorm(o, norm_g).reshape(B, L, GLA_V_W) * jax.nn.silu(g)


def _short_conv(u, w, b):
    L = u.shape[1]
    pad = HY_SHORT // 2
    up = jnp.pad(u, ((0, 0), (pad, pad), (0, 0)))
    return sum(up[:, j:j + L] * w[j] for j in range(HY_SHORT)) + b


def _hyena_filters(L, w1, b1, w2, b2, w3, freq):
    t = jnp.linspace(0.0, 1.0, L, dtype=jnp.float32)[:, None]
    w = 2.0 * math.pi * jnp.arange(L, dtype=jnp.float32)[:, None] / L
    f = jnp.linspace(1e-4, HY_BANDS - 1, HY_BANDS, dtype=jnp.float32)[None, :]
    z = jnp.concatenate([t, jnp.cos(f * w), -jnp.sin(f * w)], axis=-1)
    hdn = jnp.sin(freq * (z @ w1 + b1))
    hdn = jnp.sin(freq * (hdn @ w2 + b2))
    h = (hdn @ w3).astype(jnp.float32).reshape(L, HY_ORDER, 2, HY_WIDTH)
    deltas = jnp.abs(jnp.linspace(HY_MIN_DECAY, HY_MAX_DECAY, HY_WIDTH, dtype=jnp.float32))
    h = h * jnp.exp(-t * deltas)[:, None, None, :]
    return jnp.transpose(h, (1, 2, 0, 3))


def _bidir_fftconv(u, hf, hb, bias):
    L = u.shape[1]
    uf = u.astype(jnp.float32)
    kern = jnp.concatenate([hf, jnp.zeros_like(hf[:1]), hb[:0:-1]], axis=0)
    y = jnp.fft.irfft(jnp.fft.rfft(uf, n=2 * L, axis=1) * jnp.fft.rfft(kern, axis=0), n=2 * L, axis=1)[:, :L]
    return (y + uf * bias.astype(jnp.float32)).astype(u.dtype)


def _hyena(hp, conv_w, conv_b, filt, bias):
    u = _short_conv(hp, conv_w, conv_b)
    v, x1, x2 = jnp.split(u, 3, axis=-1)
    z = x1 * _bidir_fftconv(v, filt[0, 0], filt[0, 1], bias[0])
    return x2 * _bidir_fftconv(z, filt[1, 0], filt[1, 1], bias[1])


def _ec_moe(h, w_router, w_gate, w_up, w_down):
    B, L, _ = h.shape
    cap = EC_FACTOR * L // N_EXPERTS
    aff = jax.nn.softmax((h @ w_router).astype(jnp.float32), axis=-1)
    vals, idx = lax.top_k(jnp.swapaxes(aff, 1, 2), cap)
    bidx = jnp.arange(B)[:, None, None]
    xs = h[bidx, idx]
    hid = jax.nn.silu(jnp.einsum('becd,edf->becf', xs, w_gate)) * jnp.einsum('becd,edf->becf', xs, w_up)
    y = jnp.einsum('becf,efd->becd', hid, w_down) * vals[..., None].astype(h.dtype)
    return jnp.zeros_like(h).at[bidx, idx].add(y.astype(h.dtype))


def setup_inputs(seed: int = 0) -> dict:
    key = jax.random.key(seed)
    ks = jax.random.split(key, 29)
    f32 = jnp.float32

    def nrm(i, shape, scale):
        return jax.random.normal(ks[i], shape, f32) * scale

    def gain(i, shape):
        return 1.0 + nrm(i, shape, 0.02)

    Dp = DEPTH
    return {
        'x': nrm(0, (BATCH, SEQ, D_MODEL), 1.0),
        'c': nrm(1, (BATCH, D_MODEL), 1.0),
        'ctx': nrm(2, (BATCH, CTX_LEN, D_MODEL), 1.0),
        'c_ctx': nrm(3, (D_MODEL,), 1.0),
        'w_ada': nrm(4, (Dp, D_MODEL, N_MOD * D_MODEL), 0.5 * D_MODEL ** -0.5),
        'b_ada': nrm(5, (Dp, N_MOD * D_MODEL), 0.02),
        'norm_mix_g': gain(6, (Dp, D_MODEL)),
        'w_in': nrm(7, (Dp, D_MODEL, D_IN), D_MODEL ** -0.5),
        'gla_wa_f': nrm(8, (Dp, GLA_RANK, GLA_QK_W), GLA_RANK ** -0.5),
        'gla_ba_f': nrm(9, (Dp, GLA_QK_W), 0.1),
        'gla_wa_b': nrm(10, (Dp, GLA_RANK, GLA_QK_W), GLA_RANK ** -0.5),
        'gla_ba_b': nrm(11, (Dp, GLA_QK_W), 0.1),
        'gla_norm_g': gain(12, (Dp, GLA_DV)),
        'hy_conv_w': nrm(13, (Dp, HY_SHORT, (HY_ORDER + 1) * HY_WIDTH), HY_SHORT ** -0.5),
        'hy_conv_b': nrm(14, (Dp, (HY_ORDER + 1) * HY_WIDTH), 0.02),
        'hy_w1': nrm(15, (Dp, HY_EMB, HY_HIDDEN), HY_EMB ** -0.5),
        'hy_b1': nrm(16, (Dp, HY_HIDDEN), 0.02),
        'hy_w2': nrm(17, (Dp, HY_HIDDEN, HY_HIDDEN), HY_HIDDEN ** -0.5),
        'hy_b2': nrm(18, (Dp, HY_HIDDEN), 0.02),
        'hy_w3': nrm(19, (Dp, HY_HIDDEN, HY_ORDER * 2 * HY_WIDTH), HY_FILTER_SCALE),
        'hy_freq': gain(20, (Dp, HY_HIDDEN)),
        'hy_bias': nrm(21, (Dp, HY_ORDER, HY_WIDTH), 0.5),
        'w_out': nrm(22, (Dp, D_MODEL, D_MODEL), D_MODEL ** -0.5),
        'norm_ffn_g': gain(23, (Dp, D_MODEL)),
        'w_router': nrm(24, (Dp, D_MODEL, N_EXPERTS), D_MODEL ** -0.5),
        'w_gate': nrm(25, (Dp, N_EXPERTS, D_MODEL, D_EXPERT), D_MODEL ** -0.5),
        'w_up': nrm(26, (Dp, N_EXPERTS, D_MODEL, D_EXPERT), D_MODEL ** -0.5),
        'w_down': nrm(27, (Dp, N_EXPERTS, D_EXPERT, D_MODEL), D_EXPERT ** -0.5),
        'norm_final_g': gain(28, (D_MODEL,)),
    }


def reference(x, c, ctx, c_ctx, w_ada, b_ada, norm_mix_g, w_in, gla_wa_f, gla_ba_f, gla_wa_b, gla_ba_b,
              gla_norm_g, hy_conv_w, hy_conv_b, hy_w1, hy_b1, hy_w2, hy_b2, hy_w3, hy_freq, hy_bias,
              w_out, norm_ffn_g, w_router, w_gate, w_up, w_down, norm_final_g):
    B, n_lat, _ = x.shape
    ROWS = n_lat // GRID_W
    x = x + _pos_embed_2d(ROWS).astype(x.dtype)
    s_zero = jnp.zeros((B, GLA_HEADS, GLA_DK, GLA_DV), jnp.float32)
    for l in range(DEPTH):
        last = l == DEPTH - 1
        mod_x = jnp.split((jax.nn.silu(c) @ w_ada[l] + b_ada[l])[:, None, :], N_MOD, axis=-1)
        mod_c = jnp.split((jax.nn.silu(c_ctx) @ w_ada[l] + b_ada[l])[None, None, :], N_MOD, axis=-1)

        def project(h, mod):
            p = _modulate(_rmsnorm(h, norm_mix_g[l]), mod[0], mod[1]) @ w_in[l]
            q, k, v, g, a_f, a_b, hy = jnp.split(p, PROJ_SPLITS, axis=-1)
            Bh, L, _ = p.shape
            shp = (Bh, L, GLA_HEADS, -1)
            la_f = jax.nn.log_sigmoid((a_f @ gla_wa_f[l] + gla_ba_f[l]).astype(jnp.float32)) / GLA_GATE_NORM
            la_b = jax.nn.log_sigmoid((a_b @ gla_wa_b[l] + gla_ba_b[l]).astype(jnp.float32)) / GLA_GATE_NORM
            gla_in = (q.reshape(shp) * (GLA_DK ** -0.5), k.reshape(shp), v.reshape(shp),
                      la_f.reshape(shp), la_b.reshape(shp))
            return gla_in, g, hy

        def merge(o, g, hy):
            filt = _hyena_filters(hy.shape[1], hy_w1[l], hy_b1[l], hy_w2[l], hy_b2[l], hy_w3[l], hy_freq[l])
            y_gla = _gla_out(o, g, gla_norm_g[l])
            y_hy = _hyena(hy, hy_conv_w[l], hy_conv_b[l], filt, hy_bias[l])
            return jnp.concatenate([y_gla, y_hy], axis=-1) @ w_out[l]

        def channel_mix(h, mod):
            hn = _modulate(_rmsnorm(h, norm_ffn_g[l]), mod[3], mod[4])
            return mod[5] * _ec_moe(hn, w_router[l], w_gate[l], w_up[l], w_down[l])

        gla_c, g_c, hy_c = project(ctx, mod_c)
        gla_x, g_x, hy_x = project(x, mod_x)
        o_c, s_f, s_b = _gla_bidir(*gla_c, s_zero, s_zero, not last)
        o_x, _, _ = _gla_bidir(*gla_x, s_f, s_b, True)
        x = x + mod_x[2] * merge(o_x, g_x, hy_x)
        x = x + channel_mix(x, mod_x)
        if not last:
            ctx = ctx + mod_c[2] * merge(o_c, g_c, hy_c)
            ctx = ctx + channel_mix(ctx, mod_c)
    return _rmsnorm(x, norm_final_g)
```

```python
import math
from contextlib import ExitStack
import numpy as np
import concourse.bass as bass
import concourse.mybir as mybir
from concourse.bass_utils import run_bass_kernel_spmd

F32 = mybir.dt.float32
BF16 = mybir.dt.bfloat16
AF = mybir.ActivationFunctionType
ALU = mybir.AluOpType
AX = mybir.AxisListType

D = 1024
L = 4096
LC = 256
NT = L // 128
NTC = LC // 128
DIN = 3104
EPS = 1e-6
NF = 33
NE = 16
CAP = 512


class Buf:
    __slots__ = ("w", "r")

    def __init__(self):
        self.w = None
        self.r = {}


class Sch:
    def __init__(self, nc, es):
        self.nc = nc
        self.eng = {"pe": nc.tensor, "act": nc.scalar, "dve": nc.vector, "pool": nc.gpsimd, "sp": nc.sync}
        self.sem = {k: es.enter_context(nc.semaphore("s_" + k)) for k in self.eng}
        self.cnt = {k: 0 for k in self.eng}
        self.seen = {k: {} for k in self.eng}
        self.NDS = 24
        self.dsem = [es.enter_context(nc.semaphore("d%d" % i)) for i in range(self.NDS)]
        self.dcnt = [0] * self.NDS
        self.dnext = 0
        self.csem = es.enter_context(nc.semaphore("s_cc"))
        self.ccnt = 0

    def _wait(self, e, tok):
        if tok is None:
            return
        kind, key, val = tok
        if kind == "e" and key == e and e == "pe":
            return
        sk = (kind, key)
        if self.seen[e].get(sk, 0) >= val:
            return
        sem = self.sem[key] if kind == "e" else (self.dsem[key] if kind == "d" else self.csem)
        self.eng[e].wait_ge(sem, val)
        self.seen[e][sk] = val

    def _deps(self, e, reads, writes):
        for b in reads:
            self._wait(e, b.w)
        for b in writes:
            self._wait(e, b.w)
            for t in list(b.r.values()):
                self._wait(e, t)

    def op(self, e, fn, reads=(), writes=()):
        self._deps(e, reads, writes)
        inst = fn(self.eng[e])
        self.cnt[e] += 1
        inst.then_inc(self.sem[e], 1)
        tok = ("e", e, self.cnt[e])
        for b in reads:
            b.r[e] = tok
        for b in writes:
            b.w = tok
            b.r = {}
        return tok

    def dma(self, fn, reads=(), writes=(), q="sp"):
        i = self.dnext
        self.dnext = (i + 1) % self.NDS
        if self.dcnt[i] > 0:
            self._wait(q, ("d", i, self.dcnt[i]))
        self._deps(q, reads, writes)
        inst = fn(self.eng[q])
        self.dcnt[i] += 16
        inst.then_inc(self.dsem[i], 16)
        tok = ("d", i, self.dcnt[i])
        for b in reads:
            b.r[("d", i)] = tok
        for b in writes:
            b.w = tok
            b.r = {}
        return tok

    def coll(self, fn, reads=(), writes=()):
        self._deps("pool", reads, writes)
        inst = fn(self.eng["pool"])
        self.ccnt += 1
        inst.then_inc(self.csem)
        tok = ("c", 0, self.ccnt)
        for b in reads:
            b.r[("c", 0)] = tok
        for b in writes:
            b.w = tok
            b.r = {}
        return tok

    def barrier(self, wait_coll=False):
        for e in ("pe", "act", "dve", "pool", "sp"):
            for f in ("pe", "act", "dve", "pool"):
                if self.cnt[f]:
                    self._wait(e, ("e", f, self.cnt[f]))
            for i in range(self.NDS):
                if self.dcnt[i]:
                    self._wait(e, ("d", i, self.dcnt[i]))
            if self.ccnt and wait_coll:
                self._wait(e, ("c", 0, self.ccnt))


def host_consts():
    f32 = np.float32
    rows = L // 64
    r = np.repeat(np.arange(rows, dtype=f32), 64)
    col = np.tile(np.arange(64, dtype=f32), rows)
    quarter = D // 4
    omega = (1.0 / (np.float32(10000.0) ** (np.arange(quarter, dtype=f32) / np.float32(quarter)))).astype(f32)
    er = r[:, None] * omega
    ec = col[:, None] * omega
    pos = np.concatenate([np.sin(er), np.cos(er), np.sin(ec), np.cos(ec)], axis=-1).astype(f32)
    ident = np.eye(128, dtype=f32)
    si = np.arange(128)[:, None]
    ci = np.arange(128)[None, :]
    same = (si // 64) == (ci // 64)
    maskF = (same & (si <= ci)).astype(f32)
    maskB = (same & (si >= ci)).astype(f32)
    import ml_dtypes
    bf = ml_dtypes.bfloat16
    def zfeat(idx):
        t = (idx / np.float32(L - 1)).astype(f32)[:, None]
        w = (2.0 * math.pi * idx[:, None] / L).astype(f32)
        fb = np.linspace(1e-4, 15, 16, dtype=f32)[None, :]
        return np.concatenate([t, np.cos(fb * w), -np.sin(fb * w)], axis=-1).astype(f32), t[:, 0]
    idx = np.arange(L, dtype=f32)
    z0, t0_ = zfeat(idx)
    z1, t1_ = zfeat(idx + 1.0)
    z0[:, 0] = np.linspace(0.0, 1.0, L, dtype=f32)
    t0_ = np.linspace(0.0, 1.0, L, dtype=f32)
    zT = np.ascontiguousarray(np.stack([z0.T, z1.T], axis=1))
    tnorm = np.ascontiguousarray(np.stack([t0_.reshape(NT, 128).T, t1_.reshape(NT, 128).T], axis=1))
    deltas = np.abs(np.linspace(math.log(1e-2) / 1.5, math.log(1e-2) / 0.3, 512, dtype=f32)).astype(f32)
    Nb, Lb, FCn = 2048, 1024, 9
    fidx = np.arange(FCn * 128)
    fs = np.where(fidx > Lb, 0.0, np.where((fidx == 0) | (fidx == Lb), 1.0 / Nb, 2.0 / Nb))
    sg = np.where(fidx % 2 == 0, 1.0, -1.0)
    th = 2.0 * np.pi / Nb
    cfv, sfv = np.cos(th * fidx), np.sin(th * fidx)
    hcoef = np.stack([fs, sg, sg * cfv, sg * sfv, -sg * sfv, -sg * cfv], axis=0)
    hcoef = np.ascontiguousarray(hcoef.reshape(6, FCn, 128).transpose(2, 0, 1)).astype(f32)
    fscale = np.zeros((128, NF), dtype=f32)
    ang = th * np.arange(Nb, dtype=np.float64)
    ctab, stab = np.cos(ang), np.sin(ang)
    a_ = np.arange(FCn * 128, dtype=np.int64)
    prod = (a_[:, None] * a_[None, :]) % Nb
    dft = np.empty((128, FCn, 2, FCn * 128), dtype=bf)
    dft[:, :, 0, :] = ctab[prod].astype(f32).reshape(FCn, 128, FCn * 128).transpose(1, 0, 2).astype(bf)
    dft[:, :, 1, :] = stab[prod].astype(f32).reshape(FCn, 128, FCn * 128).transpose(1, 0, 2).astype(bf)
    ltri = (np.arange(128)[:, None] < np.arange(128)[None, :]).astype(bf)
    iotaJ = np.tile(np.arange(512, dtype=f32)[None, :], (128, 1))
    jvec = (np.arange(128, dtype=f32)[:, None] + 128.0 * np.arange(4, dtype=f32)[None, :]).astype(f32)
    selE = np.tile(np.arange(16, dtype=f32)[:, None], (1, 128))
    eoff = np.tile((np.arange(NE, dtype=f32) * CAP - NE * CAP)[None, :], (128, 1)).astype(f32)
    tt_ = (np.arange(NT)[None, :] * 128 + np.arange(128)[:, None])
    tidx = np.stack([tt_ // 64, tt_ % 64], axis=-1).astype(bf)
    return {"pos": pos, "ident": ident, "maskF": maskF, "maskB": maskB, "zT": zT, "tnorm": tnorm, "deltas": deltas,
            "fscale": fscale, "dft": dft, "hcoef": hcoef, "ltri": ltri, "iotaJ": iotaJ, "jvec": jvec, "selE": selE, "tidx": tidx, "eoff": eoff}


class Ctx:
    pass


def build(upto=99, taps=()):
    nc = bass.Bass("TRN2", target_bir_lowering=False)
    es = ExitStack()
    S = Sch(nc, es)
    K = Ctx()
    K.nc, K.S, K.es = nc, S, es
    K.taps = {}
    K.tap_names = taps

    def din(name, shape, dt=F32):
        return nc.dram_tensor(name, list(shape), dt, kind="ExternalInput").ap()

    I = {}
    I["x"] = din("x", [L, D])
    I["ctx"] = din("ctx", [LC, D])
    I["pos"] = din("pos", [L, D])
    I["ident"] = din("ident", [128, 128])
    I["cc"] = din("cc", [128, 16])
    I["w_ada"] = din("w_ada", [D, 6 * D])
    I["b_ada"] = din("b_ada", [6 * D])
    I["norm_mix_g"] = din("norm_mix_g", [D])
    I["norm_ffn_g"] = din("norm_ffn_g", [D])
    I["norm_final_g"] = din("norm_final_g", [D])
    I["w_gl"] = din("w_gl", [D, 800])
    I["wa_f"] = din("wa_f", [16, 128])
    I["wa_b"] = din("wa_b", [16, 128])
    I["gla_ba"] = din("gla_ba", [128, 2])
    I["gla_norm_g"] = din("gla_norm_g", [128])
    I["hy_conv_w"] = din("hy_conv_w", [3, 768])
    I["hy_conv_b"] = din("hy_conv_b", [768])
    I["w_hy"] = din("w_hy", [D, 768])
    I["rmask"] = din("rmask", [128, 2])
    I["maskF"] = din("maskF", [128, 128])
    I["maskB"] = din("maskB", [128, 128])
    I["w_out"] = din("w_out", [D, D])
    I["zT"] = din("zT", [33, 2, L])
    I["hy_w1"] = din("hy_w1", [33, 64])
    I["hy_w2"] = din("hy_w2", [64, 64])
    I["hy_w3"] = din("hy_w3", [64, 1024])
    I["hy_fb"] = din("hy_fb", [64, 4])
    I["hy_bias"] = din("hy_bias", [2, 256])
    I["tnorm"] = din("tnorm", [128, 2, NT])
    I["deltas"] = din("deltas", [256])
    I["fscale"] = din("fscale", [128, NF])
    I["dft"] = din("dft", [128, 9, 2, 1152], BF16)
    I["hcoef"] = din("hcoef", [128, 6, 9])
    I["w_router"] = din("w_router", [D, NE])
    I["w_gate"] = din("w_gate", [NE // 2, D, D])
    I["w_up"] = din("w_up", [NE // 2, D, D])
    I["w_down"] = din("w_down", [NE // 2, D, D])
    I["ltri"] = din("ltri", [128, 128], BF16)
    I["iotaJ"] = din("iotaJ", [128, 512])
    I["jvec"] = din("jvec", [128, 4])
    I["eoff"] = din("eoff", [128, NE])
    I["tidx"] = din("tidx", [128, NT, 2], BF16)
    I["selE"] = din("selE", [16, 128])

    def scratch(name, shape, dt=F32):
        kind = "ExternalOutput" if name in taps else "Internal"
        return nc.dram_tensor(("tap_" if name in taps else "") + name, list(shape), dt, kind=kind).ap()

    K.scratch = scratch
    out = nc.dram_tensor("out", [L, D], F32, kind="ExternalOutput").ap()
    K.I, K.out = I, out

    def sb(name, shape, dt=F32, stack=None):
        return (stack or es).enter_context(nc.sbuf_tensor("sb_" + name, list(shape), dt))

    K.sb = sb
    K.ps = [es.enter_context(nc.psum_tensor("ps%d" % i, [128, 512], F32)) for i in range(8)]
    K.psb = [Buf() for _ in range(8)]
    K.psi = 0

    def next_ps():
        i = K.psi
        K.psi = (i + 1) % 8
        return K.ps[i], K.psb[i]

    K.next_ps = next_ps

    def tap(name, ap_sb, buf, shape, dt=F32):
        if name not in K.tap_names:
            return
        t = nc.dram_tensor("tap_" + name, list(shape), dt, kind="ExternalOutput").ap()
        S.dma(lambda q: q.dma_start(out=t, in_=ap_sb), reads=[buf])
        K.taps[name] = t

    K.tap = tap

    phase_a(K)
    if upto >= 2:
        phase_b(K)
    if upto >= 3:
        phase_c(K)
        K.es_bc.close()
    if upto >= 4:
        phase_g(K)
    if upto >= 5:
        phase_gn(K)
    if upto >= 6:
        phase_h(K)
    if upto >= 7:
        phase_e(K)
    if upto >= 8:
        phase_f(K)
    S.barrier(wait_coll=True)
    return nc, K


def phase_a(K):
    nc, S, I = K.nc, K.S, K.I
    K.identt = K.sb("identt", [128, 128])
    K.es_bc = ExitStack()
    K.modx = K.sb("modx", [128, 6 * D], stack=K.es_bc)
    K.modc = K.sb("modc", [128, 2 * D], stack=K.es_bc)
    K.b_modx = [Buf() for _ in range(12)]
    K.b_modc = [Buf() for _ in range(4)]
    K.b_ident = Buf()
    S.dma(lambda q: q.dma_start(out=K.identt[:], in_=I["ident"]), writes=[K.b_ident])
    with ExitStack() as ph:
        cc = K.sb("cc", [128, 16], stack=ph)
        sc = K.sb("sc", [128, 16], stack=ph)
        rep = K.sb("rep", [128, 16, 128], stack=ph)
        bada = K.sb("bada", [128, 6 * D], stack=ph)
        wst = [K.sb("wst%d" % i, [128, 8, 512], stack=ph) for i in range(2)]
        b_cc, b_sc, b_rep, b_bada = Buf(), Buf(), Buf(), Buf()
        b_wst = [Buf(), Buf()]
        S.dma(lambda q: q.dma_start(out=cc[:], in_=I["cc"]), writes=[b_cc])
        S.dma(lambda q: q.dma_start(out=bada[:], in_=I["b_ada"].partition_broadcast(128)), writes=[b_bada])
        S.op("act", lambda e: e.activation(out=sc[:], in_=cc[:], func=AF.Silu), reads=[b_cc], writes=[b_sc])
        S.op("dve", lambda e: e.tensor_copy(out=rep[:], in_=sc[:].unsqueeze(2).to_broadcast([128, 16, 128])),
             reads=[b_sc], writes=[b_rep])
        wv = I["w_ada"].rearrange("(kc p) j -> p kc j", p=128)
        for jc in range(12):
            w, bw = wst[jc % 2], b_wst[jc % 2]
            S.dma(lambda q: q.dma_start(out=w[:], in_=wv[:, :, jc * 512:(jc + 1) * 512]), writes=[bw])
            for s in range(2):
                if s == 1 and jc >= 4:
                    continue
                ps, pb = K.next_ps()
                for kc in range(8):
                    S.op("pe", lambda e: e.matmul(ps[:], lhsT=rep[:, s * 8 + kc, :], rhs=w[:, kc, :],
                                                  start=(kc == 0), stop=(kc == 7)),
                         reads=[b_rep, bw], writes=[pb])
                dst = (K.modx if s == 0 else K.modc)
                db = (K.b_modx if s == 0 else K.b_modc)[jc]
                S.op("dve", lambda e: e.tensor_tensor(out=dst[:, jc * 512:(jc + 1) * 512], in0=ps[:],
                                                      in1=bada[:, jc * 512:(jc + 1) * 512], op=ALU.add),
                     reads=[pb, b_bada], writes=[db])
        gm = K.sb("gm", [128, D], stack=ph)
        gf = K.sb("gf", [128, D], stack=ph)
        b_gm, b_gf = Buf(), Buf()
        S.dma(lambda q: q.dma_start(out=gm[:], in_=I["norm_mix_g"].partition_broadcast(128)), writes=[b_gm])
        S.dma(lambda q: q.dma_start(out=gf[:], in_=I["norm_ffn_g"].partition_broadcast(128)), writes=[b_gf])
        for (t, bl, lo, g, bg) in ((K.modx, K.b_modx, 1, gm, b_gm), (K.modx, K.b_modx, 4, gf, b_gf),
                                   (K.modc, K.b_modc, 1, gm, b_gm)):
            for h in range(2):
                sl = slice(lo * D + h * 512, lo * D + (h + 1) * 512)
                S.op("dve", lambda e: e.scalar_tensor_tensor(out=t[:, sl], in0=t[:, sl], scalar=1.0,
                                                             in1=g[:, h * 512:(h + 1) * 512],
                                                             op0=ALU.add, op1=ALU.mult),
                     reads=[bl[lo * 2 + h], bg], writes=[bl[lo * 2 + h]])
        K.tap("modx", K.modx[:], K.b_modx[11], [128, 6 * D])
        K.tap("modc", K.modc[:], K.b_modc[3], [128, 2 * D])
        S.barrier()


def phase_b(K):
    nc, S, I = K.nc, K.S, K.I
    NCOL = L + 2 + LC
    K.hnT = K.sb("hnT", [128, 8, NCOL], BF16, stack=K.es_bc)
    K.b_hnT = [Buf() for _ in range(NT + NTC)]
    b_pad = Buf()
    S.op("pool", lambda e: e.memset(K.hnT[:, :, 0:1], 0.0), writes=[b_pad])
    S.op("pool", lambda e: e.memset(K.hnT[:, :, L + 1:L + 2], 0.0), writes=[b_pad])
    with ExitStack() as ph:
        xt = [K.sb("xt%d" % i, [128, D], stack=ph) for i in range(2)]
        pt = [K.sb("pt%d" % i, [128, D], stack=ph) for i in range(2)]
        hn = [K.sb("hn%d" % i, [128, D], stack=ph) for i in range(2)]
        junk = K.sb("junk", [128, D], stack=ph)
        st = K.sb("st", [128, 4 * (NT + NTC)], stack=ph)
        b_xt, b_pt, b_hn = [Buf(), Buf()], [Buf(), Buf()], [Buf(), Buf()]
        b_junk, b_st = Buf(), Buf()
        for ti in range(NT + NTC):
            isx = ti < NT
            k = ti % 2
            x_, p_, h_ = xt[k], pt[k], hn[k]
            src = I["x"][ti * 128:(ti + 1) * 128, :] if isx else I["ctx"][(ti - NT) * 128:(ti - NT + 1) * 128, :]
            S.dma(lambda q: q.dma_start(out=x_[:], in_=src), writes=[b_xt[k]])
            if isx:
                S.dma(lambda q: q.dma_start(out=p_[:], in_=I["pos"][ti * 128:(ti + 1) * 128, :]), writes=[b_pt[k]])
                S.op("pool", lambda e: e.tensor_tensor(out=x_[:], in0=x_[:], in1=p_[:], op=ALU.add),
                     reads=[b_pt[k]], writes=[b_xt[k]])
            ss = st[:, 4 * ti:4 * ti + 1]
            rs = st[:, 4 * ti + 1:4 * ti + 2]
            S.op("act", lambda e: e.activation(out=junk[:], in_=x_[:], func=AF.Square, accum_out=ss),
                 reads=[b_xt[k]], writes=[b_junk, b_st])
            S.op("act", lambda e: e.activation(out=rs, in_=ss, func=AF.Sqrt, scale=1.0 / D, bias=EPS),
                 reads=[b_st], writes=[b_st])
            S.op("dve", lambda e: e.reciprocal(out=rs, in_=rs), reads=[b_st], writes=[b_st])
            G = K.modx[:, D:2 * D] if isx else K.modc[:, D:2 * D]
            Sh = K.modx[:, 0:D] if isx else K.modc[:, 0:D]
            S.op("dve", lambda e: e.scalar_tensor_tensor(out=h_[:], in0=x_[:], scalar=rs, in1=G,
                                                         op0=ALU.mult, op1=ALU.mult),
                 reads=[b_xt[k], b_st], writes=[b_hn[k]])
            S.op("pool", lambda e: e.tensor_tensor(out=h_[:], in0=h_[:], in1=Sh, op=ALU.add),
                 reads=[], writes=[b_hn[k]])
            c0 = 1 + ti * 128 if isx else L + 2 + (ti - NT) * 128
            for hh in range(2):
                ps, pb = K.next_ps()
                for j in range(4):
                    kc = hh * 4 + j
                    S.op("pe", lambda e: e.transpose(out=ps[:, j * 128:(j + 1) * 128],
                                                     in_=h_[:, kc * 128:(kc + 1) * 128], identity=K.identt[:]),
                         reads=[b_hn[k], K.b_ident], writes=[pb])
                S.op("act", lambda e: e.activation(out=K.hnT[:, hh * 4:hh * 4 + 4, c0:c0 + 128],
                                                   in_=ps[:].rearrange("p (a b) -> p a b", a=4), func=AF.Copy),
                     reads=[pb], writes=[K.b_hnT[ti]])
        K.tap("hnT", K.hnT[:], K.b_hnT[NT + NTC - 1], [128, 8, NCOL], BF16)
        K.sc_mod = K.scratch("sc_mod", [128, 6 * D])
        K.b_scmod = Buf()
        S.dma(lambda q: q.dma_start(out=K.sc_mod, in_=K.modx[:]), writes=[K.b_scmod])
        S.barrier()


def make_in_maps(inputs):
    hc = host_consts()
    maps = []
    for core in range(8):
        b = core // 2
        r = core % 2
        cc = np.stack([np.asarray(inputs["c"][b]), np.asarray(inputs["c_ctx"])]).reshape(2, 8, 128)
        cc = np.ascontiguousarray(cc.transpose(2, 0, 1).reshape(128, 16)).astype(np.float32)
        m = {
            "x": np.ascontiguousarray(inputs["x"][b]),
            "ctx": np.ascontiguousarray(inputs["ctx"][b]),
            "pos": hc["pos"],
            "ident": hc["ident"],
            "cc": cc,
            "w_ada": np.ascontiguousarray(inputs["w_ada"][0]),
            "b_ada": np.ascontiguousarray(inputs["b_ada"][0]),
            "norm_mix_g": np.ascontiguousarray(inputs["norm_mix_g"][0]),
            "norm_ffn_g": np.ascontiguousarray(inputs["norm_ffn_g"][0]),
            "norm_final_g": np.ascontiguousarray(inputs["norm_final_g"]),
            "w_gl": np.ascontiguousarray(np.concatenate([inputs["w_in"][0][:, r * 128:(r + 1) * 128],
                                                         inputs["w_in"][0][:, 256 + r * 128:256 + (r + 1) * 128],
                                                         inputs["w_in"][0][:, 512 + r * 256:512 + (r + 1) * 256],
                                                         inputs["w_in"][0][:, 1024 + r * 256:1024 + (r + 1) * 256],
                                                         inputs["w_in"][0][:, 1536:1568]], axis=1)),
            "wa_f": np.ascontiguousarray(inputs["gla_wa_f"][0][:, r * 128:(r + 1) * 128]),
            "wa_b": np.ascontiguousarray(inputs["gla_wa_b"][0][:, r * 128:(r + 1) * 128]),
            "gla_ba": np.ascontiguousarray(np.stack([inputs["gla_ba_f"][0][r * 128:(r + 1) * 128],
                                                     inputs["gla_ba_b"][0][r * 128:(r + 1) * 128]], axis=1)),
            "gla_norm_g": np.ascontiguousarray(inputs["gla_norm_g"][0]),
            "hy_conv_w": np.ascontiguousarray(np.concatenate([inputs["hy_conv_w"][0][:, g * 512 + r * 256:g * 512 + (r + 1) * 256] for g in range(3)], axis=1)),
            "hy_conv_b": np.ascontiguousarray(np.concatenate([inputs["hy_conv_b"][0][g * 512 + r * 256:g * 512 + (r + 1) * 256] for g in range(3)])),
            "w_hy": np.ascontiguousarray(np.concatenate([inputs["w_in"][0][:, 1568 + g * 512 + r * 256:1568 + g * 512 + (r + 1) * 256] for g in range(3)], axis=1)),
            "rmask": np.ascontiguousarray(np.tile(np.array([[1.0 - r, float(r)]], dtype=np.float32), (128, 1))),
            "maskF": hc["maskF"],
            "maskB": hc["maskB"],
            "w_out": np.ascontiguousarray(inputs["w_out"][0]),
            "zT": hc["zT"], "tnorm": hc["tnorm"], "deltas": np.ascontiguousarray(hc["deltas"][r * 256:(r + 1) * 256]), "fscale": hc["fscale"], "dft": hc["dft"], "hcoef": hc["hcoef"],
            "ltri": hc["ltri"], "iotaJ": hc["iotaJ"], "jvec": hc["jvec"], "selE": hc["selE"], "tidx": hc["tidx"], "eoff": hc["eoff"],
            "hy_w1": np.ascontiguousarray(inputs["hy_w1"][0]),
            "hy_w2": np.ascontiguousarray(inputs["hy_w2"][0]),
            "hy_w3": np.ascontiguousarray(np.concatenate([inputs["hy_w3"][0][:, od * 512 + r * 256:od * 512 + (r + 1) * 256] for od in range(4)], axis=1)),
            "hy_fb": np.ascontiguousarray(np.stack([inputs["hy_freq"][0], inputs["hy_b1"][0], inputs["hy_b2"][0],
                                                    inputs["hy_b2"][0]], axis=1)),
            "hy_bias": np.ascontiguousarray(inputs["hy_bias"][0][:, r * 256:(r + 1) * 256]),
            "w_router": np.ascontiguousarray(np.concatenate([inputs["w_router"][0][:, r * 8:(r + 1) * 8],
                                                             inputs["w_router"][0][:, (1 - r) * 8:(2 - r) * 8]], axis=1)),
            "w_gate": np.ascontiguousarray(inputs["w_gate"][0][r * 8:(r + 1) * 8]),
            "w_up": np.ascontiguousarray(inputs["w_up"][0][r * 8:(r + 1) * 8]),
            "w_down": np.ascontiguousarray(inputs["w_down"][0][r * 8:(r + 1) * 8]),
        }
        maps.append(m)
    return maps


def kernel(**inputs):
    inputs = {k: np.asarray(v) for k, v in inputs.items()}
    nc, K = build()
    maps = make_in_maps(inputs)
    res = run_bass_kernel_spmd(nc, maps, core_ids=list(range(8)))
    outs = [res.results[2 * b]["out"] for b in range(4)]
    return np.stack(outs).astype(np.float32)


def tokcol(ti):
    return 1 + ti * 128 if ti < NT else L + 2 + (ti - NT) * 128


def phase_c(K):
    nc, S, I = K.nc, K.S, K.I
    K.sc_qk = K.scratch("sc_qk", [256, L + LC])
    K.sc_a = K.scratch("sc_a", [32, L + LC])
    K.sc_v = K.scratch("sc_v", [L + LC, 256], BF16)
    K.sc_sg = K.scratch("sc_sg", [L, 256])
    K.sc_u = K.scratch("sc_u", [L, 768])
    K.b_scr = Buf()
    with ExitStack() as ph:
        wb = [K.sb("wb%d" % i, [128, 8, 512], BF16, stack=ph) for i in range(2)]
        b_wb = [Buf(), Buf()]
        wj = [K.sb("wj%d" % j, [128, 8, 512], BF16, stack=ph) for j in range(3)]
        b_wj = [Buf(), Buf(), Buf()]
        cw = K.sb("cw", [128, 3, 512], stack=ph)
        cb = K.sb("cb", [128, 512], stack=ph)
        b_cw, b_cb = Buf(), Buf()
        stg = [K.sb("stg%d" % i, [128, 512], stack=ph) for i in range(3)]
        b_stg = [Buf(), Buf(), Buf()]
        stgb = [K.sb("stgb%d" % i, [128, 512], BF16, stack=ph) for i in range(2)]
        b_stgb = [Buf(), Buf()]
        wv = I["w_gl"].rearrange("(kc p) j -> p kc j", p=128)
        cnt = {"w": 0, "s": 0, "sb": 0}

        def loadw(c0, n):
            k = cnt["w"] % 2
            cnt["w"] += 1
            S.dma(lambda q: q.dma_start(out=wb[k][:, :, 0:n], in_=wv[:, :, c0:c0 + n]), writes=[b_wb[k]], q="pool")
            return wb[k], b_wb[k]

        def nstg():
            k = cnt["s"] % 3
            cnt["s"] += 1
            return stg[k], b_stg[k]

        allb = K.b_hnT
        tgs = [(1 + g * 512, g * 512, 512) for g in range(8)] + [(L + 2, L, 256)]
        for (c0w, nw, dst, rows) in ((0, 256, K.sc_qk, 128), (768, 32, K.sc_a, 32)):
            w, bw = loadw(c0w, nw)
            for cch in range(nw // rows):
                for (hc0, t0, n) in tgs:
                    ps, pb = K.next_ps()
                    for kc in range(8):
                        S.op("pe", lambda e: e.matmul(ps[0:rows, 0:n], lhsT=w[:, kc, cch * rows:(cch + 1) * rows],
                                                      rhs=K.hnT[:, kc, hc0:hc0 + n], start=(kc == 0), stop=(kc == 7)),
                             reads=[bw] + allb, writes=[pb])
                    st_, bs_ = nstg()
                    S.op("act", lambda e: e.activation(out=st_[0:rows, 0:n], in_=ps[0:rows, 0:n], func=AF.Copy),
                         reads=[pb], writes=[bs_])
                    S.dma(lambda q: q.dma_start(out=dst[cch * rows:(cch + 1) * rows, t0:t0 + n], in_=st_[0:rows, 0:n]),
                          reads=[bs_], writes=[K.b_scr])
        w, bw = loadw(256, 512)
        for ti in range(NT + NTC):
            col = tokcol(ti)
            ps, pb = K.next_ps()
            nn = 512 if ti < NT else 256
            for kc in range(8):
                S.op("pe", lambda e: e.matmul(ps[:, 0:nn], lhsT=K.hnT[:, kc, col:col + 128], rhs=w[:, kc, 0:nn],
                                              start=(kc == 0), stop=(kc == 7)), reads=[bw] + allb, writes=[pb])
            k = cnt["sb"] % 2
            cnt["sb"] += 1
            S.op("act", lambda e: e.activation(out=stgb[k][:, 0:256], in_=ps[:, 0:256], func=AF.Copy), reads=[pb], writes=[b_stgb[k]])
            S.dma(lambda q: q.dma_start(out=K.sc_v[ti * 128:(ti + 1) * 128, :], in_=stgb[k][:, 0:256]),
                  reads=[b_stgb[k]], writes=[K.b_scr])
            if ti < NT:
                st_, bs_ = nstg()
                S.op("act", lambda e: e.activation(out=st_[:, 0:256], in_=ps[:, 256:512], func=AF.Silu), reads=[pb], writes=[bs_])
                S.dma(lambda q: q.dma_start(out=K.sc_sg[ti * 128:(ti + 1) * 128, :], in_=st_[:, 0:256]),
                      reads=[bs_], writes=[K.b_scr])
        wvh = I["w_hy"].rearrange("(kc p) j -> p kc j", p=128)
        for (c0, n) in ((0, 512), (512, 256)):
            k = cnt["w"] % 2
            cnt["w"] += 1
            w, bw = wb[k], b_wb[k]
            S.dma(lambda q: q.dma_start(out=w[:, :, 0:n], in_=wvh[:, :, c0:c0 + n]), writes=[bw], q="pool")
            S.dma(lambda q: q.dma_start(out=cw[:, :, 0:n], in_=I["hy_conv_w"][:, c0:c0 + n].partition_broadcast(128)), writes=[b_cw])
            S.dma(lambda q: q.dma_start(out=cb[:, 0:n], in_=I["hy_conv_b"][c0:c0 + n].partition_broadcast(128)), writes=[b_cb])
            for j in range(3):
                S.op("dve", lambda e: e.tensor_tensor(out=wj[j][:, :, 0:n], in0=w[:, :, 0:n],
                                                      in1=cw[:, j:j + 1, 0:n].to_broadcast([128, 8, n]), op=ALU.mult),
                     reads=[bw, b_cw], writes=[b_wj[j]])
            for ti in range(NT):
                col = tokcol(ti)
                ps, pb = K.next_ps()
                m = 0
                for j in range(3):
                    for kc in range(8):
                        S.op("pe", lambda e: e.matmul(ps[:, 0:n], lhsT=K.hnT[:, kc, col + j - 1:col + j - 1 + 128],
                                                      rhs=wj[j][:, kc, 0:n], start=(m == 0), stop=(m == 23)),
                             reads=[b_wj[j]] + allb, writes=[pb])
                        m += 1
                st_, bs_ = nstg()
                S.op("dve", lambda e: e.tensor_tensor(out=st_[:, 0:n], in0=ps[:, 0:n], in1=cb[:, 0:n], op=ALU.add),
                     reads=[pb, b_cb], writes=[bs_])
                S.dma(lambda q: q.dma_start(out=K.sc_u[ti * 128:(ti + 1) * 128, c0:c0 + n], in_=st_[:, 0:n]),
                      reads=[bs_], writes=[K.b_scr])
        S.barrier()


def phase_g(K):
    nc, S, I = K.nc, K.S, K.I
    K.sc_o = K.scratch("sc_o", [L, 256])
    NCH = (L + LC) // 64
    with ExitStack() as ph:
        sb = lambda name, shape, dt=F32: K.sb("g_" + name, shape, dt, stack=ph)
        vtm = sb("vtm", [128, NT + NTC, 256], BF16)
        b_vtm = Buf()
        S.dma(lambda q: q.dma_start(out=vtm[:], in_=K.sc_v.rearrange("(t p) c -> p t c", p=128)),
              reads=[K.b_scr], writes=[b_vtm])
        wa = [sb("wa%d" % d, [16, 128]) for d in range(2)]
        b_wa = Buf()
        S.dma(lambda q: q.dma_start(out=wa[0][:], in_=I["wa_f"]), writes=[b_wa])
        S.dma(lambda q: q.dma_start(out=wa[1][:], in_=I["wa_b"]), writes=[b_wa])
        nba = sb("nba", [128, 2])
        b_nba = Buf()
        S.dma(lambda q: q.dma_start(out=nba[:], in_=I["gla_ba"]), writes=[b_nba])
        S.op("dve", lambda e: e.tensor_scalar(out=nba[:], in0=nba[:], scalar1=-1.0, scalar2=None, op0=ALU.mult),
             reads=[], writes=[b_nba])
        mk = [sb("mk%d" % d, [128, 128]) for d in range(2)]
        b_mk = Buf()
        S.dma(lambda q: q.dma_start(out=mk[0][:], in_=I["maskF"]), writes=[b_mk])
        S.dma(lambda q: q.dma_start(out=mk[1][:], in_=I["maskB"]), writes=[b_mk])
        smask = sb("smask", [128, 512])
        b_sm = Buf()
        S.op("pool", lambda e: e.memset(smask[:], 1.0), writes=[b_sm])
        S.op("pool", lambda e: e.memset(smask[:].rearrange("p (n c) -> p n c", c=64)[:, :, 0:1], 0.0), writes=[b_sm])
        QF = [sb("QF%d" % d, [128, L], BF16) for d in range(2)]
        KF = [sb("KF%d" % d, [128, L], BF16) for d in range(2)]
        QS = [sb("QS%d" % d, [128, L], BF16) for d in range(2)]
        KU = [sb("KU%d" % d, [128, NT + NTC, 128], BF16) for d in range(2)]
        dec = [sb("dec%d" % d, [128, NCH]) for d in range(2)]
        Sh = [sb("Sh%d" % d, [128, 64, 128], BF16) for d in range(2)]
        b_prep = [Buf(), Buf()]
        b_Sh = [Buf(), Buf()]
        Sst = [[sb("S%d_%d" % (d, i), [128, 128]) for i in range(2)] for d in range(2)]
        b_S = [[Buf(), Buf()], [Buf(), Buf()]]
        q32s = [sb("q32_%d" % i, [128, 512]) for i in range(2)]; k32s = [sb("k32_%d" % i, [128, 512]) for i in range(2)]
        a16s = [[sb("a16_%d_%d" % (i, d), [16, 512]) for d in range(2)] for i in range(2)]
        b_ins = [Buf(), Buf()]
        tmpd = [{n: sb("%s%d" % (n, d), [128, 512]) for n in ("tl", "tb", "tx", "td", "te", "tku")} for d in range(2)]
        b_td = [{n: Buf() for n in ("tl", "tb", "tx", "td", "te", "tku")} for d in range(2)]
        scf = [sb("scf%d" % i, [128, 128], BF16) for i in range(2)]
        scb = [sb("scb%d" % i, [128, 128], BF16) for i in range(2)]
        b_sc = [[Buf(), Buf()], [Buf(), Buf()]]
        ost = [sb("ost%d" % i, [128, 256]) for i in range(2)]
        b_ost = [Buf(), Buf()]
        tgs = [(g * 512, 512) for g in range(8)] + [(L, 256)]
        for hp in range(1):
            def g_load(gi):
                (t0, n) = tgs[gi]
                kk = gi % 2
                S.dma(lambda q: q.dma_start(out=k32s[kk][:, 0:n], in_=K.sc_qk[128:256, t0:t0 + n]),
                      reads=[K.b_scr], writes=[b_ins[kk]])
                if t0 < L:
                    S.dma(lambda q: q.dma_start(out=q32s[kk][:, 0:n], in_=K.sc_qk[0:128, t0:t0 + n]),
                          reads=[K.b_scr], writes=[b_ins[kk]])
                for d in range(2):
                    S.dma(lambda q: q.dma_start(out=a16s[kk][d][:, 0:n], in_=K.sc_a[d * 16:(d + 1) * 16, t0:t0 + n]),
                          reads=[K.b_scr], writes=[b_ins[kk]])

            g_load(0)
            for gi, (t0, n) in enumerate(tgs):
                isx = t0 < L
                nch = n // 64
                if gi + 1 < len(tgs):
                    g_load(gi + 1)
                q32, k32, a16, b_in = q32s[gi % 2], k32s[gi % 2], a16s[gi % 2], b_ins[gi % 2]
                for d in range(2):
                    tl, tb, tx, td, te, tku = (tmpd[d][n_] for n_ in ("tl", "tb", "tx", "td", "te", "tku"))
                    b_t = b_td[d]
                    ps, pb = K.next_ps()
                    S.op("pe", lambda e: e.matmul(ps[:, 0:n], lhsT=wa[d][:, hp * 128:(hp + 1) * 128], rhs=a16[d][:, 0:n],
                                                  start=True, stop=True), reads=[b_wa, b_in], writes=[pb])
                    S.op("act", lambda e: e.activation(out=te[:, 0:n], in_=ps[:, 0:n], func=AF.Exp, scale=-1.0,
                                                       bias=nba[:, d:d + 1]),
                         reads=[pb, b_nba], writes=[b_t["te"]])
                    S.op("act", lambda e: e.activation(out=tl[:, 0:n], in_=te[:, 0:n], func=AF.Ln, bias=1.0),
                         reads=[b_t["te"]], writes=[b_t["tl"]])
                    S.op("dve", lambda e: e.tensor_tensor_scan(out=tb[:, 0:n], data0=smask[:, 0:n], data1=tl[:, 0:n],
                                                               initial=0.0, op0=ALU.mult, op1=ALU.add),
                         reads=[b_t["tl"], b_sm], writes=[b_t["tb"]])
                    tbv = tb[:, 0:n].rearrange("p (n c) -> p n c", c=64)
                    if d == 0:
                        X, bX, ridx, tidx = tb, b_t["tb"], 32, 63
                    else:
                        S.op("dve", lambda e: e.tensor_tensor(out=tx[:, 0:n], in0=tl[:, 0:n], in1=tb[:, 0:n], op=ALU.subtract),
                             reads=[b_t["tl"], b_t["tb"]], writes=[b_t["tx"]])
                        S.op("dve", lambda e: e.tensor_tensor(out=tx[:, 0:n].rearrange("p (n c) -> p n c", c=64),
                                                              in0=tx[:, 0:n].rearrange("p (n c) -> p n c", c=64),
                                                              in1=tbv[:, :, 63:64].to_broadcast([128, nch, 64]), op=ALU.add),
                             reads=[b_t["tb"]], writes=[b_t["tx"]])
                        X, bX, ridx, tidx = tx, b_t["tx"], 31, 0
                    Xv = X[:, 0:n].rearrange("p (n c) -> p n c", c=64)
                    tdv = td[:, 0:n].rearrange("p (n c) -> p n c", c=64)
                    c0 = t0 // 64
                    S.op("act", lambda e: e.activation(out=dec[d][:, c0:c0 + nch].unsqueeze(2), in_=Xv[:, :, tidx:tidx + 1],
                                                       func=AF.Exp, scale=-1.0 / 16), reads=[bX], writes=[b_prep[d]])
                    if isx:
                        S.op("dve", lambda e: e.tensor_tensor(out=tdv, in0=Xv, in1=Xv[:, :, ridx:ridx + 1].to_broadcast([128, nch, 64]),
                                                              op=ALU.subtract), reads=[bX], writes=[b_t["td"]])
                        S.op("act", lambda e: e.activation(out=te[:, 0:n], in_=td[:, 0:n], func=AF.Exp, scale=-1.0 / 16),
                             reads=[b_t["td"]], writes=[b_t["te"]])
                        S.op("dve", lambda e: e.scalar_tensor_tensor(out=QF[d][:, t0:t0 + n], in0=q32[:, 0:n], scalar=0.125,
                                                                     in1=te[:, 0:n], op0=ALU.mult, op1=ALU.mult),
                             reads=[b_in, b_t["te"]], writes=[b_prep[d]])
                        S.op("act", lambda e: e.activation(out=te[:, 0:n], in_=td[:, 0:n], func=AF.Exp, scale=1.0 / 16),
                             reads=[b_t["td"]], writes=[b_t["te"]])
                        S.op("dve", lambda e: e.tensor_tensor(out=KF[d][:, t0:t0 + n], in0=k32[:, 0:n], in1=te[:, 0:n], op=ALU.mult),
                             reads=[b_in, b_t["te"]], writes=[b_prep[d]])
                        S.op("act", lambda e: e.activation(out=te[:, 0:n], in_=X[:, 0:n], func=AF.Exp, scale=-1.0 / 16),
                             reads=[bX], writes=[b_t["te"]])
                        S.op("dve", lambda e: e.scalar_tensor_tensor(out=QS[d][:, t0:t0 + n], in0=q32[:, 0:n], scalar=0.125,
                                                                     in1=te[:, 0:n], op0=ALU.mult, op1=ALU.mult),
                             reads=[b_in, b_t["te"]], writes=[b_prep[d]])
                    S.op("dve", lambda e: e.tensor_tensor(out=tdv, in0=Xv, in1=Xv[:, :, tidx:tidx + 1].to_broadcast([128, nch, 64]),
                                                          op=ALU.subtract), reads=[bX], writes=[b_t["td"]])
                    S.op("act", lambda e: e.activation(out=te[:, 0:n], in_=td[:, 0:n], func=AF.Exp, scale=1.0 / 16),
                         reads=[b_t["td"]], writes=[b_t["te"]])
                    S.op("dve", lambda e: e.tensor_tensor(out=tku[:, 0:n], in0=k32[:, 0:n], in1=te[:, 0:n], op=ALU.mult),
                         reads=[b_in, b_t["te"]], writes=[b_t["tku"]])
                    for j in range(n // 128):
                        ps, pb = K.next_ps()
                        S.op("pe", lambda e: e.transpose(out=ps[:, 0:128], in_=tku[:, j * 128:(j + 1) * 128], identity=K.identt[:]),
                             reads=[b_t["tku"]], writes=[pb])
                        S.op("act", lambda e: e.activation(out=KU[d][:, t0 // 128 + j, :], in_=ps[:, 0:128], func=AF.Copy),
                             reads=[pb], writes=[b_prep[d]])
            orders = [[64, 65, 66, 67] + list(range(64)), [67, 66, 65, 64] + list(range(63, -1, -1))]
            curs = [0, 0]
            for d in range(2):
                S.op("pool", lambda e: e.memset(Sst[d][0][:], 0.0), writes=[b_S[d][0]])
            for step in range(68):
                for d in range(2):
                    n_ = orders[d][step]
                    cur = curs[d]
                    tile, off = n_ // 2, (n_ % 2) * 64
                    ps, pb = K.next_ps()
                    for h in range(2):
                        S.op("pe", lambda e: e.matmul(ps[h * 64:(h + 1) * 64, 0:128],
                                                      lhsT=KU[d][off:off + 64, tile, h * 64:(h + 1) * 64],
                                                      rhs=vtm[off:off + 64, tile, (2 * hp + h) * 128:(2 * hp + h + 1) * 128],
                                                      start=True, stop=True),
                             reads=[b_prep[d], b_vtm], writes=[pb])
                    if n_ < 64:
                        S.op("act", lambda e: e.activation(out=Sh[d][:, n_, :], in_=Sst[d][cur][:], func=AF.Copy),
                             reads=[b_S[d][cur]], writes=[b_Sh[d]])
                    S.op("dve", lambda e: e.scalar_tensor_tensor(out=Sst[d][1 - cur][:], in0=Sst[d][cur][:],
                                                                 scalar=dec[d][:, n_:n_ + 1], in1=ps[:, 0:128],
                                                                 op0=ALU.mult, op1=ALU.add),
                         reads=[b_S[d][cur], pb, b_prep[d]], writes=[b_S[d][1 - cur]])
                    curs[d] = 1 - cur
            for ti in range(NT):
                os_, bo_ = ost[ti % 2], b_ost[ti % 2]
                for h in range(2):
                    hs = slice(h * 64, (h + 1) * 64)
                    tk = slice(ti * 128, (ti + 1) * 128)
                    scs = []
                    for d in range(2):
                        ps, pb = K.next_ps()
                        S.op("pe", lambda e: e.matmul(ps[:, 0:128], lhsT=KF[d][hs, tk], rhs=QF[d][hs, tk], start=True, stop=True),
                             reads=[b_prep[d]], writes=[pb])
                        sc_ = (scf if d == 0 else scb)[h]
                        S.op("dve", lambda e: e.tensor_tensor(out=sc_[:], in0=ps[:, 0:128], in1=mk[d][:], op=ALU.mult),
                             reads=[pb, b_mk], writes=[b_sc[d][h]])
                        scs.append(sc_)
                    ps, pb = K.next_ps()
                    vh = vtm[:, ti, (2 * hp + h) * 128:(2 * hp + h + 1) * 128]
                    S.op("pe", lambda e: e.matmul(ps[:, 0:128], lhsT=scs[0][:], rhs=vh, start=True, stop=False),
                         reads=[b_sc[0][h], b_vtm], writes=[pb])
                    S.op("pe", lambda e: e.matmul(ps[:, 0:128], lhsT=scs[1][:], rhs=vh, start=False, stop=False),
                         reads=[b_sc[1][h], b_vtm], writes=[pb])
                    for d in range(2):
                        for j in range(2):
                            n_ = 2 * ti + j
                            last = (d == 1 and j == 1)
                            S.op("pe", lambda e: e.matmul(ps[j * 64:(j + 1) * 64, 0:128],
                                                          lhsT=QS[d][hs, n_ * 64:(n_ + 1) * 64], rhs=Sh[d][hs, n_, :],
                                                          start=False, stop=last),
                                 reads=[b_prep[d], b_Sh[d]], writes=[pb])
                    S.op("act", lambda e: e.activation(out=os_[:, h * 128:(h + 1) * 128], in_=ps[:, 0:128], func=AF.Copy),
                         reads=[pb], writes=[bo_])
                S.dma(lambda q: q.dma_start(out=K.sc_o[ti * 128:(ti + 1) * 128, hp * 256:(hp + 1) * 256], in_=os_[:]),
                      reads=[bo_], writes=[K.b_scr])
            S.barrier()


def phase_gn(K):
    nc, S, I = K.nc, K.S, K.I
    K.sc_ygla = K.scratch("sc_ygla", [L, 512])
    K.sc_ygla2 = K.scratch("sc_ygla2", [L, 512])
    K.b_ygla2 = Buf()
    with ExitStack() as ph:
        sb = lambda name, shape, dt=F32: K.sb("n_" + name, shape, dt, stack=ph)
        ng = sb("ng", [128, 128]); b_ng = Buf()
        S.dma(lambda q: q.dma_start(out=ng[:], in_=I["gla_norm_g"].partition_broadcast(128)), writes=[b_ng])
        rmask = sb("rmask", [128, 2]); b_rm = Buf()
        S.dma(lambda q: q.dma_start(out=rmask[:], in_=I["rmask"]), writes=[b_rm])
        ot = [sb("ot%d" % i, [128, 256]) for i in range(2)]
        gt = [sb("gt%d" % i, [128, 256]) for i in range(2)]
        yt = [sb("yt%d" % i, [128, 256]) for i in range(2)]
        ym_ = [sb("ym%d" % i, [128, 2, 256]) for i in range(2)]
        junk = sb("junk", [128, 128])
        st = sb("st", [128, 8 * NT])
        b_o, b_g, b_y, b_ym = [Buf(), Buf()], [Buf(), Buf()], [Buf(), Buf()], [Buf(), Buf()]
        b_junk, b_st = Buf(), Buf()
        def gn_load(ti):
            k = ti % 2
            S.dma(lambda q: q.dma_start(out=ot[k][:], in_=K.sc_o[ti * 128:(ti + 1) * 128, :]), reads=[K.b_scr], writes=[b_o[k]])
            S.dma(lambda q: q.dma_start(out=gt[k][:], in_=K.sc_sg[ti * 128:(ti + 1) * 128, :]), reads=[K.b_scr], writes=[b_g[k]])

        gn_load(0)
        for ti in range(NT):
            k = ti % 2
            if ti + 1 < NT:
                gn_load(ti + 1)
            ss = st[:, 8 * ti:8 * ti + 2]
            rs = st[:, 8 * ti + 4:8 * ti + 6]
            for h in range(2):
                S.op("act", lambda e: e.activation(out=junk[:], in_=ot[k][:, h * 128:(h + 1) * 128], func=AF.Square,
                                                   accum_out=st[:, 8 * ti + h:8 * ti + h + 1]),
                     reads=[b_o[k]], writes=[b_junk, b_st])
            S.op("act", lambda e: e.activation(out=rs, in_=ss, func=AF.Sqrt, scale=1.0 / 128, bias=EPS), reads=[b_st], writes=[b_st])
            S.op("dve", lambda e: e.reciprocal(out=rs, in_=rs), reads=[b_st], writes=[b_st])
            for h in range(2):
                hs = slice(h * 128, (h + 1) * 128)
                S.op("dve", lambda e: e.scalar_tensor_tensor(out=yt[k][:, hs], in0=ot[k][:, hs], scalar=st[:, 8 * ti + 4 + h:8 * ti + 5 + h],
                                                             in1=gt[k][:, hs], op0=ALU.mult, op1=ALU.mult),
                     reads=[b_o[k], b_g[k], b_st], writes=[b_y[k]])
                S.op("pool", lambda e: e.tensor_tensor(out=yt[k][:, hs], in0=yt[k][:, hs], in1=ng[:], op=ALU.mult),
                     reads=[b_ng], writes=[b_y[k]])
            for m_ in range(2):
                S.op("dve", lambda e: e.tensor_scalar(out=ym_[k][:, m_, :], in0=yt[k][:], scalar1=rmask[:, m_:m_ + 1], scalar2=None, op0=ALU.mult),
                     reads=[b_y[k], b_rm], writes=[b_ym[k]])
            S.dma(lambda q: q.dma_start(out=K.sc_ygla[ti * 128:(ti + 1) * 128, :].rearrange("p (a c) -> p a c", a=2), in_=ym_[k][:]),
                  reads=[b_ym[k]], writes=[K.b_scr])
        S.barrier()
        for ch in range(2):
            S.coll(lambda g: g.collective_compute("AllReduce", ALU.add, replica_groups=[[0, 1], [2, 3], [4, 5], [6, 7]],
                                                  ins=[K.sc_ygla[ch * 2048:(ch + 1) * 2048, :]], outs=[K.sc_ygla2[ch * 2048:(ch + 1) * 2048, :]]),
                   reads=[K.b_scr], writes=[K.b_ygla2])
        S.barrier()


def phase_h(K):
    nc, S, I = K.nc, K.S, K.I
    PI = math.pi
    NB, PT, FC = 4, 8, 9
    with ExitStack() as ph:
        sb = lambda name, shape, dt=F32: K.sb("h_" + name, shape, dt, stack=ph)
        K.sc_h = K.scratch("sc_h", [4, L, 256], BF16)
        with ExitStack() as ph2:
            sb2 = lambda name, shape, dt=F32: K.sb("h2_" + name, shape, dt, stack=ph2)
            hd2 = [sb2("hd2_%d" % d, [64, L]) for d in range(2)]; b_hd2 = [Buf(), Buf()]
            w3 = sb2("w3", [64, 1024]); b_w3 = Buf()
            S.dma(lambda q: q.dma_start(out=w3[:], in_=I["hy_w3"]), writes=[b_w3])
            fbv = sb2("fb", [64, 4]); b_fb = Buf()
            S.dma(lambda q: q.dma_start(out=fbv[:], in_=I["hy_fb"]), writes=[b_fb])
            fbb = sb2("fbb", [64, 2])
            S.op("dve", lambda e: e.tensor_tensor(out=fbb[:], in0=fbv[:, 1:3], in1=fbv[:, 0:1].to_broadcast([64, 2]), op=ALU.mult),
                 reads=[b_fb], writes=[b_fb])
            tn = sb2("tn", [128, 2, NT]); dl = sb2("dl", [128, 256]); b_c2 = Buf()
            S.dma(lambda q: q.dma_start(out=tn[:], in_=I["tnorm"]), writes=[b_c2])
            S.dma(lambda q: q.dma_start(out=dl[:], in_=I["deltas"].partition_broadcast(128)), writes=[b_c2])
            S.op("dve", lambda e: e.tensor_scalar(out=tn[:], in0=tn[:], scalar1=-1.0, scalar2=None, op0=ALU.mult), reads=[], writes=[b_c2])
            brow = sb2("brow", [1, 512]); b_brow = Buf()
            S.dma(lambda q: q.dma_start(out=brow[:], in_=I["hy_bias"].rearrange("o c -> (o c)").unsqueeze(0)), writes=[b_brow])
            zT = sb2("zT", [33, 2, L]); b_z = Buf()
            S.dma(lambda q: q.dma_start(out=zT[:], in_=I["zT"]), writes=[b_z])
            w1 = sb2("w1", [33, 64]); w2 = sb2("w2", [64, 64]); b_w = Buf()
            S.dma(lambda q: q.dma_start(out=w1[:], in_=I["hy_w1"]), writes=[b_w])
            S.dma(lambda q: q.dma_start(out=w2[:], in_=I["hy_w2"]), writes=[b_w])
            hd1 = sb2("hd1", [64, L]); b_hd1 = Buf()
            arg = [sb2("arg%d" % i, [64, 512]) for i in range(2)]; b_arg = [Buf(), Buf()]
            wr1 = sb2("wr1", [64, 512]); wr2 = sb2("wr2", [64, 512]); b_wr1, b_wr2 = Buf(), Buf()
            for dr in range(2):
                for layer in range(2):
                    for tg in range(8):
                        ts_ = slice(tg * 512, (tg + 1) * 512)
                        ps, pb = K.next_ps()
                        if layer == 0:
                            S.op("pe", lambda e: e.matmul(ps[0:64, :], lhsT=w1[:], rhs=zT[:, dr, ts_], start=True, stop=True),
                                 reads=[b_w, b_z], writes=[pb])
                        else:
                            S.op("pe", lambda e: e.matmul(ps[0:64, :], lhsT=w2[:], rhs=hd1[:, ts_], start=True, stop=True),
                                 reads=[b_w, b_hd1], writes=[pb])
                        a_, ba_ = arg[tg % 2], b_arg[tg % 2]
                        S.op("act", lambda e: e.activation(out=a_[:], in_=ps[0:64, :], func=AF.Identity, scale=fbv[:, 0:1],
                                                           bias=fbb[:, layer:layer + 1]), reads=[pb, b_fb], writes=[ba_])
                        S.op("dve", lambda e: e.tensor_scalar(out=wr1[:], in0=a_[:], scalar1=PI, scalar2=-2 * PI, op0=ALU.is_gt, op1=ALU.mult),
                             reads=[ba_], writes=[b_wr1])
                        S.op("dve", lambda e: e.tensor_scalar(out=wr2[:], in0=a_[:], scalar1=-PI, scalar2=2 * PI, op0=ALU.is_lt, op1=ALU.mult),
                             reads=[ba_], writes=[b_wr2])
                        S.op("dve", lambda e: e.tensor_tensor(out=a_[:], in0=a_[:], in1=wr1[:], op=ALU.add), reads=[b_wr1], writes=[ba_])
                        S.op("dve", lambda e: e.tensor_tensor(out=a_[:], in0=a_[:], in1=wr2[:], op=ALU.add), reads=[b_wr2], writes=[ba_])
                        dst, bd = (hd1, b_hd1) if layer == 0 else (hd2[dr], b_hd2[dr])
                        S.op("act", lambda e: e.activation(out=dst[:, ts_], in_=a_[:], func=AF.Sin), reads=[ba_], writes=[bd])
            dk = [sb2("dk%d" % d, [128, 256]) for d in range(2)]; b_dk = [Buf(), Buf()]
            ht = [sb2("ht%d" % i, [128, 256]) for i in range(2)]; b_ht = [Buf(), Buf()]
            hbf = [sb2("hbf%d" % i, [128, 256], BF16) for i in range(3)]; b_hbf = [Buf(), Buf(), Buf()]
            hn_ = 0
            for ti in range(NT):
                for dr in range(2):
                    S.op("act", lambda e: e.activation(out=dk[dr][:], in_=dl[:], func=AF.Exp, scale=tn[:, dr, ti:ti + 1]),
                         reads=[b_c2], writes=[b_dk[dr]])
                for od in range(4):
                    dr = od % 2
                    ps, pb = K.next_ps()
                    S.op("pe", lambda e: e.matmul(ps[:, 0:256], lhsT=hd2[dr][:, ti * 128:(ti + 1) * 128],
                                                  rhs=w3[:, od * 256:(od + 1) * 256], start=True, stop=True),
                         reads=[b_hd2[dr], b_w3], writes=[pb])
                    h_, bh_ = ht[hn_ % 2], b_ht[hn_ % 2]
                    hb_, bhb_ = hbf[hn_ % 3], b_hbf[hn_ % 3]
                    hn_ += 1
                    if ti == 0 and dr == 0:
                        o_ = od // 2
                        S.op("dve", lambda e: e.tensor_tensor(out=h_[:], in0=ps[:, 0:256], in1=dk[dr][:], op=ALU.mult),
                             reads=[pb, b_dk[dr]], writes=[bh_])
                        S.op("dve", lambda e: e.tensor_tensor(out=h_[0:1, :], in0=h_[0:1, :], in1=brow[:, o_ * 256:(o_ + 1) * 256], op=ALU.add),
                             reads=[b_brow], writes=[bh_])
                        S.op("act", lambda e: e.activation(out=hb_[:], in_=h_[:], func=AF.Copy), reads=[bh_], writes=[bhb_])
                    else:
                        S.op("dve", lambda e: e.tensor_tensor(out=hb_[:], in0=ps[:, 0:256], in1=dk[dr][:], op=ALU.mult),
                             reads=[pb, b_dk[dr]], writes=[bhb_])
                    S.dma(lambda q: q.dma_start(out=K.sc_h[od, ti * 128:(ti + 1) * 128, :], in_=hb_[:]), reads=[bhb_], writes=[K.b_scr])
            S.barrier()
        TAB = sb("TAB", [128, FC, 2, FC * 128], BF16); b_TAB = Buf()
        S.dma(lambda q: q.dma_start(out=TAB[:], in_=I["dft"]), writes=[b_TAB])
        cf = sb("cf", [128, 6, FC]); b_cf = Buf()
        S.dma(lambda q: q.dma_start(out=cf[:], in_=I["hcoef"]), writes=[b_cf])
        HF = sb("HF", [128, NT, 256], BF16); HB = sb("HB", [128, NT, 256], BF16); U = sb("U", [128, NT, 256], BF16)
        b_HF, b_HB, b_U = Buf(), Buf(), Buf()
        Y = sb("Y", [128, NB, FC, 2, 256], BF16); b_Y = Buf()
        RR2 = [[sb("RR%d_%d" % (bf_, tab), [128, 4, 256]) for tab in range(2)] for bf_ in range(2)]
        b_RR2 = [[Buf(), Buf()], [Buf(), Buf()]]
        PA2 = [[sb("PA%d_%d" % (bf_, tab), [128, 8, 256]) for tab in range(2)] for bf_ in range(2)]
        b_PA2 = [[Buf(), Buf()], [Buf(), Buf()]]
        SX2 = [[sb("SX%d_%d" % (bf_, tab), [128, 4, 256], BF16) for tab in range(2)] for bf_ in range(2)]
        b_SX2 = [[Buf(), Buf()], [Buf(), Buf()]]
        CK = [sb("CK%d" % tab, [128, 7, 256], BF16) for tab in range(2)]; b_CK = [Buf(), Buf()]
        XN = sb("XN", [128, 4, 256], BF16); b_XN = Buf()
        T1 = [sb("T1_%d" % i, [128, 4, 256]) for i in range(2)]; b_T1 = [Buf(), Buf()]
        TB = [sb("TB%d" % i, [128, 4, 256], BF16) for i in range(2)]; b_TB = [Buf() for _ in range(2)]
        identb = sb("identb", [128, 128], BF16); b_idb = Buf()
        S.op("act", lambda e: e.activation(out=identb[:], in_=K.identt[:], func=AF.Copy), reads=[K.b_ident], writes=[b_idb])
        xg = [sb("xg%d" % i, [128, 256]) for i in range(2)]; b_xg = [Buf(), Buf()]
        yo = [sb("yo%d" % i, [128, 256]) for i in range(2)]; b_yo = [Buf(), Buf()]
        ld, b_ld = yo, b_yo
        K.sc_yhy = K.scratch("sc_yhy", [L, 512])
        K.sc_yhy2 = K.scratch("sc_yhy2", [L, 512])
        K.b_yhy2 = Buf()
        st_bufs = [[], []]
        rmask = sb("rmask", [128, 2]); b_rm = Buf()
        S.dma(lambda q: q.dma_start(out=rmask[:], in_=I["rmask"]), writes=[b_rm])
        for hh in range(1):
            cs = slice(0, 256)
            for ti in range(NT):
                k = ti % 2
                S.dma(lambda q: q.dma_start(out=ld[k][:], in_=K.sc_u[ti * 128:(ti + 1) * 128, 0:256]), reads=[K.b_scr], writes=[b_ld[k]])
                S.op("act", lambda e: e.activation(out=U[:, ti, :], in_=ld[k][:], func=AF.Copy), reads=[b_ld[k]], writes=[b_U])
            for o in range(2):
                for g4 in range(4):
                    gs = slice(g4 * 8, (g4 + 1) * 8)
                    S.dma(lambda q: q.dma_start(out=HF[:, gs, :], in_=K.sc_h[2 * o].rearrange("(t p) c -> p t c", p=128)[:, gs, cs]),
                          reads=[K.b_scr], writes=[b_HF])
                    gr = slice((3 - g4) * 8, (4 - g4) * 8)
                    S.dma(lambda q: q.dma_start(out=HB[:, gr, :], in_=K.sc_h[2 * o + 1].rearrange("(t p) c -> p t c", p=128)[:, gs, cs]),
                          reads=[K.b_scr], writes=[b_HB])
                sigs = [(HF, b_HF, n) for n in range(4)] + [(HB, b_HB, n) for n in range(4)] + [(U, b_U, n) for n in range(4)]
                def emit_tr(fc):
                    fs_ = slice(fc * 128, (fc + 1) * 128)
                    PA, RR, SX = PA2[fc % 2], RR2[fc % 2], SX2[fc % 2]
                    b_PA, b_RR, b_SX = b_PA2[fc % 2], b_RR2[fc % 2], b_SX2[fc % 2]
                    for tab in range(2):
                        banks = [K.next_ps() for _ in range(6)]
                        for a in range(PT):
                            for s2 in range(6):
                                src, bsrc, n = sigs[2 * s2]
                                ps, pb = banks[s2]
                                S.op("pe", lambda e: e.matmul(ps[:].rearrange("p (n c) -> p n c", n=2), lhsT=TAB[:, a, tab, fs_],
                                                              rhs=src[:].rearrange("p (n a) c -> p a n c", a=PT)[:, a, n:n + 2, :],
                                                              start=(a == 0), stop=(a == PT - 1)),
                                     reads=[b_TAB, bsrc], writes=[pb])
                        for s2 in range(6):
                            ps, pb = banks[s2]
                            pv = ps[:].rearrange("p (n c) -> p n c", n=2)
                            if s2 < 2:
                                S.op("act", lambda e: e.activation(out=PA[tab][:, 4 + 2 * s2:6 + 2 * s2, :], in_=pv, func=AF.Copy, scale=cf[:, 0, fc:fc + 1]),
                                     reads=[pb, b_cf], writes=[b_PA[tab]])
                            elif s2 < 4:
                                S.op("act", lambda e: e.activation(out=RR[tab][:, 2 * (s2 - 2):2 * (s2 - 2) + 2, :], in_=pv, func=AF.Copy,
                                                                   scale=cf[:, 0, fc:fc + 1]), reads=[pb, b_cf], writes=[b_RR[tab]])
                            else:
                                S.op("act", lambda e: e.activation(out=SX[tab][:, 2 * (s2 - 4):2 * (s2 - 4) + 2, :], in_=pv, func=AF.Copy),
                                     reads=[pb], writes=[b_SX[tab]])
                def emit_mt(fc):
                    PA, RR, SX = PA2[fc % 2], RR2[fc % 2], SX2[fc % 2]
                    b_PA, b_RR, b_SX = b_PA2[fc % 2], b_RR2[fc % 2], b_SX2[fc % 2]
                    for (tab, ia, ib, op_) in ((0, 3, 2, ALU.subtract), (1, 5, 4, ALU.add)):
                        S.op("pool", lambda e: e.tensor_scalar(out=T1[0][:], in0=RR[1][:], scalar1=cf[:, ia, fc:fc + 1], scalar2=0.0, op0=ALU.mult, op1=ALU.add),
                             reads=[b_RR[1], b_cf], writes=[b_T1[0]])
                        S.op("pool", lambda e: e.tensor_scalar(out=T1[1][:], in0=RR[0][:], scalar1=cf[:, ib, fc:fc + 1], scalar2=0.0, op0=ALU.mult, op1=ALU.add),
                             reads=[b_RR[0], b_cf], writes=[b_T1[1]])
                        S.op("pool", lambda e: e.tensor_tensor(out=PA[tab][:, 0:4, :], in0=T1[1][:], in1=T1[0][:], op=op_),
                             reads=[b_T1[0], b_T1[1]], writes=[b_PA[tab]])
                    for tab in range(2):
                        S.op("dve", lambda e: e.scalar_tensor_tensor(out=CK[tab][:], in0=PA[tab][:, 0:7, :], scalar=cf[:, 1, fc:fc + 1],
                                                                     in1=PA[tab][:, 1:8, :], op0=ALU.mult, op1=ALU.add),
                             reads=[b_PA[tab], b_cf], writes=[b_CK[tab]])
                    S.op("pool", lambda e: e.tensor_scalar(out=XN[:], in0=SX[1][:], scalar1=-1.0, scalar2=0.0, op0=ALU.mult, op1=ALU.add),
                         reads=[b_SX[1]], writes=[b_XN])
                    for tab in range(2):
                        pA, pbA = K.next_ps()
                        pB, pbB = K.next_ps()
                        n_ = 0
                        for j in range(4):
                            if tab == 0:
                                terms = ((CK[0], b_CK[0], SX[0], b_SX[0]), (CK[1], b_CK[1], XN, b_XN))
                            else:
                                terms = ((CK[0], b_CK[0], SX[1], b_SX[1]), (CK[1], b_CK[1], SX[0], b_SX[0]))
                            for (ca, bca, xa, bxa) in terms:
                                cav = ca[:, 3 - j:7 - j, :]
                                xav = xa[:, j:j + 1, :].to_broadcast([128, 4, 256])
                                tb_, btb_ = TB[n_ % 2], b_TB[n_ % 2]
                                e_ = "pool" if n_ == 7 else "dve"
                                S.op(e_, lambda e: e.tensor_tensor(out=tb_[:], in0=cav, in1=xav, op=ALU.mult), reads=[bca, bxa], writes=[btb_])
                                S.op("pe", lambda e: e.matmul(pA[:], lhsT=identb[:], rhs=tb_[:, 0:2, :].rearrange("p a b -> p (a b)"),
                                                              start=(n_ == 0), stop=(n_ == 7)), reads=[b_idb, btb_], writes=[pbA])
                                S.op("pe", lambda e: e.matmul(pB[:], lhsT=identb[:], rhs=tb_[:, 2:4, :].rearrange("p a b -> p (a b)"),
                                                              start=(n_ == 0), stop=(n_ == 7)), reads=[b_idb, btb_], writes=[pbB])
                                n_ += 1
                        S.op("act", lambda e: e.activation(out=Y[:, 0:2, fc, tab, :], in_=pA[:].rearrange("p (a b) -> p a b", a=2), func=AF.Copy),
                             reads=[pbA], writes=[b_Y])
                        S.op("act", lambda e: e.activation(out=Y[:, 2:4, fc, tab, :], in_=pB[:].rearrange("p (a b) -> p a b", a=2), func=AF.Copy),
                             reads=[pbB], writes=[b_Y])
                for fc in range(FC):
                    emit_tr(fc)
                    if fc > 0:
                        emit_mt(fc - 1)
                emit_mt(FC - 1)
                for i2 in range(2):
                    for a in range(PT):
                        ps, pb = K.next_ps()
                        n = 0
                        for fc in range(FC):
                            for tab in range(2):
                                S.op("pe", lambda e: e.matmul(ps[:].rearrange("p (n c) -> p n c", n=2), lhsT=TAB[:, fc, tab, a * 128:(a + 1) * 128],
                                                              rhs=Y[:, 2 * i2:2 * i2 + 2, fc, tab, :], start=(n == 0), stop=(n == 2 * FC - 1)),
                                     reads=[b_TAB, b_Y], writes=[pb])
                                n += 1
                        for i in (2 * i2, 2 * i2 + 1):
                            ti = i * PT + a
                            k = i % 2
                            S.dma(lambda q: q.dma_start(out=xg[k][:], in_=K.sc_u[ti * 128:(ti + 1) * 128, 256 * (1 + o):256 * (2 + o)]),
                                  reads=[K.b_scr], writes=[b_xg[k]])
                            if o == 0:
                                S.op("dve", lambda e: e.tensor_tensor(out=U[:, ti, :], in0=ps[:, (i % 2) * 256:(i % 2 + 1) * 256], in1=xg[k][:], op=ALU.mult),
                                     reads=[pb, b_xg[k]], writes=[b_U])
                            else:
                                for m_ in range(2):
                                    S.op("dve", lambda e: e.scalar_tensor_tensor(out=yo[m_][:], in0=ps[:, (i % 2) * 256:(i % 2 + 1) * 256],
                                                                                 scalar=rmask[:, m_:m_ + 1], in1=xg[k][:], op0=ALU.mult, op1=ALU.mult),
                                         reads=[pb, b_xg[k], b_rm], writes=[b_yo[m_]])
                                    bst = Buf()
                                    st_bufs[i2].append(bst)
                                    S.dma(lambda q: q.dma_start(out=K.sc_yhy[ti * 128:(ti + 1) * 128, m_ * 256:(m_ + 1) * 256], in_=yo[m_][:]),
                                          reads=[b_yo[m_]], writes=[K.b_scr, bst])
                    if o == 1:
                        S.coll(lambda g: g.collective_compute("AllReduce", ALU.add, replica_groups=[[0, 1], [2, 3], [4, 5], [6, 7]],
                                                              ins=[K.sc_yhy[i2 * 2048:(i2 + 1) * 2048, :]], outs=[K.sc_yhy2[i2 * 2048:(i2 + 1) * 2048, :]]),
                               reads=st_bufs[i2], writes=[K.b_yhy2])
        S.barrier()


def phase_e(K):
    nc, S, I = K.nc, K.S, K.I
    K.sc_x1 = K.scratch("sc_x1", [L, D])
    K.sc_hn2 = K.scratch("sc_hn2", [L, D], BF16)
    K.aff = K.sb("aff", [128, NT, NE]); K.b_aff = Buf()
    with ExitStack() as ph:
        sb = lambda name, shape, dt=F32: K.sb("e_" + name, shape, dt, stack=ph)
        wo = sb("wo", [128, 8, D], BF16); b_wo = Buf()
        S.dma(lambda q: q.dma_start(out=wo[:], in_=I["w_out"].rearrange("(kc p) j -> p kc j", p=128)), writes=[b_wo], q="pool")
        wr = sb("wr", [128, 8, NE]); b_wr = Buf()
        S.dma(lambda q: q.dma_start(out=wr[:], in_=I["w_router"].rearrange("(kc p) j -> p kc j", p=128)), writes=[b_wr])
        md = sb("md", [128, 3, D]); b_md = Buf()
        S.dma(lambda q: q.dma_start(out=md[:], in_=K.sc_mod[:, 2 * D:5 * D].rearrange("p (a d) -> p a d", a=3)),
              reads=[K.b_scmod], writes=[b_md])
        ym = [sb("ym%d" % i, [128, D]) for i in range(2)]; b_ym = [Buf(), Buf()]
        yT = [sb("yT%d" % i, [128, 8, 128], BF16) for i in range(2)]; b_yT = [Buf(), Buf()]
        xt = [sb("xt%d" % i, [128, D]) for i in range(2)]; b_xt = [Buf(), Buf()]
        pt = [sb("pt%d" % i, [128, D]) for i in range(2)]; b_pt = [Buf(), Buf()]
        hn = [sb("hn%d" % i, [128, D]) for i in range(2)]; b_hn = [Buf(), Buf()]
        hb = [sb("hb%d" % i, [128, D], BF16) for i in range(2)]; b_hb = [Buf(), Buf()]
        hT = [sb("hT%d" % i, [128, 8, 128]) for i in range(2)]; b_hT = [Buf(), Buf()]
        junk = sb("junk", [128, D]); b_junk = Buf()
        st = sb("st", [128, 2 * NT]); b_st = Buf()
        lg = sb("lg", [128, NT, NE]); b_lg = Buf()
        def e_load(ti):
            k = ti % 2
            rows = slice(ti * 128, (ti + 1) * 128)
            S.dma(lambda q: q.dma_start(out=ym[k][:, 0:512], in_=K.sc_ygla2[rows, :]), reads=[K.b_ygla2], writes=[b_ym[k]])
            S.dma(lambda q: q.dma_start(out=ym[k][:, 512:1024], in_=K.sc_yhy2[rows, :]), reads=[K.b_yhy2], writes=[b_ym[k]])
            S.dma(lambda q: q.dma_start(out=xt[k][:], in_=I["x"][rows, :]), writes=[b_xt[k]])
            S.dma(lambda q: q.dma_start(out=pt[k][:], in_=I["pos"][rows, :]), writes=[b_pt[k]])

        e_load(0)
        for ti in range(NT):
            k = ti % 2
            rows = slice(ti * 128, (ti + 1) * 128)
            if ti + 1 < NT:
                e_load(ti + 1)
            S.op("pool", lambda e: e.tensor_tensor(out=xt[k][:], in0=xt[k][:], in1=pt[k][:], op=ALU.add), reads=[b_pt[k]], writes=[b_xt[k]])
            for hh in range(2):
                ps, pb = K.next_ps()
                for j in range(4):
                    kc = hh * 4 + j
                    S.op("pe", lambda e: e.transpose(out=ps[:, j * 128:(j + 1) * 128], in_=ym[k][:, kc * 128:(kc + 1) * 128],
                                                     identity=K.identt[:]), reads=[b_ym[k], K.b_ident], writes=[pb])
                S.op("act", lambda e: e.activation(out=yT[k][:, hh * 4:hh * 4 + 4, :], in_=ps[:].rearrange("p (a b) -> p a b", a=4),
                                                   func=AF.Copy), reads=[pb], writes=[b_yT[k]])
            for half in range(2):
                hs = slice(half * 512, (half + 1) * 512)
                ps, pb = K.next_ps()
                for kc in range(8):
                    S.op("pe", lambda e: e.matmul(ps[:], lhsT=yT[k][:, kc, :], rhs=wo[:, kc, hs], start=(kc == 0), stop=(kc == 7)),
                         reads=[b_yT[k], b_wo], writes=[pb])
                S.op("dve", lambda e: e.tensor_tensor(out=hn[k][:, hs], in0=ps[:], in1=md[:, 0, hs], op=ALU.mult),
                     reads=[pb, b_md], writes=[b_hn[k]])
                S.op("pool", lambda e: e.tensor_tensor(out=xt[k][:, hs], in0=xt[k][:, hs], in1=hn[k][:, hs], op=ALU.add),
                     reads=[b_hn[k]], writes=[b_xt[k]])
            S.dma(lambda q: q.dma_start(out=K.sc_x1[rows, :], in_=xt[k][:]), reads=[b_xt[k]], writes=[K.b_scr])
            ss = st[:, 2 * ti:2 * ti + 1]
            rs = st[:, 2 * ti + 1:2 * ti + 2]
            S.op("act", lambda e: e.activation(out=junk[:], in_=xt[k][:], func=AF.Square, accum_out=ss), reads=[b_xt[k]], writes=[b_junk, b_st])
            S.op("act", lambda e: e.activation(out=rs, in_=ss, func=AF.Sqrt, scale=1.0 / D, bias=EPS), reads=[b_st], writes=[b_st])
            S.op("dve", lambda e: e.reciprocal(out=rs, in_=rs), reads=[b_st], writes=[b_st])
            S.op("dve", lambda e: e.scalar_tensor_tensor(out=hn[k][:], in0=xt[k][:], scalar=rs, in1=md[:, 2, :], op0=ALU.mult, op1=ALU.mult),
                 reads=[b_xt[k], b_st, b_md], writes=[b_hn[k]])
            S.op("pool", lambda e: e.tensor_tensor(out=hn[k][:], in0=hn[k][:], in1=md[:, 1, :], op=ALU.add), reads=[b_md], writes=[b_hn[k]])
            S.op("act", lambda e: e.activation(out=hb[k][:], in_=hn[k][:], func=AF.Copy), reads=[b_hn[k]], writes=[b_hb[k]])
            S.dma(lambda q: q.dma_start(out=K.sc_hn2[rows, :], in_=hb[k][:]), reads=[b_hb[k]], writes=[K.b_scr])
            for hh in range(2):
                ps, pb = K.next_ps()
                for j in range(4):
                    kc = hh * 4 + j
                    S.op("pe", lambda e: e.transpose(out=ps[:, j * 128:(j + 1) * 128], in_=hn[k][:, kc * 128:(kc + 1) * 128],
                                                     identity=K.identt[:]), reads=[b_hn[k], K.b_ident], writes=[pb])
                S.op("act", lambda e: e.activation(out=hT[k][:, hh * 4:hh * 4 + 4, :], in_=ps[:].rearrange("p (a b) -> p a b", a=4),
                                                   func=AF.Copy), reads=[pb], writes=[b_hT[k]])
            ps, pb = K.next_ps()
            for kc in range(8):
                S.op("pe", lambda e: e.matmul(ps[:, 0:NE], lhsT=hT[k][:, kc, :], rhs=wr[:, kc, :], start=(kc == 0), stop=(kc == 7)),
                     reads=[b_hT[k], b_wr], writes=[pb])
            S.op("act", lambda e: e.activation(out=lg[:, ti, :], in_=ps[:, 0:NE], func=AF.Copy), reads=[pb], writes=[b_lg])
        mx = sb("mx", [128, NT]); sm = sb("sm", [128, NT]); b_mx = Buf()
        S.op("dve", lambda e: e.tensor_reduce(out=mx[:], in_=lg[:], axis=AX.X, op=ALU.max), reads=[b_lg], writes=[b_mx])
        S.op("dve", lambda e: e.tensor_tensor(out=lg[:], in0=lg[:], in1=mx[:].unsqueeze(2).to_broadcast([128, NT, NE]), op=ALU.subtract),
             reads=[b_mx], writes=[b_lg])
        S.op("act", lambda e: e.activation(out=lg[:], in_=lg[:], func=AF.Exp), reads=[], writes=[b_lg])
        S.op("dve", lambda e: e.tensor_reduce(out=sm[:], in_=lg[:], axis=AX.X, op=ALU.add), reads=[b_lg], writes=[b_mx])
        S.op("dve", lambda e: e.reciprocal(out=sm[:], in_=sm[:]), reads=[], writes=[b_mx])
        S.op("dve", lambda e: e.tensor_tensor(out=K.aff[:], in0=lg[:], in1=sm[:].unsqueeze(2).to_broadcast([128, NT, NE]), op=ALU.mult),
             reads=[b_lg, b_mx], writes=[K.b_aff])
        K.tap("aff", K.aff[:], K.b_aff, [128, NT, NE])
        S.barrier()


def phase_f(K):
    nc, S, I = K.nc, K.S, K.I
    U32 = mybir.dt.uint32
    aff, b_aff = K.aff, K.b_aff
    ZR = NE * CAP
    b_R = Buf()
    K.sc_moe = K.scratch("sc_moe", [L, D], BF16)
    K.sc_moe2 = K.scratch("sc_moe2", [L, D], BF16)
    b_moe2 = Buf()
    with ExitStack() as ph:
        sb = lambda name, shape, dt=F32: K.sb("f_" + name, shape, dt, stack=ph)
        ones = sb("ones", [128, 128]); onesb = sb("onesb", [128, 128], BF16); b_one = Buf()
        S.op("dve", lambda e: e.memset(ones[:], 1.0), writes=[b_one])
        S.op("dve", lambda e: e.memset(onesb[:], 1.0), writes=[b_one])
        identb = sb("identb", [128, 128], BF16); b_idb = Buf()
        S.op("act", lambda e: e.activation(out=identb[:], in_=K.identt[:], func=AF.Copy), reads=[K.b_ident], writes=[b_idb])
        zt = sb("zt", [128, 8, D], BF16); b_zt = Buf()
        S.op("pool", lambda e: e.memset(zt[:], 0.0), writes=[b_zt])
        for zi in range(4):
            S.dma(lambda q: q.dma_start(out=K.sc_moe[zi * 1024:(zi + 1) * 1024, :].rearrange("(a p) d -> p a d", p=128), in_=zt[:]),
                  reads=[b_zt], writes=[b_R])
        ltri = sb("ltri", [128, 128], BF16); iotaJ = sb("iotaJ", [128, 512]); eoff = sb("eoff", [128, NE]); b_cst = Buf()
        S.dma(lambda q: q.dma_start(out=ltri[:], in_=I["ltri"]), writes=[b_cst])
        S.dma(lambda q: q.dma_start(out=iotaJ[:], in_=I["iotaJ"]), writes=[b_cst])
        S.dma(lambda q: q.dma_start(out=eoff[:], in_=I["eoff"]), writes=[b_cst])
        rhs5 = sb("rhs5", [128, NT, NE, 5], BF16); b_r5 = Buf()
        tidxt = sb("tidxt", [128, NT, 2], BF16); b_tidx = Buf()
        S.dma(lambda q: q.dma_start(out=tidxt[:], in_=I["tidx"]), writes=[b_tidx])
        S.op("dve", lambda e: e.tensor_copy(out=rhs5[:, :, :, 0:2], in_=tidxt[:].unsqueeze(2).to_broadcast([128, NT, NE, 2])),
             reads=[b_tidx], writes=[b_r5])
        lo = sb("lo", [128, NE]); posm = sb("posm", [128, NT, NE])
        g5 = sb("g5", [128, D]); nfg = sb("nfg", [128, D]); b_g5 = Buf()
        S.dma(lambda q: q.dma_start(out=g5[:], in_=K.sc_mod[:, 5 * D:6 * D]), reads=[K.b_scmod], writes=[b_g5])
        S.dma(lambda q: q.dma_start(out=nfg[:], in_=I["norm_final_g"].partition_broadcast(128)), writes=[b_g5])
        pht = ExitStack()
        sb_main = sb
        sb = lambda name, shape, dt=F32: K.sb("ft_" + name, shape, dt, stack=pht)
        r1 = sb("r1", [128, NT, NE]); r2 = sb("r2", [128, NT, NE]); b_r = Buf()
        S.op("act", lambda e: e.activation(out=rhs5[:, :, :, 2], in_=aff[:], func=AF.Copy), reads=[b_aff], writes=[b_r5])
        S.op("dve", lambda e: e.tensor_tensor(out=r1[:], in0=aff[:], in1=rhs5[:, :, :, 2], op=ALU.subtract), reads=[b_aff, b_r5], writes=[b_r])
        S.op("act", lambda e: e.activation(out=rhs5[:, :, :, 3], in_=r1[:], func=AF.Copy), reads=[b_r], writes=[b_r5])
        S.op("dve", lambda e: e.tensor_tensor(out=r2[:], in0=r1[:], in1=rhs5[:, :, :, 3], op=ALU.subtract), reads=[b_r, b_r5], writes=[b_r])
        S.op("act", lambda e: e.activation(out=rhs5[:, :, :, 4], in_=r2[:], func=AF.Copy), reads=[b_r], writes=[b_r5])
        hi = sb("hi", [128, NE]); mid = sb("mid", [128, NE]); cntp = sb("cntp", [128, NE])
        cond = sb("cond", [128, NE], U32); ncond = sb("ncond", [128, NE], U32)
        cmp_ = sb("cmp", [128, NT, NE])
        b_lo, b_hi, b_mid, b_cnt, b_cond, b_cmp = Buf(), Buf(), Buf(), Buf(), Buf(), Buf()
        S.op("dve", lambda e: e.memset(lo[:], 0.0), writes=[b_lo])
        S.op("dve", lambda e: e.memset(hi[:], 1.0), writes=[b_hi])
        for it in range(36):
            S.op("dve", lambda e: e.tensor_tensor(out=mid[:], in0=lo[:], in1=hi[:], op=ALU.add), reads=[b_lo, b_hi], writes=[b_mid])
            S.op("dve", lambda e: e.tensor_scalar(out=mid[:], in0=mid[:], scalar1=0.5, scalar2=None, op0=ALU.mult), reads=[], writes=[b_mid])
            S.op("dve", lambda e: e.tensor_tensor(out=cmp_[:], in0=aff[:], in1=mid[:].unsqueeze(1).to_broadcast([128, NT, NE]), op=ALU.is_ge),
                 reads=[b_aff, b_mid], writes=[b_cmp])
            S.op("dve", lambda e: e.tensor_reduce(out=cntp[:], in_=cmp_[:].rearrange("p t e -> p e t"), axis=AX.X, op=ALU.add),
                 reads=[b_cmp], writes=[b_cnt])
            ps, pb = K.next_ps()
            S.op("pe", lambda e: e.matmul(ps[:, 0:NE], lhsT=ones[:], rhs=cntp[:], start=True, stop=True), reads=[b_one, b_cnt], writes=[pb])
            S.op("dve", lambda e: e.tensor_scalar(out=cond[:], in0=ps[:, 0:NE], scalar1=float(CAP), scalar2=None, op0=ALU.is_ge),
                 reads=[pb], writes=[b_cond])
            S.op("dve", lambda e: e.tensor_scalar(out=ncond[:], in0=ps[:, 0:NE], scalar1=float(CAP), scalar2=None, op0=ALU.is_lt),
                 reads=[pb], writes=[b_cond])
            S.op("dve", lambda e: e.copy_predicated(out=lo[:], mask=cond[:], data=mid[:]), reads=[b_cond, b_mid], writes=[b_lo])
            S.op("dve", lambda e: e.copy_predicated(out=hi[:], mask=ncond[:], data=mid[:]), reads=[b_cond, b_mid], writes=[b_hi])
        msk = sb("msk", [128, NT, NE]); mskb = sb("mskb", [128, NT, NE], BF16)
        off = sb("off", [128, NT, NE]); b_m, b_pos, b_off = Buf(), Buf(), Buf()
        S.op("dve", lambda e: e.tensor_tensor(out=msk[:], in0=aff[:], in1=lo[:].unsqueeze(1).to_broadcast([128, NT, NE]), op=ALU.is_ge),
             reads=[b_aff, b_lo], writes=[b_m])
        S.op("act", lambda e: e.activation(out=mskb[:], in_=msk[:], func=AF.Copy), reads=[b_m], writes=[b_m])
        psw, pbw = K.next_ps()
        S.op("pe", lambda e: e.matmul(psw[:], lhsT=ltri[:], rhs=mskb[:].rearrange("p t e -> p (t e)"), start=True, stop=True),
             reads=[b_cst, b_m], writes=[pbw])
        pst, pbt = K.next_ps()
        S.op("pe", lambda e: e.matmul(pst[:], lhsT=onesb[:], rhs=mskb[:].rearrange("p t e -> p (t e)"), start=True, stop=True),
             reads=[b_one, b_m], writes=[pbt])
        tot = sb("tot", [128, NT, NE])
        S.op("act", lambda e: e.activation(out=tot[:].rearrange("p t e -> p (t e)"), in_=pst[:], func=AF.Copy), reads=[pbt], writes=[b_off])
        S.op("dve", lambda e: e.memset(off[:, 0, :], 0.0), writes=[b_off])
        for ti in range(NT - 1):
            S.op("dve", lambda e: e.tensor_tensor(out=off[:, ti + 1, :], in0=off[:, ti, :], in1=tot[:, ti, :], op=ALU.add),
                 reads=[], writes=[b_off])
        S.op("dve", lambda e: e.tensor_tensor(out=posm[:].rearrange("p t e -> p (t e)"), in0=psw[:], in1=off[:].rearrange("p t e -> p (t e)"),
                                              op=ALU.add), reads=[pbw, b_off], writes=[b_pos])
        S.op("dve", lambda e: e.scalar_tensor_tensor(out=posm[:], in0=posm[:], scalar=1.0, in1=msk[:], op0=ALU.add, op1=ALU.mult),
             reads=[b_m], writes=[b_pos])
        S.op("dve", lambda e: e.tensor_scalar(out=posm[:], in0=posm[:], scalar1=-1.0, scalar2=None, op0=ALU.add), reads=[], writes=[b_pos])
        K.tap("posm", posm[:], b_pos, [128, NT, NE])
        S.barrier()
        pht.close()
        phx = ExitStack()
        sb = lambda name, shape, dt=F32: K.sb("fx_" + name, shape, dt, stack=phx)
        xs = sb("xs", [128, 4, D], BF16); b_xsr = [Buf() for _ in range(4)]
        idx8 = sb("idx8", [128, 4, 5]); idxf = sb("idxf", [128, 4]); idxu = sb("idxu", [128, 4], U32); valj = sb("valj", [128, 4]); b_idx = Buf()
        idxu2 = [sb("idxu2_%d" % i, [128, 4], U32) for i in range(2)]; b_idx2 = [Buf(), Buf()]
        Sg = [sb("Sg%d" % i, [128, 512], BF16) for i in range(3)]; b_Sg = [Buf() for _ in range(3)]
        xsT = sb("xsT", [128, 8, 512], BF16); b_xs = Buf()
        hidT = sb("hidT", [128, 8, 512], BF16); b_hid = Buf()
        Yw = [sb("Yw%d" % i, [128, 4, D], BF16) for i in range(2)]; b_Yw = [Buf(), Buf()]
        wg = [sb("wg%d" % i, [128, 8, 128], BF16) for i in range(2)]; b_wg = [Buf(), Buf()]
        wu = [sb("wu%d" % i, [128, 8, 128], BF16) for i in range(2)]; b_wu = [Buf(), Buf()]
        wd = [sb("wd%d" % i, [128, D], BF16) for i in range(2)]; b_wd = [Buf(), Buf()]
        sgt = [sb("sgt%d" % i, [128, 512]) for i in range(2)]; b_sgt = [Buf(), Buf()]
        for ex in range(NE // 2):
            psi, pbi = K.next_ps()
            for ti in range(NT):
                k = ti % 3
                S.op("dve", lambda e: e.tensor_scalar(out=Sg[k][:], in0=iotaJ[:], scalar1=posm[:, ti, ex:ex + 1], scalar2=None, op0=ALU.is_equal),
                     reads=[b_cst, b_pos], writes=[b_Sg[k]])
                for jc in range(4):
                    S.op("pe", lambda e: e.matmul(psi[:, 5 * jc:5 * jc + 5], lhsT=Sg[k][:, jc * 128:(jc + 1) * 128], rhs=rhs5[:, ti, ex, :],
                                                  start=(ti == 0 and jc == 0), stop=(ti == NT - 1 and jc == 3)),
                         reads=[b_Sg[k], b_r5], writes=[pbi])
            S.op("act", lambda e: e.activation(out=idx8[:].rearrange("p a b -> p (a b)"), in_=psi[:, 0:20], func=AF.Copy), reads=[pbi], writes=[b_idx])
            S.op("dve", lambda e: e.scalar_tensor_tensor(out=idxf[:].unsqueeze(2), in0=idx8[:, :, 0:1], scalar=64.0, in1=idx8[:, :, 1:2],
                                                         op0=ALU.mult, op1=ALU.add), reads=[], writes=[b_idx])
            S.op("dve", lambda e: e.tensor_copy(out=idxu[:], in_=idxf[:]), reads=[], writes=[b_idx])
            S.op("dve", lambda e: e.tensor_copy(out=idxu2[ex % 2][:], in_=idxf[:]), reads=[b_idx], writes=[b_idx2[ex % 2]])
            S.op("dve", lambda e: e.tensor_tensor(out=valj[:].unsqueeze(2), in0=idx8[:, :, 3:4], in1=idx8[:, :, 4:5], op=ALU.add), reads=[], writes=[b_idx])
            S.op("dve", lambda e: e.tensor_tensor(out=valj[:].unsqueeze(2), in0=valj[:].unsqueeze(2), in1=idx8[:, :, 2:3], op=ALU.add), reads=[], writes=[b_idx])
            for jc in range(4):
                S.dma(lambda q: q.indirect_dma_start(out=xs[:, jc, :], out_offset=None, in_=K.sc_hn2,
                                                     in_offset=bass.IndirectOffsetOnAxis(ap=idxu[:, jc:jc + 1], axis=0)),
                      reads=[b_idx, K.b_scr], writes=[b_xsr[jc]], q="pool")
            for jc in range(4):
                ps, pb = K.next_ps()
                pv = ps[:].bitcast(BF16)
                for dc in range(8):
                    S.op("pe", lambda e: e.transpose(out=pv[:, dc * 128:(dc + 1) * 128], in_=xs[:, jc, dc * 128:(dc + 1) * 128], identity=identb[:]),
                         reads=[b_xsr[jc], b_idb], writes=[pb])
                S.op("act" if jc % 2 else "dve",
                     (lambda e: e.activation(out=xsT[:, :, jc * 128:(jc + 1) * 128], in_=pv.rearrange("p (a b) -> p a b", a=8), func=AF.Copy)) if jc % 2 else
                     (lambda e: e.tensor_copy(out=xsT[:, :, jc * 128:(jc + 1) * 128], in_=pv.rearrange("p (a b) -> p a b", a=8))),
                     reads=[pb], writes=[b_xs])
            for fc in range(8):
                k = fc % 2
                S.dma(lambda q: q.dma_start(out=wg[k][:], in_=I["w_gate"][ex].rearrange("(kc p) f -> p kc f", p=128)[:, :, fc * 128:(fc + 1) * 128]),
                      writes=[b_wg[k]], q="pool")
                S.dma(lambda q: q.dma_start(out=wu[k][:], in_=I["w_up"][ex].rearrange("(kc p) f -> p kc f", p=128)[:, :, fc * 128:(fc + 1) * 128]),
                      writes=[b_wu[k]], q="pool")
                psg, pbg = K.next_ps()
                for dc in range(8):
                    S.op("pe", lambda e: e.matmul(psg[:], lhsT=wg[k][:, dc, :], rhs=xsT[:, dc, :], start=(dc == 0), stop=(dc == 7)),
                         reads=[b_wg[k], b_xs], writes=[pbg])
                psu, pbu = K.next_ps()
                for dc in range(8):
                    S.op("pe", lambda e: e.matmul(psu[:], lhsT=wu[k][:, dc, :], rhs=xsT[:, dc, :], start=(dc == 0), stop=(dc == 7)),
                         reads=[b_wu[k], b_xs], writes=[pbu])
                S.op("act", lambda e: e.activation(out=sgt[k][:], in_=psg[:], func=AF.Silu), reads=[pbg], writes=[b_sgt[k]])
                S.op("dve", lambda e: e.tensor_tensor(out=hidT[:, fc, :], in0=psu[:], in1=sgt[k][:], op=ALU.mult), reads=[pbu, b_sgt[k]], writes=[b_hid])
            banks = [K.next_ps() for _ in range(8)]
            for fc in range(8):
                k = fc % 2
                S.dma(lambda q: q.dma_start(out=wd[k][:], in_=I["w_down"][ex, fc * 128:(fc + 1) * 128, :]), writes=[b_wd[k]], q="pool")
                for jc in range(4):
                    for half in range(2):
                        ps, pb = banks[jc * 2 + half]
                        S.op("pe", lambda e: e.matmul(ps[:], lhsT=hidT[:, fc, jc * 128:(jc + 1) * 128], rhs=wd[k][:, half * 512:(half + 1) * 512],
                                                      start=(fc == 0), stop=(fc == 7)), reads=[b_hid, b_wd[k]], writes=[pb])
            yw, byw = Yw[ex % 2], b_Yw[ex % 2]
            for jc in range(4):
                for half in range(2):
                    ps, pb = banks[jc * 2 + half]
                    S.op("dve", lambda e: e.scalar_tensor_tensor(out=yw[:, jc, half * 512:(half + 1) * 512], in0=ps[:], scalar=valj[:, jc:jc + 1],
                                                                 in1=g5[:, half * 512:(half + 1) * 512], op0=ALU.mult, op1=ALU.mult),
                         reads=[pb, b_idx, b_g5], writes=[byw])
            for jc in range(4):
                S.dma(lambda q: q.indirect_dma_start(out=K.sc_moe, out_offset=bass.IndirectOffsetOnAxis(ap=idxu2[ex % 2][:, jc:jc + 1], axis=0),
                                                     in_=yw[:, jc, :], in_offset=None, compute_op=ALU.add),
                      reads=[byw, b_idx2[ex % 2]], writes=[b_R], q="pool")
        S.barrier()
        phx.close()
        sb = sb_main
        for ch in range(2):
            S.coll(lambda g: g.collective_compute("AllReduce", ALU.add, replica_groups=[[0, 1], [2, 3], [4, 5], [6, 7]],
                                                  ins=[K.sc_moe[ch * 2048:(ch + 1) * 2048, :]], outs=[K.sc_moe2[ch * 2048:(ch + 1) * 2048, :]]),
                   reads=[b_R], writes=[b_moe2])
        x1t = [sb("x1t%d" % i, [128, D]) for i in range(3)]; b_x1t = [Buf(), Buf(), Buf()]
        mt = [sb("mt%d" % i, [128, D], BF16) for i in range(3)]; b_mt = [Buf(), Buf(), Buf()]
        junk = sb("junk", [128, D]); b_junk = Buf()
        st = sb("st", [128, 2 * NT]); b_st = Buf()
        def f_load(ti):
            k = ti % 3
            rows = slice(ti * 128, (ti + 1) * 128)
            S.dma(lambda q: q.dma_start(out=x1t[k][:], in_=K.sc_x1[rows, :]), reads=[K.b_scr], writes=[b_x1t[k]])
            S.dma(lambda q: q.dma_start(out=mt[k][:], in_=K.sc_moe2[rows, :]), reads=[b_moe2], writes=[b_mt[k]])

        f_load(0)
        f_load(1)
        for ti in range(NT):
            k = ti % 3
            rows = slice(ti * 128, (ti + 1) * 128)
            if ti + 2 < NT:
                f_load(ti + 2)
            S.op("pool", lambda e: e.tensor_tensor(out=x1t[k][:], in0=x1t[k][:], in1=mt[k][:], op=ALU.add), reads=[b_mt[k]], writes=[b_x1t[k]])
            ss = st[:, 2 * ti:2 * ti + 1]
            rs = st[:, 2 * ti + 1:2 * ti + 2]
            S.op("act", lambda e: e.activation(out=junk[:], in_=x1t[k][:], func=AF.Square, accum_out=ss), reads=[b_x1t[k]], writes=[b_junk, b_st])
            S.op("act", lambda e: e.activation(out=rs, in_=ss, func=AF.Sqrt, scale=1.0 / D, bias=EPS), reads=[b_st], writes=[b_st])
            S.op("dve", lambda e: e.reciprocal(out=rs, in_=rs), reads=[b_st], writes=[b_st])
            S.op("dve", lambda e: e.scalar_tensor_tensor(out=x1t[k][:], in0=x1t[k][:], scalar=rs, in1=nfg[:], op0=ALU.mult, op1=ALU.mult),
                 reads=[b_st, b_g5], writes=[b_x1t[k]])
            S.dma(lambda q: q.dma_start(out=K.out[rows, :], in_=x1t[k][:]), reads=[b_x1t[k]])
        S.barrier()
```

```python
import math
from contextlib import ExitStack
import numpy as np
import concourse.bass as bass
import concourse.mybir as mybir
from concourse.bass_utils import run_bass_kernel_spmd

F32 = mybir.dt.float32
BF16 = mybir.dt.bfloat16
AF = mybir.ActivationFunctionType
ALU = mybir.AluOpType
AX = mybir.AxisListType

D = 1024
L = 4096
LC = 256
NT = L // 128
NTC = LC // 128
DIN = 3104
EPS = 1e-6
NF = 33
NE = 16
CAP = 512


class Buf:
    __slots__ = ("w", "r")

    def __init__(self):
        self.w = None
        self.r = {}


class Sch:
    def __init__(self, nc, es):
        self.nc = nc
        self.eng = {"pe": nc.tensor, "act": nc.scalar, "dve": nc.vector, "pool": nc.gpsimd, "sp": nc.sync}
        self.sem = {k: es.enter_context(nc.semaphore("s_" + k)) for k in self.eng}
        self.cnt = {k: 0 for k in self.eng}
        self.seen = {k: {} for k in self.eng}
        self.NDS = 24
        self.dsem = [es.enter_context(nc.semaphore("d%d" % i)) for i in range(self.NDS)]
        self.dcnt = [0] * self.NDS
        self.dnext = 0
        self.csem = es.enter_context(nc.semaphore("s_cc"))
        self.ccnt = 0

    def _wait(self, e, tok):
        if tok is None:
            return
        kind, key, val = tok
        if kind == "e" and key == e and e == "pe":
            return
        sk = (kind, key)
        if self.seen[e].get(sk, 0) >= val:
            return
        sem = self.sem[key] if kind == "e" else (self.dsem[key] if kind == "d" else self.csem)
        self.eng[e].wait_ge(sem, val)
        self.seen[e][sk] = val

    def _deps(self, e, reads, writes):
        for b in reads:
            self._wait(e, b.w)
        for b in writes:
            self._wait(e, b.w)
            for t in list(b.r.values()):
                self._wait(e, t)

    def op(self, e, fn, reads=(), writes=()):
        self._deps(e, reads, writes)
        inst = fn(self.eng[e])
        self.cnt[e] += 1
        inst.then_inc(self.sem[e], 1)
        tok = ("e", e, self.cnt[e])
        for b in reads:
            b.r[e] = tok
        for b in writes:
            b.w = tok
            b.r = {}
        return tok

    def dma(self, fn, reads=(), writes=(), q="sp"):
        i = self.dnext
        self.dnext = (i + 1) % self.NDS
        if self.dcnt[i] > 0:
            self._wait(q, ("d", i, self.dcnt[i]))
        self._deps(q, reads, writes)
        inst = fn(self.eng[q])
        self.dcnt[i] += 16
        inst.then_inc(self.dsem[i], 16)
        tok = ("d", i, self.dcnt[i])
        for b in reads:
            b.r[("d", i)] = tok
        for b in writes:
            b.w = tok
            b.r = {}
        return tok

    def coll(self, fn, reads=(), writes=()):
        self._deps("pool", reads, writes)
        inst = fn(self.eng["pool"])
        self.ccnt += 1
        inst.then_inc(self.csem)
        tok = ("c", 0, self.ccnt)
        for b in reads:
            b.r[("c", 0)] = tok
        for b in writes:
            b.w = tok
            b.r = {}
        return tok

    def barrier(self, wait_coll=False):
        for e in ("pe", "act", "dve", "pool", "sp"):
            for f in ("pe", "act", "dve", "pool"):
                if self.cnt[f]:
                    self._wait(e, ("e", f, self.cnt[f]))
            for i in range(self.NDS):
                if self.dcnt[i]:
                    self._wait(e, ("d", i, self.dcnt[i]))
            if self.ccnt and wait_coll:
                self._wait(e, ("c", 0, self.ccnt))


def host_consts():
    f32 = np.float32
    rows = L // 64
    r = np.repeat(np.arange(rows, dtype=f32), 64)
    col = np.tile(np.arange(64, dtype=f32), rows)
    quarter = D // 4
    omega = (1.0 / (np.float32(10000.0) ** (np.arange(quarter, dtype=f32) / np.float32(quarter)))).astype(f32)
    er = r[:, None] * omega
    ec = col[:, None] * omega
    pos = np.concatenate([np.sin(er), np.cos(er), np.sin(ec), np.cos(ec)], axis=-1).astype(f32)
    ident = np.eye(128, dtype=f32)
    si = np.arange(128)[:, None]
    ci = np.arange(128)[None, :]
    same = (si // 64) == (ci // 64)
    maskF = (same & (si <= ci)).astype(f32)
    maskB = (same & (si >= ci)).astype(f32)
    import ml_dtypes
    bf = ml_dtypes.bfloat16
    def zfeat(idx):
        t = (idx / np.float32(L - 1)).astype(f32)[:, None]
        w = (2.0 * math.pi * idx[:, None] / L).astype(f32)
        fb = np.linspace(1e-4, 15, 16, dtype=f32)[None, :]
        return np.concatenate([t, np.cos(fb * w), -np.sin(fb * w)], axis=-1).astype(f32), t[:, 0]
    idx = np.arange(L, dtype=f32)
    z0, t0_ = zfeat(idx)
    z1, t1_ = zfeat(idx + 1.0)
    z0[:, 0] = np.linspace(0.0, 1.0, L, dtype=f32)
    t0_ = np.linspace(0.0, 1.0, L, dtype=f32)
    zT = np.ascontiguousarray(np.stack([z0.T, z1.T], axis=1))
    tnorm = np.ascontiguousarray(np.stack([t0_.reshape(NT, 128).T, t1_.reshape(NT, 128).T], axis=1))
    deltas = np.abs(np.linspace(math.log(1e-2) / 1.5, math.log(1e-2) / 0.3, 512, dtype=f32)).astype(f32)
    Nb, Lb, FCn = 2048, 1024, 9
    fidx = np.arange(FCn * 128)
    fs = np.where(fidx > Lb, 0.0, np.where((fidx == 0) | (fidx == Lb), 1.0 / Nb, 2.0 / Nb))
    sg = np.where(fidx % 2 == 0, 1.0, -1.0)
    th = 2.0 * np.pi / Nb
    cfv, sfv = np.cos(th * fidx), np.sin(th * fidx)
    hcoef = np.stack([fs, sg, sg * cfv, sg * sfv, -sg * sfv, -sg * cfv], axis=0)
    hcoef = np.ascontiguousarray(hcoef.reshape(6, FCn, 128).transpose(2, 0, 1)).astype(f32)
    fscale = np.zeros((128, NF), dtype=f32)
    ang = th * np.arange(Nb, dtype=np.float64)
    ctab, stab = np.cos(ang), np.sin(ang)
    a_ = np.arange(FCn * 128, dtype=np.int64)
    prod = (a_[:, None] * a_[None, :]) % Nb
    dft = np.empty((128, FCn, 2, FCn * 128), dtype=bf)
    dft[:, :, 0, :] = ctab[prod].astype(f32).reshape(FCn, 128, FCn * 128).transpose(1, 0, 2).astype(bf)
    dft[:, :, 1, :] = stab[prod].astype(f32).reshape(FCn, 128, FCn * 128).transpose(1, 0, 2).astype(bf)
    ltri = (np.arange(128)[:, None] < np.arange(128)[None, :]).astype(bf)
    iotaJ = np.tile(np.arange(512, dtype=f32)[None, :], (128, 1))
    jvec = (np.arange(128, dtype=f32)[:, None] + 128.0 * np.arange(4, dtype=f32)[None, :]).astype(f32)
    selE = np.tile(np.arange(16, dtype=f32)[:, None], (1, 128))
    eoff = np.tile((np.arange(NE, dtype=f32) * CAP - NE * CAP)[None, :], (128, 1)).astype(f32)
    tt_ = (np.arange(NT)[None, :] * 128 + np.arange(128)[:, None])
    tidx = np.stack([tt_ // 64, tt_ % 64], axis=-1).astype(bf)
    return {"pos": pos, "ident": ident, "maskF": maskF, "maskB": maskB, "zT": zT, "tnorm": tnorm, "deltas": deltas,
            "fscale": fscale, "dft": dft, "hcoef": hcoef, "ltri": ltri, "iotaJ": iotaJ, "jvec": jvec, "selE": selE, "tidx": tidx, "eoff": eoff}


class Ctx:
    pass


def build(upto=99, taps=()):
    nc = bass.Bass("TRN2", target_bir_lowering=False)
    es = ExitStack()
    S = Sch(nc, es)
    K = Ctx()
    K.nc, K.S, K.es = nc, S, es
    K.taps = {}
    K.tap_names = taps

    def din(name, shape, dt=F32):
        return nc.dram_tensor(name, list(shape), dt, kind="ExternalInput").ap()

    I = {}
    I["x"] = din("x", [L, D])
    I["ctx"] = din("ctx", [LC, D])
    I["pos"] = din("pos", [L, D])
    I["ident"] = din("ident", [128, 128])
    I["cc"] = din("cc", [128, 16])
    I["w_ada"] = din("w_ada", [D, 6 * D])
    I["b_ada"] = din("b_ada", [6 * D])
    I["norm_mix_g"] = din("norm_mix_g", [D])
    I["norm_ffn_g"] = din("norm_ffn_g", [D])
    I["norm_final_g"] = din("norm_final_g", [D])
    I["w_gl"] = din("w_gl", [D, 800])
    I["wa_f"] = din("wa_f", [16, 128])
    I["wa_b"] = din("wa_b", [16, 128])
    I["gla_ba"] = din("gla_ba", [128, 2])
    I["gla_norm_g"] = din("gla_norm_g", [128])
    I["hy_conv_w"] = din("hy_conv_w", [3, 768])
    I["hy_conv_b"] = din("hy_conv_b", [768])
    I["w_hy"] = din("w_hy", [D, 768])
    I["rmask"] = din("rmask", [128, 2])
    I["maskF"] = din("maskF", [128, 128])
    I["maskB"] = din("maskB", [128, 128])
    I["w_out"] = din("w_out", [D, D])
    I["zT"] = din("zT", [33, 2, L])
    I["hy_w1"] = din("hy_w1", [33, 64])
    I["hy_w2"] = din("hy_w2", [64, 64])
    I["hy_w3"] = din("hy_w3", [64, 1024])
    I["hy_fb"] = din("hy_fb", [64, 4])
    I["hy_bias"] = din("hy_bias", [2, 256])
    I["tnorm"] = din("tnorm", [128, 2, NT])
    I["deltas"] = din("deltas", [256])
    I["fscale"] = din("fscale", [128, NF])
    I["dft"] = din("dft", [128, 9, 2, 1152], BF16)
    I["hcoef"] = din("hcoef", [128, 6, 9])
    I["w_router"] = din("w_router", [D, NE])
    I["w_gate"] = din("w_gate", [NE // 2, D, D])
    I["w_up"] = din("w_up", [NE // 2, D, D])
    I["w_down"] = din("w_down", [NE // 2, D, D])
    I["ltri"] = din("ltri", [128, 128], BF16)
    I["iotaJ"] = din("iotaJ", [128, 512])
    I["jvec"] = din("jvec", [128, 4])
    I["eoff"] = din("eoff", [128, NE])
    I["tidx"] = din("tidx", [128, NT, 2], BF16)
    I["selE"] = din("selE", [16, 128])

    def scratch(name, shape, dt=F32):
        kind = "ExternalOutput" if name in taps else "Internal"
        return nc.dram_tensor(("tap_" if name in taps else "") + name, list(shape), dt, kind=kind).ap()

    K.scratch = scratch
    out = nc.dram_tensor("out", [L, D], F32, kind="ExternalOutput").ap()
    K.I, K.out = I, out

    def sb(name, shape, dt=F32, stack=None):
        return (stack or es).enter_context(nc.sbuf_tensor("sb_" + name, list(shape), dt))

    K.sb = sb
    K.ps = [es.enter_context(nc.psum_tensor("ps%d" % i, [128, 512], F32)) for i in range(8)]
    K.psb = [Buf() for _ in range(8)]
    K.psi = 0

    def next_ps():
        i = K.psi
        K.psi = (i + 1) % 8
        return K.ps[i], K.psb[i]

    K.next_ps = next_ps

    def tap(name, ap_sb, buf, shape, dt=F32):
        if name not in K.tap_names:
            return
        t = nc.dram_tensor("tap_" + name, list(shape), dt, kind="ExternalOutput").ap()
        S.dma(lambda q: q.dma_start(out=t, in_=ap_sb), reads=[buf])
        K.taps[name] = t

    K.tap = tap

    phase_a(K)
    if upto >= 2:
        phase_b(K)
    if upto >= 3:
        phase_c(K)
        K.es_bc.close()
    if upto >= 4:
        phase_g(K)
    if upto >= 5:
        phase_gn(K)
    if upto >= 6:
        phase_h(K)
    if upto >= 7:
        phase_e(K)
    if upto >= 8:
        phase_f(K)
    S.barrier(wait_coll=True)
    return nc, K


def phase_a(K):
    nc, S, I = K.nc, K.S, K.I
    K.identt = K.sb("identt", [128, 128])
    K.es_bc = ExitStack()
    K.modx = K.sb("modx", [128, 6 * D], stack=K.es_bc)
    K.modc = K.sb("modc", [128, 2 * D], stack=K.es_bc)
    K.b_modx = [Buf() for _ in range(12)]
    K.b_modc = [Buf() for _ in range(4)]
    K.b_ident = Buf()
    S.dma(lambda q: q.dma_start(out=K.identt[:], in_=I["ident"]), writes=[K.b_ident])
    with ExitStack() as ph:
        cc = K.sb("cc", [128, 16], stack=ph)
        sc = K.sb("sc", [128, 16], stack=ph)
        rep = K.sb("rep", [128, 16, 128], stack=ph)
        bada = K.sb("bada", [128, 6 * D], stack=ph)
        wst = [K.sb("wst%d" % i, [128, 8, 512], stack=ph) for i in range(2)]
        b_cc, b_sc, b_rep, b_bada = Buf(), Buf(), Buf(), Buf()
        b_wst = [Buf(), Buf()]
        S.dma(lambda q: q.dma_start(out=cc[:], in_=I["cc"]), writes=[b_cc])
        S.dma(lambda q: q.dma_start(out=bada[:], in_=I["b_ada"].partition_broadcast(128)), writes=[b_bada])
        S.op("act", lambda e: e.activation(out=sc[:], in_=cc[:], func=AF.Silu), reads=[b_cc], writes=[b_sc])
        S.op("dve", lambda e: e.tensor_copy(out=rep[:], in_=sc[:].unsqueeze(2).to_broadcast([128, 16, 128])),
             reads=[b_sc], writes=[b_rep])
        wv = I["w_ada"].rearrange("(kc p) j -> p kc j", p=128)
        for jc in range(12):
            w, bw = wst[jc % 2], b_wst[jc % 2]
            S.dma(lambda q: q.dma_start(out=w[:], in_=wv[:, :, jc * 512:(jc + 1) * 512]), writes=[bw])
            for s in range(2):
                if s == 1 and jc >= 4:
                    continue
                ps, pb = K.next_ps()
                for kc in range(8):
                    S.op("pe", lambda e: e.matmul(ps[:], lhsT=rep[:, s * 8 + kc, :], rhs=w[:, kc, :],
                                                  start=(kc == 0), stop=(kc == 7)),
                         reads=[b_rep, bw], writes=[pb])
                dst = (K.modx if s == 0 else K.modc)
                db = (K.b_modx if s == 0 else K.b_modc)[jc]
                S.op("dve", lambda e: e.tensor_tensor(out=dst[:, jc * 512:(jc + 1) * 512], in0=ps[:],
                                                      in1=bada[:, jc * 512:(jc + 1) * 512], op=ALU.add),
                     reads=[pb, b_bada], writes=[db])
        gm = K.sb("gm", [128, D], stack=ph)
        gf = K.sb("gf", [128, D], stack=ph)
        b_gm, b_gf = Buf(), Buf()
        S.dma(lambda q: q.dma_start(out=gm[:], in_=I["norm_mix_g"].partition_broadcast(128)), writes=[b_gm])
        S.dma(lambda q: q.dma_start(out=gf[:], in_=I["norm_ffn_g"].partition_broadcast(128)), writes=[b_gf])
        for (t, bl, lo, g, bg) in ((K.modx, K.b_modx, 1, gm, b_gm), (K.modx, K.b_modx, 4, gf, b_gf),
                                   (K.modc, K.b_modc, 1, gm, b_gm)):
            for h in range(2):
                sl = slice(lo * D + h * 512, lo * D + (h + 1) * 512)
                S.op("dve", lambda e: e.scalar_tensor_tensor(out=t[:, sl], in0=t[:, sl], scalar=1.0,
                                                             in1=g[:, h * 512:(h + 1) * 512],
                                                             op0=ALU.add, op1=ALU.mult),
                     reads=[bl[lo * 2 + h], bg], writes=[bl[lo * 2 + h]])
        K.tap("modx", K.modx[:], K.b_modx[11], [128, 6 * D])
        K.tap("modc", K.modc[:], K.b_modc[3], [128, 2 * D])
        S.barrier()


def phase_b(K):
    nc, S, I = K.nc, K.S, K.I
    NCOL = L + 2 + LC
    K.hnT = K.sb("hnT", [128, 8, NCOL], BF16, stack=K.es_bc)
    K.b_hnT = [Buf() for _ in range(NT + NTC)]
    b_pad = Buf()
    S.op("pool", lambda e: e.memset(K.hnT[:, :, 0:1], 0.0), writes=[b_pad])
    S.op("pool", lambda e: e.memset(K.hnT[:, :, L + 1:L + 2], 0.0), writes=[b_pad])
    with ExitStack() as ph:
        xt = [K.sb("xt%d" % i, [128, D], stack=ph) for i in range(2)]
        pt = [K.sb("pt%d" % i, [128, D], stack=ph) for i in range(2)]
        hn = [K.sb("hn%d" % i, [128, D], stack=ph) for i in range(2)]
        junk = K.sb("junk", [128, D], stack=ph)
        st = K.sb("st", [128, 4 * (NT + NTC)], stack=ph)
        b_xt, b_pt, b_hn = [Buf(), Buf()], [Buf(), Buf()], [Buf(), Buf()]
        b_junk, b_st = Buf(), Buf()
        for ti in range(NT + NTC):
            isx = ti < NT
            k = ti % 2
            x_, p_, h_ = xt[k], pt[k], hn[k]
            src = I["x"][ti * 128:(ti + 1) * 128, :] if isx else I["ctx"][(ti - NT) * 128:(ti - NT + 1) * 128, :]
            S.dma(lambda q: q.dma_start(out=x_[:], in_=src), writes=[b_xt[k]])
            if isx:
                S.dma(lambda q: q.dma_start(out=p_[:], in_=I["pos"][ti * 128:(ti + 1) * 128, :]), writes=[b_pt[k]])
                S.op("pool", lambda e: e.tensor_tensor(out=x_[:], in0=x_[:], in1=p_[:], op=ALU.add),
                     reads=[b_pt[k]], writes=[b_xt[k]])
            ss = st[:, 4 * ti:4 * ti + 1]
            rs = st[:, 4 * ti + 1:4 * ti + 2]
            S.op("act", lambda e: e.activation(out=junk[:], in_=x_[:], func=AF.Square, accum_out=ss),
                 reads=[b_xt[k]], writes=[b_junk, b_st])
            S.op("act", lambda e: e.activation(out=rs, in_=ss, func=AF.Sqrt, scale=1.0 / D, bias=EPS),
                 reads=[b_st], writes=[b_st])
            S.op("dve", lambda e: e.reciprocal(out=rs, in_=rs), reads=[b_st], writes=[b_st])
            G = K.modx[:, D:2 * D] if isx else K.modc[:, D:2 * D]
            Sh = K.modx[:, 0:D] if isx else K.modc[:, 0:D]
            S.op("dve", lambda e: e.scalar_tensor_tensor(out=h_[:], in0=x_[:], scalar=rs, in1=G,
                                                         op0=ALU.mult, op1=ALU.mult),
                 reads=[b_xt[k], b_st], writes=[b_hn[k]])
            S.op("pool", lambda e: e.tensor_tensor(out=h_[:], in0=h_[:], in1=Sh, op=ALU.add),
                 reads=[], writes=[b_hn[k]])
            c0 = 1 + ti * 128 if isx else L + 2 + (ti - NT) * 128
            for hh in range(2):
                ps, pb = K.next_ps()
                for j in range(4):
                    kc = hh * 4 + j
                    S.op("pe", lambda e: e.transpose(out=ps[:, j * 128:(j + 1) * 128],
                                                     in_=h_[:, kc * 128:(kc + 1) * 128], identity=K.identt[:]),
                         reads=[b_hn[k], K.b_ident], writes=[pb])
                S.op("act", lambda e: e.activation(out=K.hnT[:, hh * 4:hh * 4 + 4, c0:c0 + 128],
                                                   in_=ps[:].rearrange("p (a b) -> p a b", a=4), func=AF.Copy),
                     reads=[pb], writes=[K.b_hnT[ti]])
        K.tap("hnT", K.hnT[:], K.b_hnT[NT + NTC - 1], [128, 8, NCOL], BF16)
        K.sc_mod = K.scratch("sc_mod", [128, 6 * D])
        K.b_scmod = Buf()
        S.dma(lambda q: q.dma_start(out=K.sc_mod, in_=K.modx[:]), writes=[K.b_scmod])
        S.barrier()


def make_in_maps(inputs):
    hc = host_consts()
    maps = []
    for core in range(8):
        b = core // 2
        r = core % 2
        cc = np.stack([np.asarray(inputs["c"][b]), np.asarray(inputs["c_ctx"])]).reshape(2, 8, 128)
        cc = np.ascontiguousarray(cc.transpose(2, 0, 1).reshape(128, 16)).astype(np.float32)
        m = {
            "x": np.ascontiguousarray(inputs["x"][b]),
            "ctx": np.ascontiguousarray(inputs["ctx"][b]),
            "pos": hc["pos"],
            "ident": hc["ident"],
            "cc": cc,
            "w_ada": np.ascontiguousarray(inputs["w_ada"][0]),
            "b_ada": np.ascontiguousarray(inputs["b_ada"][0]),
            "norm_mix_g": np.ascontiguousarray(inputs["norm_mix_g"][0]),
            "norm_ffn_g": np.ascontiguousarray(inputs["norm_ffn_g"][0]),
            "norm_final_g": np.ascontiguousarray(inputs["norm_final_g"]),
            "w_gl": np.ascontiguousarray(np.concatenate([inputs["w_in"][0][:, r * 128:(r + 1) * 128],
                                                         inputs["w_in"][0][:, 256 + r * 128:256 + (r + 1) * 128],
                                                         inputs["w_in"][0][:, 512 + r * 256:512 + (r + 1) * 256],
                                                         inputs["w_in"][0][:, 1024 + r * 256:1024 + (r + 1) * 256],
                                                         inputs["w_in"][0][:, 1536:1568]], axis=1)),
            "wa_f": np.ascontiguousarray(inputs["gla_wa_f"][0][:, r * 128:(r + 1) * 128]),
            "wa_b": np.ascontiguousarray(inputs["gla_wa_b"][0][:, r * 128:(r + 1) * 128]),
            "gla_ba": np.ascontiguousarray(np.stack([inputs["gla_ba_f"][0][r * 128:(r + 1) * 128],
                                                     inputs["gla_ba_b"][0][r * 128:(r + 1) * 128]], axis=1)),
            "gla_norm_g": np.ascontiguousarray(inputs["gla_norm_g"][0]),
            "hy_conv_w": np.ascontiguousarray(np.concatenate([inputs["hy_conv_w"][0][:, g * 512 + r * 256:g * 512 + (r + 1) * 256] for g in range(3)], axis=1)),
            "hy_conv_b": np.ascontiguousarray(np.concatenate([inputs["hy_conv_b"][0][g * 512 + r * 256:g * 512 + (r + 1) * 256] for g in range(3)])),
            "w_hy": np.ascontiguousarray(np.concatenate([inputs["w_in"][0][:, 1568 + g * 512 + r * 256:1568 + g * 512 + (r + 1) * 256] for g in range(3)], axis=1)),
            "rmask": np.ascontiguousarray(np.tile(np.array([[1.0 - r, float(r)]], dtype=np.float32), (128, 1))),
            "maskF": hc["maskF"],
            "maskB": hc["maskB"],
            "w_out": np.ascontiguousarray(inputs["w_out"][0]),
            "zT": hc["zT"], "tnorm": hc["tnorm"], "deltas": np.ascontiguousarray(hc["deltas"][r * 256:(r + 1) * 256]), "fscale": hc["fscale"], "dft": hc["dft"], "hcoef": hc["hcoef"],
            "ltri": hc["ltri"], "iotaJ": hc["iotaJ"], "jvec": hc["jvec"], "selE": hc["selE"], "tidx": hc["tidx"], "eoff": hc["eoff"],
            "hy_w1": np.ascontiguousarray(inputs["hy_w1"][0]),
            "hy_w2": np.ascontiguousarray(inputs["hy_w2"][0]),
            "hy_w3": np.ascontiguousarray(np.concatenate([inputs["hy_w3"][0][:, od * 512 + r * 256:od * 512 + (r + 1) * 256] for od in range(4)], axis=1)),
            "hy_fb": np.ascontiguousarray(np.stack([inputs["hy_freq"][0], inputs["hy_b1"][0], inputs["hy_b2"][0],
                                                    inputs["hy_b2"][0]], axis=1)),
            "hy_bias": np.ascontiguousarray(inputs["hy_bias"][0][:, r * 256:(r + 1) * 256]),
            "w_router": np.ascontiguousarray(np.concatenate([inputs["w_router"][0][:, r * 8:(r + 1) * 8],
                                                             inputs["w_router"][0][:, (1 - r) * 8:(2 - r) * 8]], axis=1)),
            "w_gate": np.ascontiguousarray(inputs["w_gate"][0][r * 8:(r + 1) * 8]),
            "w_up": np.ascontiguousarray(inputs["w_up"][0][r * 8:(r + 1) * 8]),
            "w_down": np.ascontiguousarray(inputs["w_down"][0][r * 8:(r + 1) * 8]),
        }
        maps.append(m)
    return maps


def kernel(**inputs):
    inputs = {k: np.asarray(v) for k, v in inputs.items()}
    nc, K = build()
    maps = make_in_maps(inputs)
    res = run_bass_kernel_spmd(nc, maps, core_ids=list(range(8)))
    outs = [res.results[2 * b]["out"] for b in range(4)]
    return np.stack(outs).astype(np.float32)


def tokcol(ti):
    return 1 + ti * 128 if ti < NT else L + 2 + (ti - NT) * 128


def phase_c(K):
    nc, S, I = K.nc, K.S, K.I
    K.sc_qk = K.scratch("sc_qk", [256, L + LC])
    K.sc_a = K.scratch("sc_a", [32, L + LC])
    K.sc_v = K.scratch("sc_v", [L + LC, 256], BF16)
    K.sc_sg = K.scratch("sc_sg", [L, 256])
    K.sc_u = K.scratch("sc_u", [L, 768])
    K.b_scr = Buf()
    with ExitStack() as ph:
        wb = [K.sb("wb%d" % i, [128, 8, 512], BF16, stack=ph) for i in range(2)]
        b_wb = [Buf(), Buf()]
        wj = [K.sb("wj%d" % j, [128, 8, 512], BF16, stack=ph) for j in range(3)]
        b_wj = [Buf(), Buf(), Buf()]
        cw = K.sb("cw", [128, 3, 512], stack=ph)
        cb = K.sb("cb", [128, 512], stack=ph)
        b_cw, b_cb = Buf(), Buf()
        stg = [K.sb("stg%d" % i, [128, 512], stack=ph) for i in range(3)]
        b_stg = [Buf(), Buf(), Buf()]
        stgb = [K.sb("stgb%d" % i, [128, 512], BF16, stack=ph) for i in range(2)]
        b_stgb = [Buf(), Buf()]
        wv = I["w_gl"].rearrange("(kc p) j -> p kc j", p=128)
        cnt = {"w": 0, "s": 0, "sb": 0}

        def loadw(c0, n):
            k = cnt["w"] % 2
            cnt["w"] += 1
            S.dma(lambda q: q.dma_start(out=wb[k][:, :, 0:n], in_=wv[:, :, c0:c0 + n]), writes=[b_wb[k]], q="pool")
            return wb[k], b_wb[k]

        def nstg():
            k = cnt["s"] % 3
            cnt["s"] += 1
            return stg[k], b_stg[k]

        allb = K.b_hnT
        tgs = [(1 + g * 512, g * 512, 512) for g in range(8)] + [(L + 2, L, 256)]
        for (c0w, nw, dst, rows) in ((0, 256, K.sc_qk, 128), (768, 32, K.sc_a, 32)):
            w, bw = loadw(c0w, nw)
            for cch in range(nw // rows):
                for (hc0, t0, n) in tgs:
                    ps, pb = K.next_ps()
                    for kc in range(8):
                        S.op("pe", lambda e: e.matmul(ps[0:rows, 0:n], lhsT=w[:, kc, cch * rows:(cch + 1) * rows],
                                                      rhs=K.hnT[:, kc, hc0:hc0 + n], start=(kc == 0), stop=(kc == 7)),
                             reads=[bw] + allb, writes=[pb])
                    st_, bs_ = nstg()
                    S.op("act", lambda e: e.activation(out=st_[0:rows, 0:n], in_=ps[0:rows, 0:n], func=AF.Copy),
                         reads=[pb], writes=[bs_])
                    S.dma(lambda q: q.dma_start(out=dst[cch * rows:(cch + 1) * rows, t0:t0 + n], in_=st_[0:rows, 0:n]),
                          reads=[bs_], writes=[K.b_scr])
        w, bw = loadw(256, 512)
        for ti in range(NT + NTC):
            col = tokcol(ti)
            ps, pb = K.next_ps()
            nn = 512 if ti < NT else 256
            for kc in range(8):
                S.op("pe", lambda e: e.matmul(ps[:, 0:nn], lhsT=K.hnT[:, kc, col:col + 128], rhs=w[:, kc, 0:nn],
                                              start=(kc == 0), stop=(kc == 7)), reads=[bw] + allb, writes=[pb])
            k = cnt["sb"] % 2
            cnt["sb"] += 1
            S.op("act", lambda e: e.activation(out=stgb[k][:, 0:256], in_=ps[:, 0:256], func=AF.Copy), reads=[pb], writes=[b_stgb[k]])
            S.dma(lambda q: q.dma_start(out=K.sc_v[ti * 128:(ti + 1) * 128, :], in_=stgb[k][:, 0:256]),
                  reads=[b_stgb[k]], writes=[K.b_scr])
            if ti < NT:
                st_, bs_ = nstg()
                S.op("act", lambda e: e.activation(out=st_[:, 0:256], in_=ps[:, 256:512], func=AF.Silu), reads=[pb], writes=[bs_])
                S.dma(lambda q: q.dma_start(out=K.sc_sg[ti * 128:(ti + 1) * 128, :], in_=st_[:, 0:256]),
                      reads=[bs_], writes=[K.b_scr])
        wvh = I["w_hy"].rearrange("(kc p) j -> p kc j", p=128)
        for (c0, n) in ((0, 512), (512, 256)):
            k = cnt["w"] % 2
            cnt["w"] += 1
            w, bw = wb[k], b_wb[k]
            S.dma(lambda q: q.dma_start(out=w[:, :, 0:n], in_=wvh[:, :, c0:c0 + n]), writes=[bw], q="pool")
            S.dma(lambda q: q.dma_start(out=cw[:, :, 0:n], in_=I["hy_conv_w"][:, c0:c0 + n].partition_broadcast(128)), writes=[b_cw])
            S.dma(lambda q: q.dma_start(out=cb[:, 0:n], in_=I["hy_conv_b"][c0:c0 + n].partition_broadcast(128)), writes=[b_cb])
            for j in range(3):
                S.op("dve", lambda e: e.tensor_tensor(out=wj[j][:, :, 0:n], in0=w[:, :, 0:n],
                                                      in1=cw[:, j:j + 1, 0:n].to_broadcast([128, 8, n]), op=ALU.mult),
                     reads=[bw, b_cw], writes=[b_wj[j]])
            for ti in range(NT):
                col = tokcol(ti)
                ps, pb = K.next_ps()
                m = 0
                for j in range(3):
                    for kc in range(8):
                        S.op("pe", lambda e: e.matmul(ps[:, 0:n], lhsT=K.hnT[:, kc, col + j - 1:col + j - 1 + 128],
                                                      rhs=wj[j][:, kc, 0:n], start=(m == 0), stop=(m == 23)),
                             reads=[b_wj[j]] + allb, writes=[pb])
                        m += 1
                st_, bs_ = nstg()
                S.op("dve", lambda e: e.tensor_tensor(out=st_[:, 0:n], in0=ps[:, 0:n], in1=cb[:, 0:n], op=ALU.add),
                     reads=[pb, b_cb], writes=[bs_])
                S.dma(lambda q: q.dma_start(out=K.sc_u[ti * 128:(ti + 1) * 128, c0:c0 + n], in_=st_[:, 0:n]),
                      reads=[bs_], writes=[K.b_scr])
        S.barrier()


def phase_g(K):
    nc, S, I = K.nc, K.S, K.I
    K.sc_o = K.scratch("sc_o", [L, 256])
    NCH = (L + LC) // 64
    with ExitStack() as ph:
        sb = lambda name, shape, dt=F32: K.sb("g_" + name, shape, dt, stack=ph)
        vtm = sb("vtm", [128, NT + NTC, 256], BF16)
        b_vtm = Buf()
        S.dma(lambda q: q.dma_start(out=vtm[:], in_=K.sc_v.rearrange("(t p) c -> p t c", p=128)),
              reads=[K.b_scr], writes=[b_vtm])
        wa = [sb("wa%d" % d, [16, 128]) for d in range(2)]
        b_wa = Buf()
        S.dma(lambda q: q.dma_start(out=wa[0][:], in_=I["wa_f"]), writes=[b_wa])
        S.dma(lambda q: q.dma_start(out=wa[1][:], in_=I["wa_b"]), writes=[b_wa])
        nba = sb("nba", [128, 2])
        b_nba = Buf()
        S.dma(lambda q: q.dma_start(out=nba[:], in_=I["gla_ba"]), writes=[b_nba])
        S.op("dve", lambda e: e.tensor_scalar(out=nba[:], in0=nba[:], scalar1=-1.0, scalar2=None, op0=ALU.mult),
             reads=[], writes=[b_nba])
        mk = [sb("mk%d" % d, [128, 128]) for d in range(2)]
        b_mk = Buf()
        S.dma(lambda q: q.dma_start(out=mk[0][:], in_=I["maskF"]), writes=[b_mk])
        S.dma(lambda q: q.dma_start(out=mk[1][:], in_=I["maskB"]), writes=[b_mk])
        smask = sb("smask", [128, 512])
        b_sm = Buf()
        S.op("pool", lambda e: e.memset(smask[:], 1.0), writes=[b_sm])
        S.op("pool", lambda e: e.memset(smask[:].rearrange("p (n c) -> p n c", c=64)[:, :, 0:1], 0.0), writes=[b_sm])
        QF = [sb("QF%d" % d, [128, L], BF16) for d in range(2)]
        KF = [sb("KF%d" % d, [128, L], BF16) for d in range(2)]
        QS = [sb("QS%d" % d, [128, L], BF16) for d in range(2)]
        KU = [sb("KU%d" % d, [128, NT + NTC, 128], BF16) for d in range(2)]
        dec = [sb("dec%d" % d, [128, NCH]) for d in range(2)]
        Sh = [sb("Sh%d" % d, [128, 64, 128], BF16) for d in range(2)]
        b_prep = [Buf(), Buf()]
        b_Sh = [Buf(), Buf()]
        Sst = [[sb("S%d_%d" % (d, i), [128, 128]) for i in range(2)] for d in range(2)]
        b_S = [[Buf(), Buf()], [Buf(), Buf()]]
        q32s = [sb("q32_%d" % i, [128, 512]) for i in range(2)]; k32s = [sb("k32_%d" % i, [128, 512]) for i in range(2)]
        a16s = [[sb("a16_%d_%d" % (i, d), [16, 512]) for d in range(2)] for i in range(2)]
        b_ins = [Buf(), Buf()]
        tmpd = [{n: sb("%s%d" % (n, d), [128, 512]) for n in ("tl", "tb", "tx", "td", "te", "tku")} for d in range(2)]
        b_td = [{n: Buf() for n in ("tl", "tb", "tx", "td", "te", "tku")} for d in range(2)]
        scf = [sb("scf%d" % i, [128, 128], BF16) for i in range(2)]
        scb = [sb("scb%d" % i, [128, 128], BF16) for i in range(2)]
        b_sc = [[Buf(), Buf()], [Buf(), Buf()]]
        ost = [sb("ost%d" % i, [128, 256]) for i in range(2)]
        b_ost = [Buf(), Buf()]
        tgs = [(g * 512, 512) for g in range(8)] + [(L, 256)]
        for hp in range(1):
            def g_load(gi):
                (t0, n) = tgs[gi]
                kk = gi % 2
                S.dma(lambda q: q.dma_start(out=k32s[kk][:, 0:n], in_=K.sc_qk[128:256, t0:t0 + n]),
                      reads=[K.b_scr], writes=[b_ins[kk]])
                if t0 < L:
                    S.dma(lambda q: q.dma_start(out=q32s[kk][:, 0:n], in_=K.sc_qk[0:128, t0:t0 + n]),
                          reads=[K.b_scr], writes=[b_ins[kk]])
                for d in range(2):
                    S.dma(lambda q: q.dma_start(out=a16s[kk][d][:, 0:n], in_=K.sc_a[d * 16:(d + 1) * 16, t0:t0 + n]),
                          reads=[K.b_scr], writes=[b_ins[kk]])

            g_load(0)
            for gi, (t0, n) in enumerate(tgs):
                isx = t0 < L
                nch = n // 64
                if gi + 1 < len(tgs):
                    g_load(gi + 1)
                q32, k32, a16, b_in = q32s[gi % 2], k32s[gi % 2], a16s[gi % 2], b_ins[gi % 2]
                for d in range(2):
                    tl, tb, tx, td, te, tku = (tmpd[d][n_] for n_ in ("tl", "tb", "tx", "td", "te", "tku"))
                    b_t = b_td[d]
                    ps, pb = K.next_ps()
                    S.op("pe", lambda e: e.matmul(ps[:, 0:n], lhsT=wa[d][:, hp * 128:(hp + 1) * 128], rhs=a16[d][:, 0:n],
                                                  start=True, stop=True), reads=[b_wa, b_in], writes=[pb])
                    S.op("act", lambda e: e.activation(out=te[:, 0:n], in_=ps[:, 0:n], func=AF.Exp, scale=-1.0,
                                                       bias=nba[:, d:d + 1]),
                         reads=[pb, b_nba], writes=[b_t["te"]])
                    S.op("act", lambda e: e.activation(out=tl[:, 0:n], in_=te[:, 0:n], func=AF.Ln, bias=1.0),
                         reads=[b_t["te"]], writes=[b_t["tl"]])
                    S.op("dve", lambda e: e.tensor_tensor_scan(out=tb[:, 0:n], data0=smask[:, 0:n], data1=tl[:, 0:n],
                                                               initial=0.0, op0=ALU.mult, op1=ALU.add),
                         reads=[b_t["tl"], b_sm], writes=[b_t["tb"]])
                    tbv = tb[:, 0:n].rearrange("p (n c) -> p n c", c=64)
                    if d == 0:
                        X, bX, ridx, tidx = tb, b_t["tb"], 32, 63
                    else:
                        S.op("dve", lambda e: e.tensor_tensor(out=tx[:, 0:n], in0=tl[:, 0:n], in1=tb[:, 0:n], op=ALU.subtract),
                             reads=[b_t["tl"], b_t["tb"]], writes=[b_t["tx"]])
                        S.op("dve", lambda e: e.tensor_tensor(out=tx[:, 0:n].rearrange("p (n c) -> p n c", c=64),
                                                              in0=tx[:, 0:n].rearrange("p (n c) -> p n c", c=64),
                                                              in1=tbv[:, :, 63:64].to_broadcast([128, nch, 64]), op=ALU.add),
                             reads=[b_t["tb"]], writes=[b_t["tx"]])
                        X, bX, ridx, tidx = tx, b_t["tx"], 31, 0
                    Xv = X[:, 0:n].rearrange("p (n c) -> p n c", c=64)
                    tdv = td[:, 0:n].rearrange("p (n c) -> p n c", c=64)
                    c0 = t0 // 64
                    S.op("act", lambda e: e.activation(out=dec[d][:, c0:c0 + nch].unsqueeze(2), in_=Xv[:, :, tidx:tidx + 1],
                                                       func=AF.Exp, scale=-1.0 / 16), reads=[bX], writes=[b_prep[d]])
                    if isx:
                        S.op("dve", lambda e: e.tensor_tensor(out=tdv, in0=Xv, in1=Xv[:, :, ridx:ridx + 1].to_broadcast([128, nch, 64]),
                                                              op=ALU.subtract), reads=[bX], writes=[b_t["td"]])
                        S.op("act", lambda e: e.activation(out=te[:, 0:n], in_=td[:, 0:n], func=AF.Exp, scale=-1.0 / 16),
                             reads=[b_t["td"]], writes=[b_t["te"]])
                        S.op("dve", lambda e: e.scalar_tensor_tensor(out=QF[d][:, t0:t0 + n], in0=q32[:, 0:n], scalar=0.125,
                                                                     in1=te[:, 0:n], op0=ALU.mult, op1=ALU.mult),
                             reads=[b_in, b_t["te"]], writes=[b_prep[d]])
                        S.op("act", lambda e: e.activation(out=te[:, 0:n], in_=td[:, 0:n], func=AF.Exp, scale=1.0 / 16),
                             reads=[b_t["td"]], writes=[b_t["te"]])
                        S.op("dve", lambda e: e.tensor_tensor(out=KF[d][:, t0:t0 + n], in0=k32[:, 0:n], in1=te[:, 0:n], op=ALU.mult),
                             reads=[b_in, b_t["te"]], writes=[b_prep[d]])
                        S.op("act", lambda e: e.activation(out=te[:, 0:n], in_=X[:, 0:n], func=AF.Exp, scale=-1.0 / 16),
                             reads=[bX], writes=[b_t["te"]])
                        S.op("dve", lambda e: e.scalar_tensor_tensor(out=QS[d][:, t0:t0 + n], in0=q32[:, 0:n], scalar=0.125,
                                                                     in1=te[:, 0:n], op0=ALU.mult, op1=ALU.mult),
                             reads=[b_in, b_t["te"]], writes=[b_prep[d]])
                    S.op("dve", lambda e: e.tensor_tensor(out=tdv, in0=Xv, in1=Xv[:, :, tidx:tidx + 1].to_broadcast([128, nch, 64]),
                                                          op=ALU.subtract), reads=[bX], writes=[b_t["td"]])
                    S.op("act", lambda e: e.activation(out=te[:, 0:n], in_=td[:, 0:n], func=AF.Exp, scale=1.0 / 16),
                         reads=[b_t["td"]], writes=[b_t["te"]])
                    S.op("dve", lambda e: e.tensor_tensor(out=tku[:, 0:n], in0=k32[:, 0:n], in1=te[:, 0:n], op=ALU.mult),
                         reads=[b_in, b_t["te"]], writes=[b_t["tku"]])
                    for j in range(n // 128):
                        ps, pb = K.next_ps()
                        S.op("pe", lambda e: e.transpose(out=ps[:, 0:128], in_=tku[:, j * 128:(j + 1) * 128], identity=K.identt[:]),
                             reads=[b_t["tku"]], writes=[pb])
                        S.op("act", lambda e: e.activation(out=KU[d][:, t0 // 128 + j, :], in_=ps[:, 0:128], func=AF.Copy),
                             reads=[pb], writes=[b_prep[d]])
            orders = [[64, 65, 66, 67] + list(range(64)), [67, 66, 65, 64] + list(range(63, -1, -1))]
            curs = [0, 0]
            for d in range(2):
                S.op("pool", lambda e: e.memset(Sst[d][0][:], 0.0), writes=[b_S[d][0]])
            for step in range(68):
                for d in range(2):
                    n_ = orders[d][step]
                    cur = curs[d]
                    tile, off = n_ // 2, (n_ % 2) * 64
                    ps, pb = K.next_ps()
                    for h in range(2):
                        S.op("pe", lambda e: e.matmul(ps[h * 64:(h + 1) * 64, 0:128],
                                                      lhsT=KU[d][off:off + 64, tile, h * 64:(h + 1) * 64],
                                                      rhs=vtm[off:off + 64, tile, (2 * hp + h) * 128:(2 * hp + h + 1) * 128],
                                                      start=True, stop=True),
                             reads=[b_prep[d], b_vtm], writes=[pb])
                    if n_ < 64:
                        S.op("act", lambda e: e.activation(out=Sh[d][:, n_, :], in_=Sst[d][cur][:], func=AF.Copy),
                             reads=[b_S[d][cur]], writes=[b_Sh[d]])
                    S.op("dve", lambda e: e.scalar_tensor_tensor(out=Sst[d][1 - cur][:], in0=Sst[d][cur][:],
                                                                 scalar=dec[d][:, n_:n_ + 1], in1=ps[:, 0:128],
                                                                 op0=ALU.mult, op1=ALU.add),
                         reads=[b_S[d][cur], pb, b_prep[d]], writes=[b_S[d][1 - cur]])
                    curs[d] = 1 - cur
            for ti in range(NT):
                os_, bo_ = ost[ti % 2], b_ost[ti % 2]
                for h in range(2):
                    hs = slice(h * 64, (h + 1) * 64)
                    tk = slice(ti * 128, (ti + 1) * 128)
                    scs = []
                    for d in range(2):
                        ps, pb = K.next_ps()
                        S.op("pe", lambda e: e.matmul(ps[:, 0:128], lhsT=KF[d][hs, tk], rhs=QF[d][hs, tk], start=True, stop=True),
                             reads=[b_prep[d]], writes=[pb])
                        sc_ = (scf if d == 0 else scb)[h]
                        S.op("dve", lambda e: e.tensor_tensor(out=sc_[:], in0=ps[:, 0:128], in1=mk[d][:], op=ALU.mult),
                             reads=[pb, b_mk], writes=[b_sc[d][h]])
                        scs.append(sc_)
                    ps, pb = K.next_ps()
                    vh = vtm[:, ti, (2 * hp + h) * 128:(2 * hp + h + 1) * 128]
                    S.op("pe", lambda e: e.matmul(ps[:, 0:128], lhsT=scs[0][:], rhs=vh, start=True, stop=False),
                         reads=[b_sc[0][h], b_vtm], writes=[pb])
                    S.op("pe", lambda e: e.matmul(ps[:, 0:128], lhsT=scs[1][:], rhs=vh, start=False, stop=False),
                         reads=[b_sc[1][h], b_vtm], writes=[pb])
                    for d in range(2):
                        for j in range(2):
                            n_ = 2 * ti + j
                            last = (d == 1 and j == 1)
                            S.op("pe", lambda e: e.matmul(ps[j * 64:(j + 1) * 64, 0:128],
                                                          lhsT=QS[d][hs, n_ * 64:(n_ + 1) * 64], rhs=Sh[d][hs, n_, :],
                                                          start=False, stop=last),
                                 reads=[b_prep[d], b_Sh[d]], writes=[pb])
                    S.op("act", lambda e: e.activation(out=os_[:, h * 128:(h + 1) * 128], in_=ps[:, 0:128], func=AF.Copy),
                         reads=[pb], writes=[bo_])
                S.dma(lambda q: q.dma_start(out=K.sc_o[ti * 128:(ti + 1) * 128, hp * 256:(hp + 1) * 256], in_=os_[:]),
                      reads=[bo_], writes=[K.b_scr])
            S.barrier()


def phase_gn(K):
    nc, S, I = K.nc, K.S, K.I
    K.sc_ygla = K.scratch("sc_ygla", [L, 512])
    K.sc_ygla2 = K.scratch("sc_ygla2", [L, 512])
    K.b_ygla2 = [Buf(), Buf()]
    with ExitStack() as ph:
        sb = lambda name, shape, dt=F32: K.sb("n_" + name, shape, dt, stack=ph)
        ng = sb("ng", [128, 128]); b_ng = Buf()
        S.dma(lambda q: q.dma_start(out=ng[:], in_=I["gla_norm_g"].partition_broadcast(128)), writes=[b_ng])
        rmask = sb("rmask", [128, 2]); b_rm = Buf()
        S.dma(lambda q: q.dma_start(out=rmask[:], in_=I["rmask"]), writes=[b_rm])
        ot = [sb("ot%d" % i, [128, 256]) for i in range(2)]
        gt = [sb("gt%d" % i, [128, 256]) for i in range(2)]
        yt = [sb("yt%d" % i, [128, 256]) for i in range(2)]
        ym_ = [sb("ym%d" % i, [128, 2, 256]) for i in range(2)]
        junk = sb("junk", [128, 128])
        st = sb("st", [128, 8 * NT])
        b_o, b_g, b_y, b_ym = [Buf(), Buf()], [Buf(), Buf()], [Buf(), Buf()], [Buf(), Buf()]
        b_junk, b_st = Buf(), Buf()
        def gn_load(ti):
            k = ti % 2
            S.dma(lambda q: q.dma_start(out=ot[k][:], in_=K.sc_o[ti * 128:(ti + 1) * 128, :]), reads=[K.b_scr], writes=[b_o[k]])
            S.dma(lambda q: q.dma_start(out=gt[k][:], in_=K.sc_sg[ti * 128:(ti + 1) * 128, :]), reads=[K.b_scr], writes=[b_g[k]])

        gn_load(0)
        for ti in range(NT):
            k = ti % 2
            if ti + 1 < NT:
                gn_load(ti + 1)
            ss = st[:, 8 * ti:8 * ti + 2]
            rs = st[:, 8 * ti + 4:8 * ti + 6]
            for h in range(2):
                S.op("act", lambda e: e.activation(out=junk[:], in_=ot[k][:, h * 128:(h + 1) * 128], func=AF.Square,
                                                   accum_out=st[:, 8 * ti + h:8 * ti + h + 1]),
                     reads=[b_o[k]], writes=[b_junk, b_st])
            S.op("act", lambda e: e.activation(out=rs, in_=ss, func=AF.Sqrt, scale=1.0 / 128, bias=EPS), reads=[b_st], writes=[b_st])
            S.op("dve", lambda e: e.reciprocal(out=rs, in_=rs), reads=[b_st], writes=[b_st])
            for h in range(2):
                hs = slice(h * 128, (h + 1) * 128)
                S.op("dve", lambda e: e.scalar_tensor_tensor(out=yt[k][:, hs], in0=ot[k][:, hs], scalar=st[:, 8 * ti + 4 + h:8 * ti + 5 + h],
                                                             in1=gt[k][:, hs], op0=ALU.mult, op1=ALU.mult),
                     reads=[b_o[k], b_g[k], b_st], writes=[b_y[k]])
                S.op("pool", lambda e: e.tensor_tensor(out=yt[k][:, hs], in0=yt[k][:, hs], in1=ng[:], op=ALU.mult),
                     reads=[b_ng], writes=[b_y[k]])
            for m_ in range(2):
                S.op("dve", lambda e: e.tensor_scalar(out=ym_[k][:, m_, :], in0=yt[k][:], scalar1=rmask[:, m_:m_ + 1], scalar2=None, op0=ALU.mult),
                     reads=[b_y[k], b_rm], writes=[b_ym[k]])
            S.dma(lambda q: q.dma_start(out=K.sc_ygla[ti * 128:(ti + 1) * 128, :].rearrange("p (a c) -> p a c", a=2), in_=ym_[k][:]),
                  reads=[b_ym[k]], writes=[K.b_scr])
        S.barrier()
        for ch in range(2):
            S.coll(lambda g: g.collective_compute("AllReduce", ALU.add, replica_groups=[[0, 1], [2, 3], [4, 5], [6, 7]],
                                                  ins=[K.sc_ygla[ch * 2048:(ch + 1) * 2048, :]], outs=[K.sc_ygla2[ch * 2048:(ch + 1) * 2048, :]]),
                   reads=[K.b_scr], writes=[K.b_ygla2[ch]])
        S.barrier()


def phase_h(K):
    nc, S, I = K.nc, K.S, K.I
    PI = math.pi
    NB, PT, FC = 4, 8, 9
    with ExitStack() as ph:
        sb = lambda name, shape, dt=F32: K.sb("h_" + name, shape, dt, stack=ph)
        K.sc_h = K.scratch("sc_h", [4, L, 256], BF16)
        with ExitStack() as ph2:
            sb2 = lambda name, shape, dt=F32: K.sb("h2_" + name, shape, dt, stack=ph2)
            hd2 = [sb2("hd2_%d" % d, [64, L]) for d in range(2)]; b_hd2 = [Buf(), Buf()]
            w3 = sb2("w3", [64, 1024]); b_w3 = Buf()
            S.dma(lambda q: q.dma_start(out=w3[:], in_=I["hy_w3"]), writes=[b_w3])
            fbv = sb2("fb", [64, 4]); b_fb = Buf()
            S.dma(lambda q: q.dma_start(out=fbv[:], in_=I["hy_fb"]), writes=[b_fb])
            fbb = sb2("fbb", [64, 2])
            S.op("dve", lambda e: e.tensor_tensor(out=fbb[:], in0=fbv[:, 1:3], in1=fbv[:, 0:1].to_broadcast([64, 2]), op=ALU.mult),
                 reads=[b_fb], writes=[b_fb])
            tn = sb2("tn", [128, 2, NT]); dl = sb2("dl", [128, 256]); b_c2 = Buf()
            S.dma(lambda q: q.dma_start(out=tn[:], in_=I["tnorm"]), writes=[b_c2])
            S.dma(lambda q: q.dma_start(out=dl[:], in_=I["deltas"].partition_broadcast(128)), writes=[b_c2])
            S.op("dve", lambda e: e.tensor_scalar(out=tn[:], in0=tn[:], scalar1=-1.0, scalar2=None, op0=ALU.mult), reads=[], writes=[b_c2])
            brow = sb2("brow", [1, 512]); b_brow = Buf()
            S.dma(lambda q: q.dma_start(out=brow[:], in_=I["hy_bias"].rearrange("o c -> (o c)").unsqueeze(0)), writes=[b_brow])
            zT = sb2("zT", [33, 2, L]); b_z = Buf()
            S.dma(lambda q: q.dma_start(out=zT[:], in_=I["zT"]), writes=[b_z])
            w1 = sb2("w1", [33, 64]); w2 = sb2("w2", [64, 64]); b_w = Buf()
            S.dma(lambda q: q.dma_start(out=w1[:], in_=I["hy_w1"]), writes=[b_w])
            S.dma(lambda q: q.dma_start(out=w2[:], in_=I["hy_w2"]), writes=[b_w])
            hd1 = sb2("hd1", [64, L]); b_hd1 = Buf()
            arg = [sb2("arg%d" % i, [64, 512]) for i in range(2)]; b_arg = [Buf(), Buf()]
            wr1 = sb2("wr1", [64, 512]); wr2 = sb2("wr2", [64, 512]); b_wr1, b_wr2 = Buf(), Buf()
            for dr in range(2):
                for layer in range(2):
                    for tg in range(8):
                        ts_ = slice(tg * 512, (tg + 1) * 512)
                        ps, pb = K.next_ps()
                        if layer == 0:
                            S.op("pe", lambda e: e.matmul(ps[0:64, :], lhsT=w1[:], rhs=zT[:, dr, ts_], start=True, stop=True),
                                 reads=[b_w, b_z], writes=[pb])
                        else:
                            S.op("pe", lambda e: e.matmul(ps[0:64, :], lhsT=w2[:], rhs=hd1[:, ts_], start=True, stop=True),
                                 reads=[b_w, b_hd1], writes=[pb])
                        a_, ba_ = arg[tg % 2], b_arg[tg % 2]
                        S.op("act", lambda e: e.activation(out=a_[:], in_=ps[0:64, :], func=AF.Identity, scale=fbv[:, 0:1],
                                                           bias=fbb[:, layer:layer + 1]), reads=[pb, b_fb], writes=[ba_])
                        S.op("dve", lambda e: e.tensor_scalar(out=wr1[:], in0=a_[:], scalar1=PI, scalar2=-2 * PI, op0=ALU.is_gt, op1=ALU.mult),
                             reads=[ba_], writes=[b_wr1])
                        S.op("dve", lambda e: e.tensor_scalar(out=wr2[:], in0=a_[:], scalar1=-PI, scalar2=2 * PI, op0=ALU.is_lt, op1=ALU.mult),
                             reads=[ba_], writes=[b_wr2])
                        S.op("dve", lambda e: e.tensor_tensor(out=a_[:], in0=a_[:], in1=wr1[:], op=ALU.add), reads=[b_wr1], writes=[ba_])
                        S.op("dve", lambda e: e.tensor_tensor(out=a_[:], in0=a_[:], in1=wr2[:], op=ALU.add), reads=[b_wr2], writes=[ba_])
                        dst, bd = (hd1, b_hd1) if layer == 0 else (hd2[dr], b_hd2[dr])
                        S.op("act", lambda e: e.activation(out=dst[:, ts_], in_=a_[:], func=AF.Sin), reads=[ba_], writes=[bd])
            dk = [sb2("dk%d" % d, [128, 256]) for d in range(2)]; b_dk = [Buf(), Buf()]
            ht = [sb2("ht%d" % i, [128, 256]) for i in range(2)]; b_ht = [Buf(), Buf()]
            hbf = [sb2("hbf%d" % i, [128, 256], BF16) for i in range(3)]; b_hbf = [Buf(), Buf(), Buf()]
            hn_ = 0
            for ti in range(NT):
                for dr in range(2):
                    S.op("act", lambda e: e.activation(out=dk[dr][:], in_=dl[:], func=AF.Exp, scale=tn[:, dr, ti:ti + 1]),
                         reads=[b_c2], writes=[b_dk[dr]])
                for od in range(4):
                    dr = od % 2
                    ps, pb = K.next_ps()
                    S.op("pe", lambda e: e.matmul(ps[:, 0:256], lhsT=hd2[dr][:, ti * 128:(ti + 1) * 128],
                                                  rhs=w3[:, od * 256:(od + 1) * 256], start=True, stop=True),
                         reads=[b_hd2[dr], b_w3], writes=[pb])
                    h_, bh_ = ht[hn_ % 2], b_ht[hn_ % 2]
                    hb_, bhb_ = hbf[hn_ % 3], b_hbf[hn_ % 3]
                    hn_ += 1
                    if ti == 0 and dr == 0:
                        o_ = od // 2
                        S.op("dve", lambda e: e.tensor_tensor(out=h_[:], in0=ps[:, 0:256], in1=dk[dr][:], op=ALU.mult),
                             reads=[pb, b_dk[dr]], writes=[bh_])
                        S.op("dve", lambda e: e.tensor_tensor(out=h_[0:1, :], in0=h_[0:1, :], in1=brow[:, o_ * 256:(o_ + 1) * 256], op=ALU.add),
                             reads=[b_brow], writes=[bh_])
                        S.op("act", lambda e: e.activation(out=hb_[:], in_=h_[:], func=AF.Copy), reads=[bh_], writes=[bhb_])
                    else:
                        S.op("dve", lambda e: e.tensor_tensor(out=hb_[:], in0=ps[:, 0:256], in1=dk[dr][:], op=ALU.mult),
                             reads=[pb, b_dk[dr]], writes=[bhb_])
                    S.dma(lambda q: q.dma_start(out=K.sc_h[od, ti * 128:(ti + 1) * 128, :], in_=hb_[:]), reads=[bhb_], writes=[K.b_scr])
            S.barrier()
        TAB = sb("TAB", [128, FC, 2, FC * 128], BF16); b_TAB = Buf()
        S.dma(lambda q: q.dma_start(out=TAB[:], in_=I["dft"]), writes=[b_TAB])
        cf = sb("cf", [128, 6, FC]); b_cf = Buf()
        S.dma(lambda q: q.dma_start(out=cf[:], in_=I["hcoef"]), writes=[b_cf])
        HF = sb("HF", [128, NT, 256], BF16); HB = sb("HB", [128, NT, 256], BF16); U = sb("U", [128, NT, 256], BF16)
        b_HF, b_HB, b_U = Buf(), Buf(), Buf()
        Y = sb("Y", [128, NB, FC, 2, 256], BF16); b_Y = Buf()
        RR2 = [[sb("RR%d_%d" % (bf_, tab), [128, 4, 256]) for tab in range(2)] for bf_ in range(2)]
        b_RR2 = [[Buf(), Buf()], [Buf(), Buf()]]
        PA2 = [[sb("PA%d_%d" % (bf_, tab), [128, 8, 256]) for tab in range(2)] for bf_ in range(2)]
        b_PA2 = [[Buf(), Buf()], [Buf(), Buf()]]
        SX2 = [[sb("SX%d_%d" % (bf_, tab), [128, 4, 256], BF16) for tab in range(2)] for bf_ in range(2)]
        b_SX2 = [[Buf(), Buf()], [Buf(), Buf()]]
        CK = [sb("CK%d" % tab, [128, 7, 256], BF16) for tab in range(2)]; b_CK = [Buf(), Buf()]
        XN = sb("XN", [128, 4, 256], BF16); b_XN = Buf()
        T1 = [sb("T1_%d" % i, [128, 4, 256]) for i in range(2)]; b_T1 = [Buf(), Buf()]
        TB = [sb("TB%d" % i, [128, 4, 256], BF16) for i in range(2)]; b_TB = [Buf() for _ in range(2)]
        identb = sb("identb", [128, 128], BF16); b_idb = Buf()
        S.op("act", lambda e: e.activation(out=identb[:], in_=K.identt[:], func=AF.Copy), reads=[K.b_ident], writes=[b_idb])
        xg = [sb("xg%d" % i, [128, 256]) for i in range(2)]; b_xg = [Buf(), Buf()]
        yo = [sb("yo%d" % i, [128, 256]) for i in range(2)]; b_yo = [Buf(), Buf()]
        ld, b_ld = yo, b_yo
        K.sc_yhy = K.scratch("sc_yhy", [L, 512])
        K.sc_yhy2 = K.scratch("sc_yhy2", [L, 512])
        K.b_yhy2 = [Buf(), Buf()]
        st_bufs = [[], []]
        rmask = sb("rmask", [128, 2]); b_rm = Buf()
        S.dma(lambda q: q.dma_start(out=rmask[:], in_=I["rmask"]), writes=[b_rm])
        for hh in range(1):
            cs = slice(0, 256)
            for ti in range(NT):
                k = ti % 2
                S.dma(lambda q: q.dma_start(out=ld[k][:], in_=K.sc_u[ti * 128:(ti + 1) * 128, 0:256]), reads=[K.b_scr], writes=[b_ld[k]])
                S.op("act", lambda e: e.activation(out=U[:, ti, :], in_=ld[k][:], func=AF.Copy), reads=[b_ld[k]], writes=[b_U])
            for o in range(2):
                for g4 in range(4):
                    gs = slice(g4 * 8, (g4 + 1) * 8)
                    S.dma(lambda q: q.dma_start(out=HF[:, gs, :], in_=K.sc_h[2 * o].rearrange("(t p) c -> p t c", p=128)[:, gs, cs]),
                          reads=[K.b_scr], writes=[b_HF])
                    gr = slice((3 - g4) * 8, (4 - g4) * 8)
                    S.dma(lambda q: q.dma_start(out=HB[:, gr, :], in_=K.sc_h[2 * o + 1].rearrange("(t p) c -> p t c", p=128)[:, gs, cs]),
                          reads=[K.b_scr], writes=[b_HB])
                sigs = [(HF, b_HF, n) for n in range(4)] + [(HB, b_HB, n) for n in range(4)] + [(U, b_U, n) for n in range(4)]
                def emit_tr(fc):
                    fs_ = slice(fc * 128, (fc + 1) * 128)
                    PA, RR, SX = PA2[fc % 2], RR2[fc % 2], SX2[fc % 2]
                    b_PA, b_RR, b_SX = b_PA2[fc % 2], b_RR2[fc % 2], b_SX2[fc % 2]
                    for tab in range(2):
                        banks = [K.next_ps() for _ in range(6)]
                        for a in range(PT):
                            for s2 in range(6):
                                src, bsrc, n = sigs[2 * s2]
                                ps, pb = banks[s2]
                                S.op("pe", lambda e: e.matmul(ps[:].rearrange("p (n c) -> p n c", n=2), lhsT=TAB[:, a, tab, fs_],
                                                              rhs=src[:].rearrange("p (n a) c -> p a n c", a=PT)[:, a, n:n + 2, :],
                                                              start=(a == 0), stop=(a == PT - 1)),
                                     reads=[b_TAB, bsrc], writes=[pb])
                        for s2 in range(6):
                            ps, pb = banks[s2]
                            pv = ps[:].rearrange("p (n c) -> p n c", n=2)
                            if s2 < 2:
                                S.op("act", lambda e: e.activation(out=PA[tab][:, 4 + 2 * s2:6 + 2 * s2, :], in_=pv, func=AF.Copy, scale=cf[:, 0, fc:fc + 1]),
                                     reads=[pb, b_cf], writes=[b_PA[tab]])
                            elif s2 < 4:
                                S.op("act", lambda e: e.activation(out=RR[tab][:, 2 * (s2 - 2):2 * (s2 - 2) + 2, :], in_=pv, func=AF.Copy,
                                                                   scale=cf[:, 0, fc:fc + 1]), reads=[pb, b_cf], writes=[b_RR[tab]])
                            else:
                                S.op("act", lambda e: e.activation(out=SX[tab][:, 2 * (s2 - 4):2 * (s2 - 4) + 2, :], in_=pv, func=AF.Copy),
                                     reads=[pb], writes=[b_SX[tab]])
                def emit_mt(fc):
                    PA, RR, SX = PA2[fc % 2], RR2[fc % 2], SX2[fc % 2]
                    b_PA, b_RR, b_SX = b_PA2[fc % 2], b_RR2[fc % 2], b_SX2[fc % 2]
                    for (tab, ia, ib, op_) in ((0, 3, 2, ALU.subtract), (1, 5, 4, ALU.add)):
                        S.op("pool", lambda e: e.tensor_scalar(out=T1[0][:], in0=RR[1][:], scalar1=cf[:, ia, fc:fc + 1], scalar2=0.0, op0=ALU.mult, op1=ALU.add),
                             reads=[b_RR[1], b_cf], writes=[b_T1[0]])
                        S.op("pool", lambda e: e.tensor_scalar(out=T1[1][:], in0=RR[0][:], scalar1=cf[:, ib, fc:fc + 1], scalar2=0.0, op0=ALU.mult, op1=ALU.add),
                             reads=[b_RR[0], b_cf], writes=[b_T1[1]])
                        S.op("pool", lambda e: e.tensor_tensor(out=PA[tab][:, 0:4, :], in0=T1[1][:], in1=T1[0][:], op=op_),
                             reads=[b_T1[0], b_T1[1]], writes=[b_PA[tab]])
                    for tab in range(2):
                        S.op("dve", lambda e: e.scalar_tensor_tensor(out=CK[tab][:], in0=PA[tab][:, 0:7, :], scalar=cf[:, 1, fc:fc + 1],
                                                                     in1=PA[tab][:, 1:8, :], op0=ALU.mult, op1=ALU.add),
                             reads=[b_PA[tab], b_cf], writes=[b_CK[tab]])
                    S.op("pool", lambda e: e.tensor_scalar(out=XN[:], in0=SX[1][:], scalar1=-1.0, scalar2=0.0, op0=ALU.mult, op1=ALU.add),
                         reads=[b_SX[1]], writes=[b_XN])
                    for tab in range(2):
                        pA, pbA = K.next_ps()
                        pB, pbB = K.next_ps()
                        n_ = 0
                        for j in range(4):
                            if tab == 0:
                                terms = ((CK[0], b_CK[0], SX[0], b_SX[0]), (CK[1], b_CK[1], XN, b_XN))
                            else:
                                terms = ((CK[0], b_CK[0], SX[1], b_SX[1]), (CK[1], b_CK[1], SX[0], b_SX[0]))
                            for (ca, bca, xa, bxa) in terms:
                                cav = ca[:, 3 - j:7 - j, :]
                                xav = xa[:, j:j + 1, :].to_broadcast([128, 4, 256])
                                tb_, btb_ = TB[n_ % 2], b_TB[n_ % 2]
                                e_ = "pool" if n_ == 7 else "dve"
                                S.op(e_, lambda e: e.tensor_tensor(out=tb_[:], in0=cav, in1=xav, op=ALU.mult), reads=[bca, bxa], writes=[btb_])
                                S.op("pe", lambda e: e.matmul(pA[:], lhsT=identb[:], rhs=tb_[:, 0:2, :].rearrange("p a b -> p (a b)"),
                                                              start=(n_ == 0), stop=(n_ == 7)), reads=[b_idb, btb_], writes=[pbA])
                                S.op("pe", lambda e: e.matmul(pB[:], lhsT=identb[:], rhs=tb_[:, 2:4, :].rearrange("p a b -> p (a b)"),
                                                              start=(n_ == 0), stop=(n_ == 7)), reads=[b_idb, btb_], writes=[pbB])
                                n_ += 1
                        S.op("act", lambda e: e.activation(out=Y[:, 0:2, fc, tab, :], in_=pA[:].rearrange("p (a b) -> p a b", a=2), func=AF.Copy),
                             reads=[pbA], writes=[b_Y])
                        S.op("act", lambda e: e.activation(out=Y[:, 2:4, fc, tab, :], in_=pB[:].rearrange("p (a b) -> p a b", a=2), func=AF.Copy),
                             reads=[pbB], writes=[b_Y])
                for fc in range(FC):
                    emit_tr(fc)
                    if fc > 0:
                        emit_mt(fc - 1)
                emit_mt(FC - 1)
                for i2 in range(2):
                    for a in range(PT):
                        ps, pb = K.next_ps()
                        n = 0
                        for fc in range(FC):
                            for tab in range(2):
                                S.op("pe", lambda e: e.matmul(ps[:].rearrange("p (n c) -> p n c", n=2), lhsT=TAB[:, fc, tab, a * 128:(a + 1) * 128],
                                                              rhs=Y[:, 2 * i2:2 * i2 + 2, fc, tab, :], start=(n == 0), stop=(n == 2 * FC - 1)),
                                     reads=[b_TAB, b_Y], writes=[pb])
                                n += 1
                        for i in (2 * i2, 2 * i2 + 1):
                            ti = i * PT + a
                            k = i % 2
                            S.dma(lambda q: q.dma_start(out=xg[k][:], in_=K.sc_u[ti * 128:(ti + 1) * 128, 256 * (1 + o):256 * (2 + o)]),
                                  reads=[K.b_scr], writes=[b_xg[k]])
                            if o == 0:
                                S.op("dve", lambda e: e.tensor_tensor(out=U[:, ti, :], in0=ps[:, (i % 2) * 256:(i % 2 + 1) * 256], in1=xg[k][:], op=ALU.mult),
                                     reads=[pb, b_xg[k]], writes=[b_U])
                            else:
                                for m_ in range(2):
                                    S.op("dve", lambda e: e.scalar_tensor_tensor(out=yo[m_][:], in0=ps[:, (i % 2) * 256:(i % 2 + 1) * 256],
                                                                                 scalar=rmask[:, m_:m_ + 1], in1=xg[k][:], op0=ALU.mult, op1=ALU.mult),
                                         reads=[pb, b_xg[k], b_rm], writes=[b_yo[m_]])
                                    bst = Buf()
                                    st_bufs[i2].append(bst)
                                    S.dma(lambda q: q.dma_start(out=K.sc_yhy[ti * 128:(ti + 1) * 128, m_ * 256:(m_ + 1) * 256], in_=yo[m_][:]),
                                          reads=[b_yo[m_]], writes=[K.b_scr, bst])
                    if o == 1:
                        S.coll(lambda g: g.collective_compute("AllReduce", ALU.add, replica_groups=[[0, 1], [2, 3], [4, 5], [6, 7]],
                                                              ins=[K.sc_yhy[i2 * 2048:(i2 + 1) * 2048, :]], outs=[K.sc_yhy2[i2 * 2048:(i2 + 1) * 2048, :]]),
                               reads=st_bufs[i2], writes=[K.b_yhy2[i2]])
        S.barrier()


def phase_e(K):
    nc, S, I = K.nc, K.S, K.I
    K.sc_x1 = K.scratch("sc_x1", [L, D])
    K.sc_hn2 = K.scratch("sc_hn2", [L, D], BF16)
    K.aff = K.sb("aff", [128, NT, NE]); K.b_aff = Buf()
    with ExitStack() as ph:
        sb = lambda name, shape, dt=F32: K.sb("e_" + name, shape, dt, stack=ph)
        wo = sb("wo", [128, 8, D], BF16); b_wo = Buf()
        S.dma(lambda q: q.dma_start(out=wo[:], in_=I["w_out"].rearrange("(kc p) j -> p kc j", p=128)), writes=[b_wo], q="pool")
        wr = sb("wr", [128, 8, NE]); b_wr = Buf()
        S.dma(lambda q: q.dma_start(out=wr[:], in_=I["w_router"].rearrange("(kc p) j -> p kc j", p=128)), writes=[b_wr])
        md = sb("md", [128, 3, D]); b_md = Buf()
        S.dma(lambda q: q.dma_start(out=md[:], in_=K.sc_mod[:, 2 * D:5 * D].rearrange("p (a d) -> p a d", a=3)),
              reads=[K.b_scmod], writes=[b_md])
        ym = [sb("ym%d" % i, [128, D]) for i in range(2)]; b_ym = [Buf(), Buf()]
        yT = [sb("yT%d" % i, [128, 8, 128], BF16) for i in range(2)]; b_yT = [Buf(), Buf()]
        xt = [sb("xt%d" % i, [128, D]) for i in range(2)]; b_xt = [Buf(), Buf()]
        pt = [sb("pt%d" % i, [128, D]) for i in range(2)]; b_pt = [Buf(), Buf()]
        hn = [sb("hn%d" % i, [128, D]) for i in range(2)]; b_hn = [Buf(), Buf()]
        hb = [sb("hb%d" % i, [128, D], BF16) for i in range(2)]; b_hb = [Buf(), Buf()]
        hT = [sb("hT%d" % i, [128, 8, 128]) for i in range(2)]; b_hT = [Buf(), Buf()]
        junk = sb("junk", [128, D]); b_junk = Buf()
        st = sb("st", [128, 2 * NT]); b_st = Buf()
        lg = sb("lg", [128, NT, NE]); b_lg = Buf()
        def e_load(ti):
            k = ti % 2
            rows = slice(ti * 128, (ti + 1) * 128)
            S.dma(lambda q: q.dma_start(out=ym[k][:, 0:512], in_=K.sc_ygla2[rows, :]), reads=[K.b_ygla2[ti // 16]], writes=[b_ym[k]])
            S.dma(lambda q: q.dma_start(out=ym[k][:, 512:1024], in_=K.sc_yhy2[rows, :]), reads=[K.b_yhy2[ti // 16]], writes=[b_ym[k]])
            S.dma(lambda q: q.dma_start(out=xt[k][:], in_=I["x"][rows, :]), writes=[b_xt[k]])
            S.dma(lambda q: q.dma_start(out=pt[k][:], in_=I["pos"][rows, :]), writes=[b_pt[k]])

        e_load(0)
        for ti in range(NT):
            k = ti % 2
            rows = slice(ti * 128, (ti + 1) * 128)
            if ti + 1 < NT:
                e_load(ti + 1)
            S.op("pool", lambda e: e.tensor_tensor(out=xt[k][:], in0=xt[k][:], in1=pt[k][:], op=ALU.add), reads=[b_pt[k]], writes=[b_xt[k]])
            for hh in range(2):
                ps, pb = K.next_ps()
                for j in range(4):
                    kc = hh * 4 + j
                    S.op("pe", lambda e: e.transpose(out=ps[:, j * 128:(j + 1) * 128], in_=ym[k][:, kc * 128:(kc + 1) * 128],
                                                     identity=K.identt[:]), reads=[b_ym[k], K.b_ident], writes=[pb])
                S.op("act", lambda e: e.activation(out=yT[k][:, hh * 4:hh * 4 + 4, :], in_=ps[:].rearrange("p (a b) -> p a b", a=4),
                                                   func=AF.Copy), reads=[pb], writes=[b_yT[k]])
            for half in range(2):
                hs = slice(half * 512, (half + 1) * 512)
                ps, pb = K.next_ps()
                for kc in range(8):
                    S.op("pe", lambda e: e.matmul(ps[:], lhsT=yT[k][:, kc, :], rhs=wo[:, kc, hs], start=(kc == 0), stop=(kc == 7)),
                         reads=[b_yT[k], b_wo], writes=[pb])
                S.op("dve", lambda e: e.tensor_tensor(out=hn[k][:, hs], in0=ps[:], in1=md[:, 0, hs], op=ALU.mult),
                     reads=[pb, b_md], writes=[b_hn[k]])
                S.op("pool", lambda e: e.tensor_tensor(out=xt[k][:, hs], in0=xt[k][:, hs], in1=hn[k][:, hs], op=ALU.add),
                     reads=[b_hn[k]], writes=[b_xt[k]])
            S.dma(lambda q: q.dma_start(out=K.sc_x1[rows, :], in_=xt[k][:]), reads=[b_xt[k]], writes=[K.b_scr])
            ss = st[:, 2 * ti:2 * ti + 1]
            rs = st[:, 2 * ti + 1:2 * ti + 2]
            S.op("act", lambda e: e.activation(out=junk[:], in_=xt[k][:], func=AF.Square, accum_out=ss), reads=[b_xt[k]], writes=[b_junk, b_st])
            S.op("act", lambda e: e.activation(out=rs, in_=ss, func=AF.Sqrt, scale=1.0 / D, bias=EPS), reads=[b_st], writes=[b_st])
            S.op("dve", lambda e: e.reciprocal(out=rs, in_=rs), reads=[b_st], writes=[b_st])
            S.op("dve", lambda e: e.scalar_tensor_tensor(out=hn[k][:], in0=xt[k][:], scalar=rs, in1=md[:, 2, :], op0=ALU.mult, op1=ALU.mult),
                 reads=[b_xt[k], b_st, b_md], writes=[b_hn[k]])
            S.op("pool", lambda e: e.tensor_tensor(out=hn[k][:], in0=hn[k][:], in1=md[:, 1, :], op=ALU.add), reads=[b_md], writes=[b_hn[k]])
            S.op("act", lambda e: e.activation(out=hb[k][:], in_=hn[k][:], func=AF.Copy), reads=[b_hn[k]], writes=[b_hb[k]])
            S.dma(lambda q: q.dma_start(out=K.sc_hn2[rows, :], in_=hb[k][:]), reads=[b_hb[k]], writes=[K.b_scr])
            for hh in range(2):
                ps, pb = K.next_ps()
                for j in range(4):
                    kc = hh * 4 + j
                    S.op("pe", lambda e: e.transpose(out=ps[:, j * 128:(j + 1) * 128], in_=hn[k][:, kc * 128:(kc + 1) * 128],
                                                     identity=K.identt[:]), reads=[b_hn[k], K.b_ident], writes=[pb])
                S.op("act", lambda e: e.activation(out=hT[k][:, hh * 4:hh * 4 + 4, :], in_=ps[:].rearrange("p (a b) -> p a b", a=4),
                                                   func=AF.Copy), reads=[pb], writes=[b_hT[k]])
            ps, pb = K.next_ps()
            for kc in range(8):
                S.op("pe", lambda e: e.matmul(ps[:, 0:NE], lhsT=hT[k][:, kc, :], rhs=wr[:, kc, :], start=(kc == 0), stop=(kc == 7)),
                     reads=[b_hT[k], b_wr], writes=[pb])
            S.op("act", lambda e: e.activation(out=lg[:, ti, :], in_=ps[:, 0:NE], func=AF.Copy), reads=[pb], writes=[b_lg])
        mx = sb("mx", [128, NT]); sm = sb("sm", [128, NT]); b_mx = Buf()
        S.op("dve", lambda e: e.tensor_reduce(out=mx[:], in_=lg[:], axis=AX.X, op=ALU.max), reads=[b_lg], writes=[b_mx])
        S.op("dve", lambda e: e.tensor_tensor(out=lg[:], in0=lg[:], in1=mx[:].unsqueeze(2).to_broadcast([128, NT, NE]), op=ALU.subtract),
             reads=[b_mx], writes=[b_lg])
        S.op("act", lambda e: e.activation(out=lg[:], in_=lg[:], func=AF.Exp), reads=[], writes=[b_lg])
        S.op("dve", lambda e: e.tensor_reduce(out=sm[:], in_=lg[:], axis=AX.X, op=ALU.add), reads=[b_lg], writes=[b_mx])
        S.op("dve", lambda e: e.reciprocal(out=sm[:], in_=sm[:]), reads=[], writes=[b_mx])
        S.op("dve", lambda e: e.tensor_tensor(out=K.aff[:], in0=lg[:], in1=sm[:].unsqueeze(2).to_broadcast([128, NT, NE]), op=ALU.mult),
             reads=[b_lg, b_mx], writes=[K.b_aff])
        K.tap("aff", K.aff[:], K.b_aff, [128, NT, NE])
        S.barrier()


def phase_f(K):
    nc, S, I = K.nc, K.S, K.I
    U32 = mybir.dt.uint32
    aff, b_aff = K.aff, K.b_aff
    ZR = NE * CAP
    b_R = Buf()
    K.sc_moe = K.scratch("sc_moe", [L, D], BF16)
    K.sc_moe2 = K.scratch("sc_moe2", [L, D], BF16)
    b_moe2 = [Buf(), Buf()]
    with ExitStack() as ph:
        sb = lambda name, shape, dt=F32: K.sb("f_" + name, shape, dt, stack=ph)
        ones = sb("ones", [128, 128]); onesb = sb("onesb", [128, 128], BF16); b_one = Buf()
        S.op("dve", lambda e: e.memset(ones[:], 1.0), writes=[b_one])
        S.op("dve", lambda e: e.memset(onesb[:], 1.0), writes=[b_one])
        identb = sb("identb", [128, 128], BF16); b_idb = Buf()
        S.op("act", lambda e: e.activation(out=identb[:], in_=K.identt[:], func=AF.Copy), reads=[K.b_ident], writes=[b_idb])
        zt = sb("zt", [128, 8, D], BF16); b_zt = Buf()
        S.op("pool", lambda e: e.memset(zt[:], 0.0), writes=[b_zt])
        for zi in range(4):
            S.dma(lambda q: q.dma_start(out=K.sc_moe[zi * 1024:(zi + 1) * 1024, :].rearrange("(a p) d -> p a d", p=128), in_=zt[:]),
                  reads=[b_zt], writes=[b_R])
        ltri = sb("ltri", [128, 128], BF16); iotaJ = sb("iotaJ", [128, 512]); eoff = sb("eoff", [128, NE]); b_cst = Buf()
        S.dma(lambda q: q.dma_start(out=ltri[:], in_=I["ltri"]), writes=[b_cst])
        S.dma(lambda q: q.dma_start(out=iotaJ[:], in_=I["iotaJ"]), writes=[b_cst])
        S.dma(lambda q: q.dma_start(out=eoff[:], in_=I["eoff"]), writes=[b_cst])
        rhs5 = sb("rhs5", [128, NT, NE, 5], BF16); b_r5 = Buf()
        tidxt = sb("tidxt", [128, NT, 2], BF16); b_tidx = Buf()
        S.dma(lambda q: q.dma_start(out=tidxt[:], in_=I["tidx"]), writes=[b_tidx])
        S.op("dve", lambda e: e.tensor_copy(out=rhs5[:, :, :, 0:2], in_=tidxt[:].unsqueeze(2).to_broadcast([128, NT, NE, 2])),
             reads=[b_tidx], writes=[b_r5])
        lo = sb("lo", [128, NE]); posm = sb("posm", [128, NT, NE])
        g5 = sb("g5", [128, D]); nfg = sb("nfg", [128, D]); b_g5 = Buf()
        S.dma(lambda q: q.dma_start(out=g5[:], in_=K.sc_mod[:, 5 * D:6 * D]), reads=[K.b_scmod], writes=[b_g5])
        S.dma(lambda q: q.dma_start(out=nfg[:], in_=I["norm_final_g"].partition_broadcast(128)), writes=[b_g5])
        pht = ExitStack()
        sb_main = sb
        sb = lambda name, shape, dt=F32: K.sb("ft_" + name, shape, dt, stack=pht)
        r1 = sb("r1", [128, NT, NE]); r2 = sb("r2", [128, NT, NE]); b_r = Buf()
        S.op("act", lambda e: e.activation(out=rhs5[:, :, :, 2], in_=aff[:], func=AF.Copy), reads=[b_aff], writes=[b_r5])
        S.op("dve", lambda e: e.tensor_tensor(out=r1[:], in0=aff[:], in1=rhs5[:, :, :, 2], op=ALU.subtract), reads=[b_aff, b_r5], writes=[b_r])
        S.op("act", lambda e: e.activation(out=rhs5[:, :, :, 3], in_=r1[:], func=AF.Copy), reads=[b_r], writes=[b_r5])
        S.op("dve", lambda e: e.tensor_tensor(out=r2[:], in0=r1[:], in1=rhs5[:, :, :, 3], op=ALU.subtract), reads=[b_r, b_r5], writes=[b_r])
        S.op("act", lambda e: e.activation(out=rhs5[:, :, :, 4], in_=r2[:], func=AF.Copy), reads=[b_r], writes=[b_r5])
        hi = sb("hi", [128, NE]); mid = sb("mid", [128, NE]); cntp = sb("cntp", [128, NE])
        cond = sb("cond", [128, NE], U32); ncond = sb("ncond", [128, NE], U32)
        cmp_ = sb("cmp", [128, NT, NE])
        b_lo, b_hi, b_mid, b_cnt, b_cond, b_cmp = Buf(), Buf(), Buf(), Buf(), Buf(), Buf()
        S.op("dve", lambda e: e.memset(lo[:], 0.0), writes=[b_lo])
        S.op("dve", lambda e: e.memset(hi[:], 1.0), writes=[b_hi])
        for it in range(36):
            S.op("dve", lambda e: e.tensor_tensor(out=mid[:], in0=lo[:], in1=hi[:], op=ALU.add), reads=[b_lo, b_hi], writes=[b_mid])
            S.op("dve", lambda e: e.tensor_scalar(out=mid[:], in0=mid[:], scalar1=0.5, scalar2=None, op0=ALU.mult), reads=[], writes=[b_mid])
            S.op("dve", lambda e: e.tensor_tensor(out=cmp_[:], in0=aff[:], in1=mid[:].unsqueeze(1).to_broadcast([128, NT, NE]), op=ALU.is_ge),
                 reads=[b_aff, b_mid], writes=[b_cmp])
            S.op("dve", lambda e: e.tensor_reduce(out=cntp[:], in_=cmp_[:].rearrange("p t e -> p e t"), axis=AX.X, op=ALU.add),
                 reads=[b_cmp], writes=[b_cnt])
            ps, pb = K.next_ps()
            S.op("pe", lambda e: e.matmul(ps[:, 0:NE], lhsT=ones[:], rhs=cntp[:], start=True, stop=True), reads=[b_one, b_cnt], writes=[pb])
            S.op("dve", lambda e: e.tensor_scalar(out=cond[:], in0=ps[:, 0:NE], scalar1=float(CAP), scalar2=None, op0=ALU.is_ge),
                 reads=[pb], writes=[b_cond])
            S.op("dve", lambda e: e.tensor_scalar(out=ncond[:], in0=ps[:, 0:NE], scalar1=float(CAP), scalar2=None, op0=ALU.is_lt),
                 reads=[pb], writes=[b_cond])
            S.op("dve", lambda e: e.copy_predicated(out=lo[:], mask=cond[:], data=mid[:]), reads=[b_cond, b_mid], writes=[b_lo])
            S.op("dve", lambda e: e.copy_predicated(out=hi[:], mask=ncond[:], data=mid[:]), reads=[b_cond, b_mid], writes=[b_hi])
        msk = sb("msk", [128, NT, NE]); mskb = sb("mskb", [128, NT, NE], BF16)
        off = sb("off", [128, NT, NE]); b_m, b_pos, b_off = Buf(), Buf(), Buf()
        S.op("dve", lambda e: e.tensor_tensor(out=msk[:], in0=aff[:], in1=lo[:].unsqueeze(1).to_broadcast([128, NT, NE]), op=ALU.is_ge),
             reads=[b_aff, b_lo], writes=[b_m])
        S.op("act", lambda e: e.activation(out=mskb[:], in_=msk[:], func=AF.Copy), reads=[b_m], writes=[b_m])
        psw, pbw = K.next_ps()
        S.op("pe", lambda e: e.matmul(psw[:], lhsT=ltri[:], rhs=mskb[:].rearrange("p t e -> p (t e)"), start=True, stop=True),
             reads=[b_cst, b_m], writes=[pbw])
        pst, pbt = K.next_ps()
        S.op("pe", lambda e: e.matmul(pst[:], lhsT=onesb[:], rhs=mskb[:].rearrange("p t e -> p (t e)"), start=True, stop=True),
             reads=[b_one, b_m], writes=[pbt])
        tot = sb("tot", [128, NT, NE])
        S.op("act", lambda e: e.activation(out=tot[:].rearrange("p t e -> p (t e)"), in_=pst[:], func=AF.Copy), reads=[pbt], writes=[b_off])
        S.op("dve", lambda e: e.memset(off[:, 0, :], 0.0), writes=[b_off])
        for ti in range(NT - 1):
            S.op("dve", lambda e: e.tensor_tensor(out=off[:, ti + 1, :], in0=off[:, ti, :], in1=tot[:, ti, :], op=ALU.add),
                 reads=[], writes=[b_off])
        S.op("dve", lambda e: e.tensor_tensor(out=posm[:].rearrange("p t e -> p (t e)"), in0=psw[:], in1=off[:].rearrange("p t e -> p (t e)"),
                                              op=ALU.add), reads=[pbw, b_off], writes=[b_pos])
        S.op("dve", lambda e: e.scalar_tensor_tensor(out=posm[:], in0=posm[:], scalar=1.0, in1=msk[:], op0=ALU.add, op1=ALU.mult),
             reads=[b_m], writes=[b_pos])
        S.op("dve", lambda e: e.tensor_scalar(out=posm[:], in0=posm[:], scalar1=-1.0, scalar2=None, op0=ALU.add), reads=[], writes=[b_pos])
        K.tap("posm", posm[:], b_pos, [128, NT, NE])
        S.barrier()
        pht.close()
        phx = ExitStack()
        sb = lambda name, shape, dt=F32: K.sb("fx_" + name, shape, dt, stack=phx)
        xs = sb("xs", [128, 4, D], BF16); b_xsr = [Buf() for _ in range(4)]
        idx8 = sb("idx8", [128, 4, 5]); idxf = sb("idxf", [128, 4]); idxu = sb("idxu", [128, 4], U32); valj = sb("valj", [128, 4]); b_idx = Buf()
        idxu2 = [sb("idxu2_%d" % i, [128, 4], U32) for i in range(2)]; b_idx2 = [Buf(), Buf()]
        Sg = [sb("Sg%d" % i, [128, 512], BF16) for i in range(3)]; b_Sg = [Buf() for _ in range(3)]
        xsT = sb("xsT", [128, 8, 512], BF16); b_xs = Buf()
        hidT = sb("hidT", [128, 8, 512], BF16); b_hid = Buf()
        Yw = [sb("Yw%d" % i, [128, 4, D], BF16) for i in range(2)]; b_Yw = [Buf(), Buf()]
        wg = [sb("wg%d" % i, [128, 8, 128], BF16) for i in range(2)]; b_wg = [Buf(), Buf()]
        wu = [sb("wu%d" % i, [128, 8, 128], BF16) for i in range(2)]; b_wu = [Buf(), Buf()]
        wd = [sb("wd%d" % i, [128, D], BF16) for i in range(2)]; b_wd = [Buf(), Buf()]
        sgt = [sb("sgt%d" % i, [128, 512]) for i in range(2)]; b_sgt = [Buf(), Buf()]
        for ex in range(NE // 2):
            psi, pbi = K.next_ps()
            for ti in range(NT):
                k = ti % 3
                S.op("dve", lambda e: e.tensor_scalar(out=Sg[k][:], in0=iotaJ[:], scalar1=posm[:, ti, ex:ex + 1], scalar2=None, op0=ALU.is_equal),
                     reads=[b_cst, b_pos], writes=[b_Sg[k]])
                for jc in range(4):
                    S.op("pe", lambda e: e.matmul(psi[:, 5 * jc:5 * jc + 5], lhsT=Sg[k][:, jc * 128:(jc + 1) * 128], rhs=rhs5[:, ti, ex, :],
                                                  start=(ti == 0 and jc == 0), stop=(ti == NT - 1 and jc == 3)),
                         reads=[b_Sg[k], b_r5], writes=[pbi])
            S.op("act", lambda e: e.activation(out=idx8[:].rearrange("p a b -> p (a b)"), in_=psi[:, 0:20], func=AF.Copy), reads=[pbi], writes=[b_idx])
            S.op("dve", lambda e: e.scalar_tensor_tensor(out=idxf[:].unsqueeze(2), in0=idx8[:, :, 0:1], scalar=64.0, in1=idx8[:, :, 1:2],
                                                         op0=ALU.mult, op1=ALU.add), reads=[], writes=[b_idx])
            S.op("dve", lambda e: e.tensor_copy(out=idxu[:], in_=idxf[:]), reads=[], writes=[b_idx])
            S.op("dve", lambda e: e.tensor_copy(out=idxu2[ex % 2][:], in_=idxf[:]), reads=[b_idx], writes=[b_idx2[ex % 2]])
            S.op("dve", lambda e: e.tensor_tensor(out=valj[:].unsqueeze(2), in0=idx8[:, :, 3:4], in1=idx8[:, :, 4:5], op=ALU.add), reads=[], writes=[b_idx])
            S.op("dve", lambda e: e.tensor_tensor(out=valj[:].unsqueeze(2), in0=valj[:].unsqueeze(2), in1=idx8[:, :, 2:3], op=ALU.add), reads=[], writes=[b_idx])
            for jc in range(4):
                S.dma(lambda q: q.indirect_dma_start(out=xs[:, jc, :], out_offset=None, in_=K.sc_hn2,
                                                     in_offset=bass.IndirectOffsetOnAxis(ap=idxu[:, jc:jc + 1], axis=0)),
                      reads=[b_idx, K.b_scr], writes=[b_xsr[jc]], q="pool")
            for jc in range(4):
                ps, pb = K.next_ps()
                pv = ps[:].bitcast(BF16)
                for dc in range(8):
                    S.op("pe", lambda e: e.transpose(out=pv[:, dc * 128:(dc + 1) * 128], in_=xs[:, jc, dc * 128:(dc + 1) * 128], identity=identb[:]),
                         reads=[b_xsr[jc], b_idb], writes=[pb])
                S.op("act" if jc % 2 else "dve",
                     (lambda e: e.activation(out=xsT[:, :, jc * 128:(jc + 1) * 128], in_=pv.rearrange("p (a b) -> p a b", a=8), func=AF.Copy)) if jc % 2 else
                     (lambda e: e.tensor_copy(out=xsT[:, :, jc * 128:(jc + 1) * 128], in_=pv.rearrange("p (a b) -> p a b", a=8))),
                     reads=[pb], writes=[b_xs])
            for fc in range(8):
                k = fc % 2
                S.dma(lambda q: q.dma_start(out=wg[k][:], in_=I["w_gate"][ex].rearrange("(kc p) f -> p kc f", p=128)[:, :, fc * 128:(fc + 1) * 128]),
                      writes=[b_wg[k]], q="pool")
                S.dma(lambda q: q.dma_start(out=wu[k][:], in_=I["w_up"][ex].rearrange("(kc p) f -> p kc f", p=128)[:, :, fc * 128:(fc + 1) * 128]),
                      writes=[b_wu[k]], q="pool")
                psg, pbg = K.next_ps()
                for dc in range(8):
                    S.op("pe", lambda e: e.matmul(psg[:], lhsT=wg[k][:, dc, :], rhs=xsT[:, dc, :], start=(dc == 0), stop=(dc == 7)),
                         reads=[b_wg[k], b_xs], writes=[pbg])
                psu, pbu = K.next_ps()
                for dc in range(8):
                    S.op("pe", lambda e: e.matmul(psu[:], lhsT=wu[k][:, dc, :], rhs=xsT[:, dc, :], start=(dc == 0), stop=(dc == 7)),
                         reads=[b_wu[k], b_xs], writes=[pbu])
                S.op("act", lambda e: e.activation(out=sgt[k][:], in_=psg[:], func=AF.Silu), reads=[pbg], writes=[b_sgt[k]])
                S.op("dve", lambda e: e.tensor_tensor(out=hidT[:, fc, :], in0=psu[:], in1=sgt[k][:], op=ALU.mult), reads=[pbu, b_sgt[k]], writes=[b_hid])
            banks = [K.next_ps() for _ in range(8)]
            for fc in range(8):
                k = fc % 2
                S.dma(lambda q: q.dma_start(out=wd[k][:], in_=I["w_down"][ex, fc * 128:(fc + 1) * 128, :]), writes=[b_wd[k]], q="pool")
                for jc in range(4):
                    for half in range(2):
                        ps, pb = banks[jc * 2 + half]
                        S.op("pe", lambda e: e.matmul(ps[:], lhsT=hidT[:, fc, jc * 128:(jc + 1) * 128], rhs=wd[k][:, half * 512:(half + 1) * 512],
                                                      start=(fc == 0), stop=(fc == 7)), reads=[b_hid, b_wd[k]], writes=[pb])
            yw, byw = Yw[ex % 2], b_Yw[ex % 2]
            for jc in range(4):
                for half in range(2):
                    ps, pb = banks[jc * 2 + half]
                    S.op("dve", lambda e: e.scalar_tensor_tensor(out=yw[:, jc, half * 512:(half + 1) * 512], in0=ps[:], scalar=valj[:, jc:jc + 1],
                                                                 in1=g5[:, half * 512:(half + 1) * 512], op0=ALU.mult, op1=ALU.mult),
                         reads=[pb, b_idx, b_g5], writes=[byw])
            for jc in range(4):
                S.dma(lambda q: q.indirect_dma_start(out=K.sc_moe, out_offset=bass.IndirectOffsetOnAxis(ap=idxu2[ex % 2][:, jc:jc + 1], axis=0),
                                                     in_=yw[:, jc, :], in_offset=None, compute_op=ALU.add),
                      reads=[byw, b_idx2[ex % 2]], writes=[b_R], q="pool")
        S.barrier()
        phx.close()
        sb = sb_main
        for ch in range(2):
            S.coll(lambda g: g.collective_compute("AllReduce", ALU.add, replica_groups=[[0, 1], [2, 3], [4, 5], [6, 7]],
                                                  ins=[K.sc_moe[ch * 2048:(ch + 1) * 2048, :]], outs=[K.sc_moe2[ch * 2048:(ch + 1) * 2048, :]]),
                   reads=[b_R], writes=[b_moe2[ch]])
        x1t = [sb("x1t%d" % i, [128, D]) for i in range(3)]; b_x1t = [Buf(), Buf(), Buf()]
        mt = [sb("mt%d" % i, [128, D], BF16) for i in range(3)]; b_mt = [Buf(), Buf(), Buf()]
        junk = sb("junk", [128, D]); b_junk = Buf()
        st = sb("st", [128, 2 * NT]); b_st = Buf()
        def f_load(ti):
            k = ti % 3
            rows = slice(ti * 128, (ti + 1) * 128)
            S.dma(lambda q: q.dma_start(out=x1t[k][:], in_=K.sc_x1[rows, :]), reads=[K.b_scr], writes=[b_x1t[k]])
            S.dma(lambda q: q.dma_start(out=mt[k][:], in_=K.sc_moe2[rows, :]), reads=[b_moe2[ti // 16]], writes=[b_mt[k]])

        f_load(0)
        f_load(1)
        for ti in range(NT):
            k = ti % 3
            rows = slice(ti * 128, (ti + 1) * 128)
            if ti + 2 < NT:
                f_load(ti + 2)
            S.op("pool", lambda e: e.tensor_tensor(out=x1t[k][:], in0=x1t[k][:], in1=mt[k][:], op=ALU.add), reads=[b_mt[k]], writes=[b_x1t[k]])
            ss = st[:, 2 * ti:2 * ti + 1]
            rs = st[:, 2 * ti + 1:2 * ti + 2]
            S.op("act", lambda e: e.activation(out=junk[:], in_=x1t[k][:], func=AF.Square, accum_out=ss), reads=[b_x1t[k]], writes=[b_junk, b_st])
            S.op("act", lambda e: e.activation(out=rs, in_=ss, func=AF.Sqrt, scale=1.0 / D, bias=EPS), reads=[b_st], writes=[b_st])
            S.op("dve", lambda e: e.reciprocal(out=rs, in_=rs), reads=[b_st], writes=[b_st])
            S.op("dve", lambda e: e.scalar_tensor_tensor(out=x1t[k][:], in0=x1t[k][:], scalar=rs, in1=nfg[:], op0=ALU.mult, op1=ALU.mult),
                 reads=[b_st, b_g5], writes=[b_x1t[k]])
            S.dma(lambda q: q.dma_start(out=K.out[rows, :], in_=x1t[k][:]), reads=[b_x1t[k]])
        S.barrier()
```

```python
import math
from contextlib import ExitStack
import numpy as np
import concourse.bass as bass
import concourse.mybir as mybir
from concourse.bass_utils import run_bass_kernel_spmd

F32 = mybir.dt.float32
BF16 = mybir.dt.bfloat16
AF = mybir.ActivationFunctionType
ALU = mybir.AluOpType
AX = mybir.AxisListType

D = 1024
L = 4096
LC = 256
NT = L // 128
NTC = LC // 128
DIN = 3104
EPS = 1e-6
NF = 33
NE = 16
CAP = 512


class Buf:
    __slots__ = ("w", "r")

    def __init__(self):
        self.w = None
        self.r = {}


class Sch:
    def __init__(self, nc, es):
        self.nc = nc
        self.eng = {"pe": nc.tensor, "act": nc.scalar, "dve": nc.vector, "pool": nc.gpsimd, "sp": nc.sync}
        self.sem = {k: es.enter_context(nc.semaphore("s_" + k)) for k in self.eng}
        self.cnt = {k: 0 for k in self.eng}
        self.seen = {k: {} for k in self.eng}
        self.NDS = 24
        self.dsem = [es.enter_context(nc.semaphore("d%d" % i)) for i in range(self.NDS)]
        self.dcnt = [0] * self.NDS
        self.dnext = 0
        self.csem = es.enter_context(nc.semaphore("s_cc"))
        self.ccnt = 0

    def _wait(self, e, tok):
        if tok is None:
            return
        kind, key, val = tok
        if kind == "e" and key == e and e == "pe":
            return
        sk = (kind, key)
        if self.seen[e].get(sk, 0) >= val:
            return
        sem = self.sem[key] if kind == "e" else (self.dsem[key] if kind == "d" else self.csem)
        self.eng[e].wait_ge(sem, val)
        self.seen[e][sk] = val

    def _deps(self, e, reads, writes):
        for b in reads:
            self._wait(e, b.w)
        for b in writes:
            self._wait(e, b.w)
            for t in list(b.r.values()):
                self._wait(e, t)

    def op(self, e, fn, reads=(), writes=()):
        self._deps(e, reads, writes)
        inst = fn(self.eng[e])
        self.cnt[e] += 1
        inst.then_inc(self.sem[e], 1)
        tok = ("e", e, self.cnt[e])
        for b in reads:
            b.r[e] = tok
        for b in writes:
            b.w = tok
            b.r = {}
        return tok

    def dma(self, fn, reads=(), writes=(), q="sp"):
        i = self.dnext
        self.dnext = (i + 1) % self.NDS
        if self.dcnt[i] > 0:
            self._wait(q, ("d", i, self.dcnt[i]))
        self._deps(q, reads, writes)
        inst = fn(self.eng[q])
        self.dcnt[i] += 16
        inst.then_inc(self.dsem[i], 16)
        tok = ("d", i, self.dcnt[i])
        for b in reads:
            b.r[("d", i)] = tok
        for b in writes:
            b.w = tok
            b.r = {}
        return tok

    def coll(self, fn, reads=(), writes=()):
        self._deps("pool", reads, writes)
        inst = fn(self.eng["pool"])
        self.ccnt += 1
        inst.then_inc(self.csem)
        tok = ("c", 0, self.ccnt)
        for b in reads:
            b.r[("c", 0)] = tok
        for b in writes:
            b.w = tok
            b.r = {}
        return tok

    def barrier(self, wait_coll=False):
        for e in ("pe", "act", "dve", "pool", "sp"):
            for f in ("pe", "act", "dve", "pool"):
                if self.cnt[f]:
                    self._wait(e, ("e", f, self.cnt[f]))
            for i in range(self.NDS):
                if self.dcnt[i]:
                    self._wait(e, ("d", i, self.dcnt[i]))
            if self.ccnt and wait_coll:
                self._wait(e, ("c", 0, self.ccnt))


def host_consts():
    f32 = np.float32
    rows = L // 64
    r = np.repeat(np.arange(rows, dtype=f32), 64)
    col = np.tile(np.arange(64, dtype=f32), rows)
    quarter = D // 4
    omega = (1.0 / (np.float32(10000.0) ** (np.arange(quarter, dtype=f32) / np.float32(quarter)))).astype(f32)
    er = r[:, None] * omega
    ec = col[:, None] * omega
    pos = np.concatenate([np.sin(er), np.cos(er), np.sin(ec), np.cos(ec)], axis=-1).astype(f32)
    ident = np.eye(128, dtype=f32)
    si = np.arange(128)[:, None]
    ci = np.arange(128)[None, :]
    same = (si // 64) == (ci // 64)
    maskF = (same & (si <= ci)).astype(f32)
    maskB = (same & (si >= ci)).astype(f32)
    import ml_dtypes
    bf = ml_dtypes.bfloat16
    def zfeat(idx):
        t = (idx / np.float32(L - 1)).astype(f32)[:, None]
        w = (2.0 * math.pi * idx[:, None] / L).astype(f32)
        fb = np.linspace(1e-4, 15, 16, dtype=f32)[None, :]
        return np.concatenate([t, np.cos(fb * w), -np.sin(fb * w)], axis=-1).astype(f32), t[:, 0]
    idx = np.arange(L, dtype=f32)
    z0, t0_ = zfeat(idx)
    z1, t1_ = zfeat(idx + 1.0)
    z0[:, 0] = np.linspace(0.0, 1.0, L, dtype=f32)
    t0_ = np.linspace(0.0, 1.0, L, dtype=f32)
    zT = np.ascontiguousarray(np.stack([z0.T, z1.T], axis=1))
    tnorm = np.ascontiguousarray(np.stack([t0_.reshape(NT, 128).T, t1_.reshape(NT, 128).T], axis=1))
    deltas = np.abs(np.linspace(math.log(1e-2) / 1.5, math.log(1e-2) / 0.3, 512, dtype=f32)).astype(f32)
    Nb, Lb, FCn = 2048, 1024, 9
    fidx = np.arange(FCn * 128)
    fs = np.where(fidx > Lb, 0.0, np.where((fidx == 0) | (fidx == Lb), 1.0 / Nb, 2.0 / Nb))
    sg = np.where(fidx % 2 == 0, 1.0, -1.0)
    th = 2.0 * np.pi / Nb
    cfv, sfv = np.cos(th * fidx), np.sin(th * fidx)
    hcoef = np.stack([fs, sg, sg * cfv, sg * sfv, -sg * sfv, -sg * cfv], axis=0)
    hcoef = np.ascontiguousarray(hcoef.reshape(6, FCn, 128).transpose(2, 0, 1)).astype(f32)
    fscale = np.zeros((128, NF), dtype=f32)
    ang = th * np.arange(Nb, dtype=np.float64)
    ctab, stab = np.cos(ang), np.sin(ang)
    a_ = np.arange(FCn * 128, dtype=np.int64)
    prod = (a_[:, None] * a_[None, :]) % Nb
    dft = np.empty((128, FCn, 2, FCn * 128), dtype=bf)
    dft[:, :, 0, :] = ctab[prod].astype(f32).reshape(FCn, 128, FCn * 128).transpose(1, 0, 2).astype(bf)
    dft[:, :, 1, :] = stab[prod].astype(f32).reshape(FCn, 128, FCn * 128).transpose(1, 0, 2).astype(bf)
    ltri = (np.arange(128)[:, None] < np.arange(128)[None, :]).astype(bf)
    iotaJ = np.tile(np.arange(512, dtype=f32)[None, :], (128, 1))
    jvec = (np.arange(128, dtype=f32)[:, None] + 128.0 * np.arange(4, dtype=f32)[None, :]).astype(f32)
    selE = np.tile(np.arange(16, dtype=f32)[:, None], (1, 128))
    eoff = np.tile((np.arange(NE, dtype=f32) * CAP - NE * CAP)[None, :], (128, 1)).astype(f32)
    tt_ = (np.arange(NT)[None, :] * 128 + np.arange(128)[:, None])
    tidx = np.stack([tt_ // 64, tt_ % 64], axis=-1).astype(bf)
    return {"pos": pos, "ident": ident, "maskF": maskF, "maskB": maskB, "zT": zT, "tnorm": tnorm, "deltas": deltas,
            "fscale": fscale, "dft": dft, "hcoef": hcoef, "ltri": ltri, "iotaJ": iotaJ, "jvec": jvec, "selE": selE, "tidx": tidx, "eoff": eoff}


class Ctx:
    pass


def build(upto=99, taps=()):
    nc = bass.Bass("TRN2", target_bir_lowering=False)
    es = ExitStack()
    S = Sch(nc, es)
    K = Ctx()
    K.nc, K.S, K.es = nc, S, es
    K.taps = {}
    K.tap_names = taps

    def din(name, shape, dt=F32):
        return nc.dram_tensor(name, list(shape), dt, kind="ExternalInput").ap()

    I = {}
    I["x"] = din("x", [L, D])
    I["ctx"] = din("ctx", [LC, D])
    I["pos"] = din("pos", [L, D])
    I["ident"] = din("ident", [128, 128])
    I["cc"] = din("cc", [128, 16])
    I["w_ada"] = din("w_ada", [D, 6 * D])
    I["b_ada"] = din("b_ada", [6 * D])
    I["norm_mix_g"] = din("norm_mix_g", [D])
    I["norm_ffn_g"] = din("norm_ffn_g", [D])
    I["norm_final_g"] = din("norm_final_g", [D])
    I["w_gl"] = din("w_gl", [D, 800])
    I["wa_f"] = din("wa_f", [16, 128])
    I["wa_b"] = din("wa_b", [16, 128])
    I["gla_ba"] = din("gla_ba", [128, 2])
    I["gla_norm_g"] = din("gla_norm_g", [128])
    I["hy_conv_w"] = din("hy_conv_w", [3, 768])
    I["hy_conv_b"] = din("hy_conv_b", [768])
    I["w_hy"] = din("w_hy", [D, 768])
    I["rmask"] = din("rmask", [128, 2])
    I["maskF"] = din("maskF", [128, 128])
    I["maskB"] = din("maskB", [128, 128])
    I["w_out"] = din("w_out", [D, D])
    I["zT"] = din("zT", [33, 2, L])
    I["hy_w1"] = din("hy_w1", [33, 64])
    I["hy_w2"] = din("hy_w2", [64, 64])
    I["hy_w3"] = din("hy_w3", [64, 1024])
    I["hy_fb"] = din("hy_fb", [64, 4])
    I["hy_bias"] = din("hy_bias", [2, 256])
    I["tnorm"] = din("tnorm", [128, 2, NT])
    I["deltas"] = din("deltas", [256])
    I["fscale"] = din("fscale", [128, NF])
    I["dft"] = din("dft", [128, 9, 2, 1152], BF16)
    I["hcoef"] = din("hcoef", [128, 6, 9])
    I["w_router"] = din("w_router", [D, NE])
    I["w_gate"] = din("w_gate", [NE // 2, D, D])
    I["w_up"] = din("w_up", [NE // 2, D, D])
    I["w_down"] = din("w_down", [NE // 2, D, D])
    I["ltri"] = din("ltri", [128, 128], BF16)
    I["iotaJ"] = din("iotaJ", [128, 512])
    I["jvec"] = din("jvec", [128, 4])
    I["eoff"] = din("eoff", [128, NE])
    I["tidx"] = din("tidx", [128, NT, 2], BF16)
    I["selE"] = din("selE", [16, 128])

    def scratch(name, shape, dt=F32):
        kind = "ExternalOutput" if name in taps else "Internal"
        return nc.dram_tensor(("tap_" if name in taps else "") + name, list(shape), dt, kind=kind).ap()

    K.scratch = scratch
    out = nc.dram_tensor("out", [L, D], F32, kind="ExternalOutput").ap()
    K.I, K.out = I, out

    def sb(name, shape, dt=F32, stack=None):
        return (stack or es).enter_context(nc.sbuf_tensor("sb_" + name, list(shape), dt))

    K.sb = sb
    K.ps = [es.enter_context(nc.psum_tensor("ps%d" % i, [128, 512], F32)) for i in range(8)]
    K.psb = [Buf() for _ in range(8)]
    K.psi = 0

    def next_ps():
        i = K.psi
        K.psi = (i + 1) % 8
        return K.ps[i], K.psb[i]

    K.next_ps = next_ps

    def tap(name, ap_sb, buf, shape, dt=F32):
        if name not in K.tap_names:
            return
        t = nc.dram_tensor("tap_" + name, list(shape), dt, kind="ExternalOutput").ap()
        S.dma(lambda q: q.dma_start(out=t, in_=ap_sb), reads=[buf])
        K.taps[name] = t

    K.tap = tap

    phase_a(K)
    if upto >= 2:
        phase_b(K)
    if upto >= 3:
        phase_c(K)
        K.es_bc.close()
    if upto >= 4:
        phase_g(K)
    if upto >= 5:
        phase_gn(K)
    if upto >= 6:
        phase_h(K)
    if upto >= 7:
        phase_e(K)
    if upto >= 8:
        phase_f(K)
    S.barrier(wait_coll=True)
    return nc, K


def phase_a(K):
    nc, S, I = K.nc, K.S, K.I
    K.identt = K.sb("identt", [128, 128])
    K.es_bc = ExitStack()
    K.modx = K.sb("modx", [128, 6 * D], stack=K.es_bc)
    K.modc = K.sb("modc", [128, 2 * D], stack=K.es_bc)
    K.b_modx = [Buf() for _ in range(12)]
    K.b_modc = [Buf() for _ in range(4)]
    K.b_ident = Buf()
    S.dma(lambda q: q.dma_start(out=K.identt[:], in_=I["ident"]), writes=[K.b_ident])
    with ExitStack() as ph:
        cc = K.sb("cc", [128, 16], stack=ph)
        sc = K.sb("sc", [128, 16], stack=ph)
        rep = K.sb("rep", [128, 16, 128], stack=ph)
        bada = K.sb("bada", [128, 6 * D], stack=ph)
        wst = [K.sb("wst%d" % i, [128, 8, 512], stack=ph) for i in range(2)]
        b_cc, b_sc, b_rep, b_bada = Buf(), Buf(), Buf(), Buf()
        b_wst = [Buf(), Buf()]
        S.dma(lambda q: q.dma_start(out=cc[:], in_=I["cc"]), writes=[b_cc])
        S.dma(lambda q: q.dma_start(out=bada[:], in_=I["b_ada"].partition_broadcast(128)), writes=[b_bada])
        S.op("act", lambda e: e.activation(out=sc[:], in_=cc[:], func=AF.Silu), reads=[b_cc], writes=[b_sc])
        S.op("dve", lambda e: e.tensor_copy(out=rep[:], in_=sc[:].unsqueeze(2).to_broadcast([128, 16, 128])),
             reads=[b_sc], writes=[b_rep])
        wv = I["w_ada"].rearrange("(kc p) j -> p kc j", p=128)
        for jc in range(12):
            w, bw = wst[jc % 2], b_wst[jc % 2]
            S.dma(lambda q: q.dma_start(out=w[:], in_=wv[:, :, jc * 512:(jc + 1) * 512]), writes=[bw])
            for s in range(2):
                if s == 1 and jc >= 4:
                    continue
                ps, pb = K.next_ps()
                for kc in range(8):
                    S.op("pe", lambda e: e.matmul(ps[:], lhsT=rep[:, s * 8 + kc, :], rhs=w[:, kc, :],
                                                  start=(kc == 0), stop=(kc == 7)),
                         reads=[b_rep, bw], writes=[pb])
                dst = (K.modx if s == 0 else K.modc)
                db = (K.b_modx if s == 0 else K.b_modc)[jc]
                S.op("dve", lambda e: e.tensor_tensor(out=dst[:, jc * 512:(jc + 1) * 512], in0=ps[:],
                                                      in1=bada[:, jc * 512:(jc + 1) * 512], op=ALU.add),
                     reads=[pb, b_bada], writes=[db])
        gm = K.sb("gm", [128, D], stack=ph)
        gf = K.sb("gf", [128, D], stack=ph)
        b_gm, b_gf = Buf(), Buf()
        S.dma(lambda q: q.dma_start(out=gm[:], in_=I["norm_mix_g"].partition_broadcast(128)), writes=[b_gm])
        S.dma(lambda q: q.dma_start(out=gf[:], in_=I["norm_ffn_g"].partition_broadcast(128)), writes=[b_gf])
        for (t, bl, lo, g, bg) in ((K.modx, K.b_modx, 1, gm, b_gm), (K.modx, K.b_modx, 4, gf, b_gf),
                                   (K.modc, K.b_modc, 1, gm, b_gm)):
            for h in range(2):
                sl = slice(lo * D + h * 512, lo * D + (h + 1) * 512)
                S.op("dve", lambda e: e.scalar_tensor_tensor(out=t[:, sl], in0=t[:, sl], scalar=1.0,
                                                             in1=g[:, h * 512:(h + 1) * 512],
                                                             op0=ALU.add, op1=ALU.mult),
                     reads=[bl[lo * 2 + h], bg], writes=[bl[lo * 2 + h]])
        K.tap("modx", K.modx[:], K.b_modx[11], [128, 6 * D])
        K.tap("modc", K.modc[:], K.b_modc[3], [128, 2 * D])
        S.barrier()


def phase_b(K):
    nc, S, I = K.nc, K.S, K.I
    NCOL = L + 2 + LC
    K.hnT = K.sb("hnT", [128, 8, NCOL], BF16, stack=K.es_bc)
    K.b_hnT = [Buf() for _ in range(NT + NTC)]
    b_pad = Buf()
    S.op("pool", lambda e: e.memset(K.hnT[:, :, 0:1], 0.0), writes=[b_pad])
    S.op("pool", lambda e: e.memset(K.hnT[:, :, L + 1:L + 2], 0.0), writes=[b_pad])
    with ExitStack() as ph:
        xt = [K.sb("xt%d" % i, [128, D], stack=ph) for i in range(2)]
        pt = [K.sb("pt%d" % i, [128, D], stack=ph) for i in range(2)]
        hn = [K.sb("hn%d" % i, [128, D], stack=ph) for i in range(2)]
        junk = K.sb("junk", [128, D], stack=ph)
        st = K.sb("st", [128, 4 * (NT + NTC)], stack=ph)
        b_xt, b_pt, b_hn = [Buf(), Buf()], [Buf(), Buf()], [Buf(), Buf()]
        b_junk, b_st = Buf(), Buf()
        for ti in range(NT + NTC):
            isx = ti < NT
            k = ti % 2
            x_, p_, h_ = xt[k], pt[k], hn[k]
            src = I["x"][ti * 128:(ti + 1) * 128, :] if isx else I["ctx"][(ti - NT) * 128:(ti - NT + 1) * 128, :]
            S.dma(lambda q: q.dma_start(out=x_[:], in_=src), writes=[b_xt[k]])
            if isx:
                S.dma(lambda q: q.dma_start(out=p_[:], in_=I["pos"][ti * 128:(ti + 1) * 128, :]), writes=[b_pt[k]])
                S.op("pool", lambda e: e.tensor_tensor(out=x_[:], in0=x_[:], in1=p_[:], op=ALU.add),
                     reads=[b_pt[k]], writes=[b_xt[k]])
            ss = st[:, 4 * ti:4 * ti + 1]
            rs = st[:, 4 * ti + 1:4 * ti + 2]
            S.op("act", lambda e: e.activation(out=junk[:], in_=x_[:], func=AF.Square, accum_out=ss),
                 reads=[b_xt[k]], writes=[b_junk, b_st])
            S.op("act", lambda e: e.activation(out=rs, in_=ss, func=AF.Sqrt, scale=1.0 / D, bias=EPS),
                 reads=[b_st], writes=[b_st])
            S.op("dve", lambda e: e.reciprocal(out=rs, in_=rs), reads=[b_st], writes=[b_st])
            G = K.modx[:, D:2 * D] if isx else K.modc[:, D:2 * D]
            Sh = K.modx[:, 0:D] if isx else K.modc[:, 0:D]
            S.op("dve", lambda e: e.scalar_tensor_tensor(out=h_[:], in0=x_[:], scalar=rs, in1=G,
                                                         op0=ALU.mult, op1=ALU.mult),
                 reads=[b_xt[k], b_st], writes=[b_hn[k]])
            S.op("pool", lambda e: e.tensor_tensor(out=h_[:], in0=h_[:], in1=Sh, op=ALU.add),
                 reads=[], writes=[b_hn[k]])
            c0 = 1 + ti * 128 if isx else L + 2 + (ti - NT) * 128
            for hh in range(2):
                ps, pb = K.next_ps()
                for j in range(4):
                    kc = hh * 4 + j
                    S.op("pe", lambda e: e.transpose(out=ps[:, j * 128:(j + 1) * 128],
                                                     in_=h_[:, kc * 128:(kc + 1) * 128], identity=K.identt[:]),
                         reads=[b_hn[k], K.b_ident], writes=[pb])
                S.op("act", lambda e: e.activation(out=K.hnT[:, hh * 4:hh * 4 + 4, c0:c0 + 128],
                                                   in_=ps[:].rearrange("p (a b) -> p a b", a=4), func=AF.Copy),
                     reads=[pb], writes=[K.b_hnT[ti]])
        K.tap("hnT", K.hnT[:], K.b_hnT[NT + NTC - 1], [128, 8, NCOL], BF16)
        K.sc_mod = K.scratch("sc_mod", [128, 6 * D])
        K.b_scmod = Buf()
        S.dma(lambda q: q.dma_start(out=K.sc_mod, in_=K.modx[:]), writes=[K.b_scmod])
        S.barrier()


def make_in_maps(inputs):
    hc = host_consts()
    maps = []
    for core in range(8):
        b = core // 2
        r = core % 2
        cc = np.stack([np.asarray(inputs["c"][b]), np.asarray(inputs["c_ctx"])]).reshape(2, 8, 128)
        cc = np.ascontiguousarray(cc.transpose(2, 0, 1).reshape(128, 16)).astype(np.float32)
        m = {
            "x": np.ascontiguousarray(inputs["x"][b]),
            "ctx": np.ascontiguousarray(inputs["ctx"][b]),
            "pos": hc["pos"],
            "ident": hc["ident"],
            "cc": cc,
            "w_ada": np.ascontiguousarray(inputs["w_ada"][0]),
            "b_ada": np.ascontiguousarray(inputs["b_ada"][0]),
            "norm_mix_g": np.ascontiguousarray(inputs["norm_mix_g"][0]),
            "norm_ffn_g": np.ascontiguousarray(inputs["norm_ffn_g"][0]),
            "norm_final_g": np.ascontiguousarray(inputs["norm_final_g"]),
            "w_gl": np.ascontiguousarray(np.concatenate([inputs["w_in"][0][:, r * 128:(r + 1) * 128],
                                                         inputs["w_in"][0][:, 256 + r * 128:256 + (r + 1) * 128],
                                                         inputs["w_in"][0][:, 512 + r * 256:512 + (r + 1) * 256],
                                                         inputs["w_in"][0][:, 1024 + r * 256:1024 + (r + 1) * 256],
                                                         inputs["w_in"][0][:, 1536:1568]], axis=1)),
            "wa_f": np.ascontiguousarray(inputs["gla_wa_f"][0][:, r * 128:(r + 1) * 128]),
            "wa_b": np.ascontiguousarray(inputs["gla_wa_b"][0][:, r * 128:(r + 1) * 128]),
            "gla_ba": np.ascontiguousarray(np.stack([inputs["gla_ba_f"][0][r * 128:(r + 1) * 128],
                                                     inputs["gla_ba_b"][0][r * 128:(r + 1) * 128]], axis=1)),
            "gla_norm_g": np.ascontiguousarray(inputs["gla_norm_g"][0]),
            "hy_conv_w": np.ascontiguousarray(np.concatenate([inputs["hy_conv_w"][0][:, g * 512 + r * 256:g * 512 + (r + 1) * 256] for g in range(3)], axis=1)),
            "hy_conv_b": np.ascontiguousarray(np.concatenate([inputs["hy_conv_b"][0][g * 512 + r * 256:g * 512 + (r + 1) * 256] for g in range(3)])),
            "w_hy": np.ascontiguousarray(np.concatenate([inputs["w_in"][0][:, 1568 + g * 512 + r * 256:1568 + g * 512 + (r + 1) * 256] for g in range(3)], axis=1)),
            "rmask": np.ascontiguousarray(np.tile(np.array([[1.0 - r, float(r)]], dtype=np.float32), (128, 1))),
            "maskF": hc["maskF"],
            "maskB": hc["maskB"],
            "w_out": np.ascontiguousarray(inputs["w_out"][0]),
            "zT": hc["zT"], "tnorm": hc["tnorm"], "deltas": np.ascontiguousarray(hc["deltas"][r * 256:(r + 1) * 256]), "fscale": hc["fscale"], "dft": hc["dft"], "hcoef": hc["hcoef"],
            "ltri": hc["ltri"], "iotaJ": hc["iotaJ"], "jvec": hc["jvec"], "selE": hc["selE"], "tidx": hc["tidx"], "eoff": hc["eoff"],
            "hy_w1": np.ascontiguousarray(inputs["hy_w1"][0]),
            "hy_w2": np.ascontiguousarray(inputs["hy_w2"][0]),
            "hy_w3": np.ascontiguousarray(np.concatenate([inputs["hy_w3"][0][:, od * 512 + r * 256:od * 512 + (r + 1) * 256] for od in range(4)], axis=1)),
            "hy_fb": np.ascontiguousarray(np.stack([inputs["hy_freq"][0], inputs["hy_b1"][0], inputs["hy_b2"][0],
                                                    inputs["hy_b2"][0]], axis=1)),
            "hy_bias": np.ascontiguousarray(inputs["hy_bias"][0][:, r * 256:(r + 1) * 256]),
            "w_router": np.ascontiguousarray(np.concatenate([inputs["w_router"][0][:, r * 8:(r + 1) * 8],
                                                             inputs["w_router"][0][:, (1 - r) * 8:(2 - r) * 8]], axis=1)),
            "w_gate": np.ascontiguousarray(inputs["w_gate"][0][r * 8:(r + 1) * 8]),
            "w_up": np.ascontiguousarray(inputs["w_up"][0][r * 8:(r + 1) * 8]),
            "w_down": np.ascontiguousarray(inputs["w_down"][0][r * 8:(r + 1) * 8]),
        }
        maps.append(m)
    return maps


def kernel(**inputs):
    inputs = {k: np.asarray(v) for k, v in inputs.items()}
    nc, K = build()
    maps = make_in_maps(inputs)
    res = run_bass_kernel_spmd(nc, maps, core_ids=list(range(8)))
    outs = [res.results[2 * b]["out"] for b in range(4)]
    return np.stack(outs).astype(np.float32)


def tokcol(ti):
    return 1 + ti * 128 if ti < NT else L + 2 + (ti - NT) * 128


def phase_c(K):
    nc, S, I = K.nc, K.S, K.I
    K.sc_qk = K.scratch("sc_qk", [256, L + LC])
    K.sc_a = K.scratch("sc_a", [32, L + LC])
    K.sc_v = K.scratch("sc_v", [L + LC, 256], BF16)
    K.sc_sg = K.scratch("sc_sg", [L, 256])
    K.sc_u = K.scratch("sc_u", [L, 768])
    K.b_scr = Buf()
    with ExitStack() as ph:
        wb = [K.sb("wb%d" % i, [128, 8, 512], BF16, stack=ph) for i in range(2)]
        b_wb = [Buf(), Buf()]
        wj = [K.sb("wj%d" % j, [128, 8, 512], BF16, stack=ph) for j in range(3)]
        b_wj = [Buf(), Buf(), Buf()]
        cw = K.sb("cw", [128, 3, 512], stack=ph)
        cb = K.sb("cb", [128, 512], stack=ph)
        b_cw, b_cb = Buf(), Buf()
        stg = [K.sb("stg%d" % i, [128, 512], stack=ph) for i in range(3)]
        b_stg = [Buf(), Buf(), Buf()]
        stgb = [K.sb("stgb%d" % i, [128, 512], BF16, stack=ph) for i in range(2)]
        b_stgb = [Buf(), Buf()]
        wv = I["w_gl"].rearrange("(kc p) j -> p kc j", p=128)
        cnt = {"w": 0, "s": 0, "sb": 0}

        def loadw(c0, n):
            k = cnt["w"] % 2
            cnt["w"] += 1
            S.dma(lambda q: q.dma_start(out=wb[k][:, :, 0:n], in_=wv[:, :, c0:c0 + n]), writes=[b_wb[k]], q="pool")
            return wb[k], b_wb[k]

        def nstg():
            k = cnt["s"] % 3
            cnt["s"] += 1
            return stg[k], b_stg[k]

        allb = K.b_hnT
        tgs = [(1 + g * 512, g * 512, 512) for g in range(8)] + [(L + 2, L, 256)]
        for (c0w, nw, dst, rows) in ((0, 256, K.sc_qk, 128), (768, 32, K.sc_a, 32)):
            w, bw = loadw(c0w, nw)
            for cch in range(nw // rows):
                for (hc0, t0, n) in tgs:
                    ps, pb = K.next_ps()
                    for kc in range(8):
                        S.op("pe", lambda e: e.matmul(ps[0:rows, 0:n], lhsT=w[:, kc, cch * rows:(cch + 1) * rows],
                                                      rhs=K.hnT[:, kc, hc0:hc0 + n], start=(kc == 0), stop=(kc == 7)),
                             reads=[bw] + allb, writes=[pb])
                    st_, bs_ = nstg()
                    S.op("act", lambda e: e.activation(out=st_[0:rows, 0:n], in_=ps[0:rows, 0:n], func=AF.Copy),
                         reads=[pb], writes=[bs_])
                    S.dma(lambda q: q.dma_start(out=dst[cch * rows:(cch + 1) * rows, t0:t0 + n], in_=st_[0:rows, 0:n]),
                          reads=[bs_], writes=[K.b_scr])
        w, bw = loadw(256, 512)
        for ti in range(NT + NTC):
            col = tokcol(ti)
            ps, pb = K.next_ps()
            nn = 512 if ti < NT else 256
            for kc in range(8):
                S.op("pe", lambda e: e.matmul(ps[:, 0:nn], lhsT=K.hnT[:, kc, col:col + 128], rhs=w[:, kc, 0:nn],
                                              start=(kc == 0), stop=(kc == 7)), reads=[bw] + allb, writes=[pb])
            k = cnt["sb"] % 2
            cnt["sb"] += 1
            S.op("act", lambda e: e.activation(out=stgb[k][:, 0:256], in_=ps[:, 0:256], func=AF.Copy), reads=[pb], writes=[b_stgb[k]])
            S.dma(lambda q: q.dma_start(out=K.sc_v[ti * 128:(ti + 1) * 128, :], in_=stgb[k][:, 0:256]),
                  reads=[b_stgb[k]], writes=[K.b_scr])
            if ti < NT:
                st_, bs_ = nstg()
                S.op("act", lambda e: e.activation(out=st_[:, 0:256], in_=ps[:, 256:512], func=AF.Silu), reads=[pb], writes=[bs_])
                S.dma(lambda q: q.dma_start(out=K.sc_sg[ti * 128:(ti + 1) * 128, :], in_=st_[:, 0:256]),
                      reads=[bs_], writes=[K.b_scr])
        wvh = I["w_hy"].rearrange("(kc p) j -> p kc j", p=128)
        for (c0, n) in ((0, 512), (512, 256)):
            k = cnt["w"] % 2
            cnt["w"] += 1
            w, bw = wb[k], b_wb[k]
            S.dma(lambda q: q.dma_start(out=w[:, :, 0:n], in_=wvh[:, :, c0:c0 + n]), writes=[bw], q="pool")
            S.dma(lambda q: q.dma_start(out=cw[:, :, 0:n], in_=I["hy_conv_w"][:, c0:c0 + n].partition_broadcast(128)), writes=[b_cw])
            S.dma(lambda q: q.dma_start(out=cb[:, 0:n], in_=I["hy_conv_b"][c0:c0 + n].partition_broadcast(128)), writes=[b_cb])
            for j in range(3):
                S.op("dve", lambda e: e.tensor_tensor(out=wj[j][:, :, 0:n], in0=w[:, :, 0:n],
                                                      in1=cw[:, j:j + 1, 0:n].to_broadcast([128, 8, n]), op=ALU.mult),
                     reads=[bw, b_cw], writes=[b_wj[j]])
            for ti in range(NT):
                col = tokcol(ti)
                ps, pb = K.next_ps()
                m = 0
                for j in range(3):
                    for kc in range(8):
                        S.op("pe", lambda e: e.matmul(ps[:, 0:n], lhsT=K.hnT[:, kc, col + j - 1:col + j - 1 + 128],
                                                      rhs=wj[j][:, kc, 0:n], start=(m == 0), stop=(m == 23)),
                             reads=[b_wj[j]] + allb, writes=[pb])
                        m += 1
                st_, bs_ = nstg()
                S.op("dve", lambda e: e.tensor_tensor(out=st_[:, 0:n], in0=ps[:, 0:n], in1=cb[:, 0:n], op=ALU.add),
                     reads=[pb, b_cb], writes=[bs_])
                S.dma(lambda q: q.dma_start(out=K.sc_u[ti * 128:(ti + 1) * 128, c0:c0 + n], in_=st_[:, 0:n]),
                      reads=[bs_], writes=[K.b_scr])
        S.barrier()


def phase_g(K):
    nc, S, I = K.nc, K.S, K.I
    K.sc_o = K.scratch("sc_o", [L, 256])
    NCH = (L + LC) // 64
    with ExitStack() as ph:
        sb = lambda name, shape, dt=F32: K.sb("g_" + name, shape, dt, stack=ph)
        vtm = sb("vtm", [128, NT + NTC, 256], BF16)
        b_vtm = Buf()
        S.dma(lambda q: q.dma_start(out=vtm[:], in_=K.sc_v.rearrange("(t p) c -> p t c", p=128)),
              reads=[K.b_scr], writes=[b_vtm])
        wa = [sb("wa%d" % d, [16, 128]) for d in range(2)]
        b_wa = Buf()
        S.dma(lambda q: q.dma_start(out=wa[0][:], in_=I["wa_f"]), writes=[b_wa])
        S.dma(lambda q: q.dma_start(out=wa[1][:], in_=I["wa_b"]), writes=[b_wa])
        nba = sb("nba", [128, 2])
        b_nba = Buf()
        S.dma(lambda q: q.dma_start(out=nba[:], in_=I["gla_ba"]), writes=[b_nba])
        S.op("dve", lambda e: e.tensor_scalar(out=nba[:], in0=nba[:], scalar1=-1.0, scalar2=None, op0=ALU.mult),
             reads=[], writes=[b_nba])
        mk = [sb("mk%d" % d, [128, 128]) for d in range(2)]
        b_mk = Buf()
        S.dma(lambda q: q.dma_start(out=mk[0][:], in_=I["maskF"]), writes=[b_mk])
        S.dma(lambda q: q.dma_start(out=mk[1][:], in_=I["maskB"]), writes=[b_mk])
        smask = sb("smask", [128, 512])
        b_sm = Buf()
        S.op("pool", lambda e: e.memset(smask[:], 1.0), writes=[b_sm])
        S.op("pool", lambda e: e.memset(smask[:].rearrange("p (n c) -> p n c", c=64)[:, :, 0:1], 0.0), writes=[b_sm])
        QF = [sb("QF%d" % d, [128, L], BF16) for d in range(2)]
        KF = [sb("KF%d" % d, [128, L], BF16) for d in range(2)]
        QS = [sb("QS%d" % d, [128, L], BF16) for d in range(2)]
        KU = [sb("KU%d" % d, [128, NT + NTC, 128], BF16) for d in range(2)]
        dec = [sb("dec%d" % d, [128, NCH]) for d in range(2)]
        Sh = [sb("Sh%d" % d, [128, 64, 128], BF16) for d in range(2)]
        b_prep = [Buf(), Buf()]
        b_Sh = [Buf(), Buf()]
        Sst = [[sb("S%d_%d" % (d, i), [128, 128]) for i in range(2)] for d in range(2)]
        b_S = [[Buf(), Buf()], [Buf(), Buf()]]
        q32s = [sb("q32_%d" % i, [128, 512]) for i in range(2)]; k32s = [sb("k32_%d" % i, [128, 512]) for i in range(2)]
        a16s = [[sb("a16_%d_%d" % (i, d), [16, 512]) for d in range(2)] for i in range(2)]
        b_ins = [Buf(), Buf()]
        tmpd = [{n: sb("%s%d" % (n, d), [128, 512]) for n in ("tl", "tb", "tx", "td", "te", "tku")} for d in range(2)]
        b_td = [{n: Buf() for n in ("tl", "tb", "tx", "td", "te", "tku")} for d in range(2)]
        scf = [sb("scf%d" % i, [128, 128], BF16) for i in range(2)]
        scb = [sb("scb%d" % i, [128, 128], BF16) for i in range(2)]
        b_sc = [[Buf(), Buf()], [Buf(), Buf()]]
        ost = [sb("ost%d" % i, [128, 256]) for i in range(2)]
        b_ost = [Buf(), Buf()]
        tgs = [(g * 512, 512) for g in range(8)] + [(L, 256)]
        for hp in range(1):
            def g_load(gi):
                (t0, n) = tgs[gi]
                kk = gi % 2
                S.dma(lambda q: q.dma_start(out=k32s[kk][:, 0:n], in_=K.sc_qk[128:256, t0:t0 + n]),
                      reads=[K.b_scr], writes=[b_ins[kk]])
                if t0 < L:
                    S.dma(lambda q: q.dma_start(out=q32s[kk][:, 0:n], in_=K.sc_qk[0:128, t0:t0 + n]),
                          reads=[K.b_scr], writes=[b_ins[kk]])
                for d in range(2):
                    S.dma(lambda q: q.dma_start(out=a16s[kk][d][:, 0:n], in_=K.sc_a[d * 16:(d + 1) * 16, t0:t0 + n]),
                          reads=[K.b_scr], writes=[b_ins[kk]])

            g_load(0)
            for gi, (t0, n) in enumerate(tgs):
                isx = t0 < L
                nch = n // 64
                if gi + 1 < len(tgs):
                    g_load(gi + 1)
                q32, k32, a16, b_in = q32s[gi % 2], k32s[gi % 2], a16s[gi % 2], b_ins[gi % 2]
                for d in range(2):
                    tl, tb, tx, td, te, tku = (tmpd[d][n_] for n_ in ("tl", "tb", "tx", "td", "te", "tku"))
                    b_t = b_td[d]
                    ps, pb = K.next_ps()
                    S.op("pe", lambda e: e.matmul(ps[:, 0:n], lhsT=wa[d][:, hp * 128:(hp + 1) * 128], rhs=a16[d][:, 0:n],
                                                  start=True, stop=True), reads=[b_wa, b_in], writes=[pb])
                    S.op("act", lambda e: e.activation(out=te[:, 0:n], in_=ps[:, 0:n], func=AF.Exp, scale=-1.0,
                                                       bias=nba[:, d:d + 1]),
                         reads=[pb, b_nba], writes=[b_t["te"]])
                    S.op("act", lambda e: e.activation(out=tl[:, 0:n], in_=te[:, 0:n], func=AF.Ln, bias=1.0),
                         reads=[b_t["te"]], writes=[b_t["tl"]])
                    S.op("dve", lambda e: e.tensor_tensor_scan(out=tb[:, 0:n], data0=smask[:, 0:n], data1=tl[:, 0:n],
                                                               initial=0.0, op0=ALU.mult, op1=ALU.add),
                         reads=[b_t["tl"], b_sm], writes=[b_t["tb"]])
                    tbv = tb[:, 0:n].rearrange("p (n c) -> p n c", c=64)
                    if d == 0:
                        X, bX, ridx, tidx = tb, b_t["tb"], 32, 63
                    else:
                        S.op("dve", lambda e: e.tensor_tensor(out=tx[:, 0:n], in0=tl[:, 0:n], in1=tb[:, 0:n], op=ALU.subtract),
                             reads=[b_t["tl"], b_t["tb"]], writes=[b_t["tx"]])
                        S.op("dve", lambda e: e.tensor_tensor(out=tx[:, 0:n].rearrange("p (n c) -> p n c", c=64),
                                                              in0=tx[:, 0:n].rearrange("p (n c) -> p n c", c=64),
                                                              in1=tbv[:, :, 63:64].to_broadcast([128, nch, 64]), op=ALU.add),
                             reads=[b_t["tb"]], writes=[b_t["tx"]])
                        X, bX, ridx, tidx = tx, b_t["tx"], 31, 0
                    Xv = X[:, 0:n].rearrange("p (n c) -> p n c", c=64)
                    tdv = td[:, 0:n].rearrange("p (n c) -> p n c", c=64)
                    c0 = t0 // 64
                    S.op("act", lambda e: e.activation(out=dec[d][:, c0:c0 + nch].unsqueeze(2), in_=Xv[:, :, tidx:tidx + 1],
                                                       func=AF.Exp, scale=-1.0 / 16), reads=[bX], writes=[b_prep[d]])
                    if isx:
                        S.op("dve", lambda e: e.tensor_tensor(out=tdv, in0=Xv, in1=Xv[:, :, ridx:ridx + 1].to_broadcast([128, nch, 64]),
                                                              op=ALU.subtract), reads=[bX], writes=[b_t["td"]])
                        S.op("act", lambda e: e.activation(out=te[:, 0:n], in_=td[:, 0:n], func=AF.Exp, scale=-1.0 / 16),
                             reads=[b_t["td"]], writes=[b_t["te"]])
                        S.op("dve", lambda e: e.scalar_tensor_tensor(out=QF[d][:, t0:t0 + n], in0=q32[:, 0:n], scalar=0.125,
                                                                     in1=te[:, 0:n], op0=ALU.mult, op1=ALU.mult),
                             reads=[b_in, b_t["te"]], writes=[b_prep[d]])
                        S.op("act", lambda e: e.activation(out=te[:, 0:n], in_=td[:, 0:n], func=AF.Exp, scale=1.0 / 16),
                             reads=[b_t["td"]], writes=[b_t["te"]])
                        S.op("dve", lambda e: e.tensor_tensor(out=KF[d][:, t0:t0 + n], in0=k32[:, 0:n], in1=te[:, 0:n], op=ALU.mult),
                             reads=[b_in, b_t["te"]], writes=[b_prep[d]])
                        S.op("act", lambda e: e.activation(out=te[:, 0:n], in_=X[:, 0:n], func=AF.Exp, scale=-1.0 / 16),
                             reads=[bX], writes=[b_t["te"]])
                        S.op("dve", lambda e: e.scalar_tensor_tensor(out=QS[d][:, t0:t0 + n], in0=q32[:, 0:n], scalar=0.125,
                                                                     in1=te[:, 0:n], op0=ALU.mult, op1=ALU.mult),
                             reads=[b_in, b_t["te"]], writes=[b_prep[d]])
                    S.op("dve", lambda e: e.tensor_tensor(out=tdv, in0=Xv, in1=Xv[:, :, tidx:tidx + 1].to_broadcast([128, nch, 64]),
                                                          op=ALU.subtract), reads=[bX], writes=[b_t["td"]])
                    S.op("act", lambda e: e.activation(out=te[:, 0:n], in_=td[:, 0:n], func=AF.Exp, scale=1.0 / 16),
                         reads=[b_t["td"]], writes=[b_t["te"]])
                    S.op("dve", lambda e: e.tensor_tensor(out=tku[:, 0:n], in0=k32[:, 0:n], in1=te[:, 0:n], op=ALU.mult),
                         reads=[b_in, b_t["te"]], writes=[b_t["tku"]])
                    for j in range(n // 128):
                        ps, pb = K.next_ps()
                        S.op("pe", lambda e: e.transpose(out=ps[:, 0:128], in_=tku[:, j * 128:(j + 1) * 128], identity=K.identt[:]),
                             reads=[b_t["tku"]], writes=[pb])
                        S.op("act", lambda e: e.activation(out=KU[d][:, t0 // 128 + j, :], in_=ps[:, 0:128], func=AF.Copy),
                             reads=[pb], writes=[b_prep[d]])
            orders = [[64, 65, 66, 67] + list(range(64)), [67, 66, 65, 64] + list(range(63, -1, -1))]
            curs = [0, 0]
            for d in range(2):
                S.op("pool", lambda e: e.memset(Sst[d][0][:], 0.0), writes=[b_S[d][0]])
            for step in range(68):
                for d in range(2):
                    n_ = orders[d][step]
                    cur = curs[d]
                    tile, off = n_ // 2, (n_ % 2) * 64
                    ps, pb = K.next_ps()
                    for h in range(2):
                        S.op("pe", lambda e: e.matmul(ps[h * 64:(h + 1) * 64, 0:128],
                                                      lhsT=KU[d][off:off + 64, tile, h * 64:(h + 1) * 64],
                                                      rhs=vtm[off:off + 64, tile, (2 * hp + h) * 128:(2 * hp + h + 1) * 128],
                                                      start=True, stop=True),
                             reads=[b_prep[d], b_vtm], writes=[pb])
                    if n_ < 64:
                        S.op("act", lambda e: e.activation(out=Sh[d][:, n_, :], in_=Sst[d][cur][:], func=AF.Copy),
                             reads=[b_S[d][cur]], writes=[b_Sh[d]])
                    S.op("dve", lambda e: e.scalar_tensor_tensor(out=Sst[d][1 - cur][:], in0=Sst[d][cur][:],
                                                                 scalar=dec[d][:, n_:n_ + 1], in1=ps[:, 0:128],
                                                                 op0=ALU.mult, op1=ALU.add),
                         reads=[b_S[d][cur], pb, b_prep[d]], writes=[b_S[d][1 - cur]])
                    curs[d] = 1 - cur
            for ti in range(NT):
                os_, bo_ = ost[ti % 2], b_ost[ti % 2]
                for h in range(2):
                    hs = slice(h * 64, (h + 1) * 64)
                    tk = slice(ti * 128, (ti + 1) * 128)
                    scs = []
                    for d in range(2):
                        ps, pb = K.next_ps()
                        S.op("pe", lambda e: e.matmul(ps[:, 0:128], lhsT=KF[d][hs, tk], rhs=QF[d][hs, tk], start=True, stop=True),
                             reads=[b_prep[d]], writes=[pb])
                        sc_ = (scf if d == 0 else scb)[h]
                        S.op("dve", lambda e: e.tensor_tensor(out=sc_[:], in0=ps[:, 0:128], in1=mk[d][:], op=ALU.mult),
                             reads=[pb, b_mk], writes=[b_sc[d][h]])
                        scs.append(sc_)
                    ps, pb = K.next_ps()
                    vh = vtm[:, ti, (2 * hp + h) * 128:(2 * hp + h + 1) * 128]
                    S.op("pe", lambda e: e.matmul(ps[:, 0:128], lhsT=scs[0][:], rhs=vh, start=True, stop=False),
                         reads=[b_sc[0][h], b_vtm], writes=[pb])
                    S.op("pe", lambda e: e.matmul(ps[:, 0:128], lhsT=scs[1][:], rhs=vh, start=False, stop=False),
                         reads=[b_sc[1][h], b_vtm], writes=[pb])
                    for d in range(2):
                        for j in range(2):
                            n_ = 2 * ti + j
                            last = (d == 1 and j == 1)
                            S.op("pe", lambda e: e.matmul(ps[j * 64:(j + 1) * 64, 0:128],
                                                          lhsT=QS[d][hs, n_ * 64:(n_ + 1) * 64], rhs=Sh[d][hs, n_, :],
                                                          start=False, stop=last),
                                 reads=[b_prep[d], b_Sh[d]], writes=[pb])
                    S.op("act", lambda e: e.activation(out=os_[:, h * 128:(h + 1) * 128], in_=ps[:, 0:128], func=AF.Copy),
                         reads=[pb], writes=[bo_])
                S.dma(lambda q: q.dma_start(out=K.sc_o[ti * 128:(ti + 1) * 128, hp * 256:(hp + 1) * 256], in_=os_[:]),
                      reads=[bo_], writes=[K.b_scr])
            S.barrier()


def phase_gn(K):
    nc, S, I = K.nc, K.S, K.I
    K.sc_ygla = K.scratch("sc_ygla", [L, 512])
    K.sc_ygla2 = K.scratch("sc_ygla2", [L, 512])
    K.b_ygla2 = [Buf(), Buf()]
    with ExitStack() as ph:
        sb = lambda name, shape, dt=F32: K.sb("n_" + name, shape, dt, stack=ph)
        ng = sb("ng", [128, 128]); b_ng = Buf()
        S.dma(lambda q: q.dma_start(out=ng[:], in_=I["gla_norm_g"].partition_broadcast(128)), writes=[b_ng])
        rmask = sb("rmask", [128, 2]); b_rm = Buf()
        S.dma(lambda q: q.dma_start(out=rmask[:], in_=I["rmask"]), writes=[b_rm])
        ot = [sb("ot%d" % i, [128, 256]) for i in range(2)]
        gt = [sb("gt%d" % i, [128, 256]) for i in range(2)]
        yt = [sb("yt%d" % i, [128, 256]) for i in range(2)]
        ym_ = [sb("ym%d" % i, [128, 2, 256]) for i in range(2)]
        junk = sb("junk", [128, 128])
        st = sb("st", [128, 8 * NT])
        b_o, b_g, b_y, b_ym = [Buf(), Buf()], [Buf(), Buf()], [Buf(), Buf()], [Buf(), Buf()]
        b_junk, b_st = Buf(), Buf()
        def gn_load(ti):
            k = ti % 2
            S.dma(lambda q: q.dma_start(out=ot[k][:], in_=K.sc_o[ti * 128:(ti + 1) * 128, :]), reads=[K.b_scr], writes=[b_o[k]])
            S.dma(lambda q: q.dma_start(out=gt[k][:], in_=K.sc_sg[ti * 128:(ti + 1) * 128, :]), reads=[K.b_scr], writes=[b_g[k]])

        gn_load(0)
        for ti in range(NT):
            k = ti % 2
            if ti + 1 < NT:
                gn_load(ti + 1)
            ss = st[:, 8 * ti:8 * ti + 2]
            rs = st[:, 8 * ti + 4:8 * ti + 6]
            for h in range(2):
                S.op("act", lambda e: e.activation(out=junk[:], in_=ot[k][:, h * 128:(h + 1) * 128], func=AF.Square,
                                                   accum_out=st[:, 8 * ti + h:8 * ti + h + 1]),
                     reads=[b_o[k]], writes=[b_junk, b_st])
            S.op("act", lambda e: e.activation(out=rs, in_=ss, func=AF.Sqrt, scale=1.0 / 128, bias=EPS), reads=[b_st], writes=[b_st])
            S.op("dve", lambda e: e.reciprocal(out=rs, in_=rs), reads=[b_st], writes=[b_st])
            for h in range(2):
                hs = slice(h * 128, (h + 1) * 128)
                S.op("dve", lambda e: e.scalar_tensor_tensor(out=yt[k][:, hs], in0=ot[k][:, hs], scalar=st[:, 8 * ti + 4 + h:8 * ti + 5 + h],
                                                             in1=gt[k][:, hs], op0=ALU.mult, op1=ALU.mult),
                     reads=[b_o[k], b_g[k], b_st], writes=[b_y[k]])
                S.op("pool", lambda e: e.tensor_tensor(out=yt[k][:, hs], in0=yt[k][:, hs], in1=ng[:], op=ALU.mult),
                     reads=[b_ng], writes=[b_y[k]])
            for m_ in range(2):
                S.op("dve", lambda e: e.tensor_scalar(out=ym_[k][:, m_, :], in0=yt[k][:], scalar1=rmask[:, m_:m_ + 1], scalar2=None, op0=ALU.mult),
                     reads=[b_y[k], b_rm], writes=[b_ym[k]])
            S.dma(lambda q: q.dma_start(out=K.sc_ygla[ti * 128:(ti + 1) * 128, :].rearrange("p (a c) -> p a c", a=2), in_=ym_[k][:]),
                  reads=[b_ym[k]], writes=[K.b_scr])
        S.barrier()
        for ch in range(2):
            S.coll(lambda g: g.collective_compute("AllReduce", ALU.add, replica_groups=[[0, 1], [2, 3], [4, 5], [6, 7]],
                                                  ins=[K.sc_ygla[ch * 2048:(ch + 1) * 2048, :]], outs=[K.sc_ygla2[ch * 2048:(ch + 1) * 2048, :]]),
                   reads=[K.b_scr], writes=[K.b_ygla2[ch]])
        S.barrier()


def phase_h(K):
    nc, S, I = K.nc, K.S, K.I
    PI = math.pi
    NB, PT, FC = 4, 8, 9
    with ExitStack() as ph:
        sb = lambda name, shape, dt=F32: K.sb("h_" + name, shape, dt, stack=ph)
        K.sc_h = K.scratch("sc_h", [4, L, 256], BF16)
        with ExitStack() as ph2:
            sb2 = lambda name, shape, dt=F32: K.sb("h2_" + name, shape, dt, stack=ph2)
            hd2 = [sb2("hd2_%d" % d, [64, L]) for d in range(2)]; b_hd2 = [Buf(), Buf()]
            w3 = sb2("w3", [64, 1024]); b_w3 = Buf()
            S.dma(lambda q: q.dma_start(out=w3[:], in_=I["hy_w3"]), writes=[b_w3])
            fbv = sb2("fb", [64, 4]); b_fb = Buf()
            S.dma(lambda q: q.dma_start(out=fbv[:], in_=I["hy_fb"]), writes=[b_fb])
            fbb = sb2("fbb", [64, 2])
            S.op("dve", lambda e: e.tensor_tensor(out=fbb[:], in0=fbv[:, 1:3], in1=fbv[:, 0:1].to_broadcast([64, 2]), op=ALU.mult),
                 reads=[b_fb], writes=[b_fb])
            tn = sb2("tn", [128, 2, NT]); dl = sb2("dl", [128, 256]); b_c2 = Buf()
            S.dma(lambda q: q.dma_start(out=tn[:], in_=I["tnorm"]), writes=[b_c2])
            S.dma(lambda q: q.dma_start(out=dl[:], in_=I["deltas"].partition_broadcast(128)), writes=[b_c2])
            S.op("dve", lambda e: e.tensor_scalar(out=tn[:], in0=tn[:], scalar1=-1.0, scalar2=None, op0=ALU.mult), reads=[], writes=[b_c2])
            brow = sb2("brow", [1, 512]); b_brow = Buf()
            S.dma(lambda q: q.dma_start(out=brow[:], in_=I["hy_bias"].rearrange("o c -> (o c)").unsqueeze(0)), writes=[b_brow])
            zT = sb2("zT", [33, 2, L]); b_z = Buf()
            S.dma(lambda q: q.dma_start(out=zT[:], in_=I["zT"]), writes=[b_z])
            w1 = sb2("w1", [33, 64]); w2 = sb2("w2", [64, 64]); b_w = Buf()
            S.dma(lambda q: q.dma_start(out=w1[:], in_=I["hy_w1"]), writes=[b_w])
            S.dma(lambda q: q.dma_start(out=w2[:], in_=I["hy_w2"]), writes=[b_w])
            hd1 = sb2("hd1", [64, L]); b_hd1 = Buf()
            arg = [sb2("arg%d" % i, [64, 512]) for i in range(2)]; b_arg = [Buf(), Buf()]
            wr1 = sb2("wr1", [64, 512]); wr2 = sb2("wr2", [64, 512]); b_wr1, b_wr2 = Buf(), Buf()
            for dr in range(2):
                for layer in range(2):
                    for tg in range(8):
                        ts_ = slice(tg * 512, (tg + 1) * 512)
                        ps, pb = K.next_ps()
                        if layer == 0:
                            S.op("pe", lambda e: e.matmul(ps[0:64, :], lhsT=w1[:], rhs=zT[:, dr, ts_], start=True, stop=True),
                                 reads=[b_w, b_z], writes=[pb])
                        else:
                            S.op("pe", lambda e: e.matmul(ps[0:64, :], lhsT=w2[:], rhs=hd1[:, ts_], start=True, stop=True),
                                 reads=[b_w, b_hd1], writes=[pb])
                        a_, ba_ = arg[tg % 2], b_arg[tg % 2]
                        S.op("act", lambda e: e.activation(out=a_[:], in_=ps[0:64, :], func=AF.Identity, scale=fbv[:, 0:1],
                                                           bias=fbb[:, layer:layer + 1]), reads=[pb, b_fb], writes=[ba_])
                        S.op("dve", lambda e: e.tensor_scalar(out=wr1[:], in0=a_[:], scalar1=PI, scalar2=-2 * PI, op0=ALU.is_gt, op1=ALU.mult),
                             reads=[ba_], writes=[b_wr1])
                        S.op("dve", lambda e: e.tensor_scalar(out=wr2[:], in0=a_[:], scalar1=-PI, scalar2=2 * PI, op0=ALU.is_lt, op1=ALU.mult),
                             reads=[ba_], writes=[b_wr2])
                        S.op("dve", lambda e: e.tensor_tensor(out=a_[:], in0=a_[:], in1=wr1[:], op=ALU.add), reads=[b_wr1], writes=[ba_])
                        S.op("dve", lambda e: e.tensor_tensor(out=a_[:], in0=a_[:], in1=wr2[:], op=ALU.add), reads=[b_wr2], writes=[ba_])
                        dst, bd = (hd1, b_hd1) if layer == 0 else (hd2[dr], b_hd2[dr])
                        S.op("act", lambda e: e.activation(out=dst[:, ts_], in_=a_[:], func=AF.Sin), reads=[ba_], writes=[bd])
            dk = [sb2("dk%d" % d, [128, 256]) for d in range(2)]; b_dk = [Buf(), Buf()]
            ht = [sb2("ht%d" % i, [128, 256]) for i in range(2)]; b_ht = [Buf(), Buf()]
            hbf = [sb2("hbf%d" % i, [128, 256], BF16) for i in range(3)]; b_hbf = [Buf(), Buf(), Buf()]
            hn_ = 0
            for ti in range(NT):
                for dr in range(2):
                    S.op("act", lambda e: e.activation(out=dk[dr][:], in_=dl[:], func=AF.Exp, scale=tn[:, dr, ti:ti + 1]),
                         reads=[b_c2], writes=[b_dk[dr]])
                for od in range(4):
                    dr = od % 2
                    ps, pb = K.next_ps()
                    S.op("pe", lambda e: e.matmul(ps[:, 0:256], lhsT=hd2[dr][:, ti * 128:(ti + 1) * 128],
                                                  rhs=w3[:, od * 256:(od + 1) * 256], start=True, stop=True),
                         reads=[b_hd2[dr], b_w3], writes=[pb])
                    h_, bh_ = ht[hn_ % 2], b_ht[hn_ % 2]
                    hb_, bhb_ = hbf[hn_ % 3], b_hbf[hn_ % 3]
                    hn_ += 1
                    if ti == 0 and dr == 0:
                        o_ = od // 2
                        S.op("dve", lambda e: e.tensor_tensor(out=h_[:], in0=ps[:, 0:256], in1=dk[dr][:], op=ALU.mult),
                             reads=[pb, b_dk[dr]], writes=[bh_])
                        S.op("dve", lambda e: e.tensor_tensor(out=h_[0:1, :], in0=h_[0:1, :], in1=brow[:, o_ * 256:(o_ + 1) * 256], op=ALU.add),
                             reads=[b_brow], writes=[bh_])
                        S.op("act", lambda e: e.activation(out=hb_[:], in_=h_[:], func=AF.Copy), reads=[bh_], writes=[bhb_])
                    else:
                        S.op("dve", lambda e: e.tensor_tensor(out=hb_[:], in0=ps[:, 0:256], in1=dk[dr][:], op=ALU.mult),
                             reads=[pb, b_dk[dr]], writes=[bhb_])
                    S.dma(lambda q: q.dma_start(out=K.sc_h[od, ti * 128:(ti + 1) * 128, :], in_=hb_[:]), reads=[bhb_], writes=[K.b_scr])
            S.barrier()
        TAB = sb("TAB", [128, FC, 2, FC * 128], BF16); b_TAB = Buf()
        S.dma(lambda q: q.dma_start(out=TAB[:], in_=I["dft"]), writes=[b_TAB])
        cf = sb("cf", [128, 6, FC]); b_cf = Buf()
        S.dma(lambda q: q.dma_start(out=cf[:], in_=I["hcoef"]), writes=[b_cf])
        HF = sb("HF", [128, NT, 256], BF16); HB = sb("HB", [128, NT, 256], BF16); U = sb("U", [128, NT, 256], BF16)
        b_HF, b_HB, b_U = Buf(), Buf(), Buf()
        Y = sb("Y", [128, NB, FC, 2, 256], BF16); b_Y = Buf()
        RR2 = [[sb("RR%d_%d" % (bf_, tab), [128, 4, 256]) for tab in range(2)] for bf_ in range(2)]
        b_RR2 = [[Buf(), Buf()], [Buf(), Buf()]]
        PA2 = [[sb("PA%d_%d" % (bf_, tab), [128, 8, 256]) for tab in range(2)] for bf_ in range(2)]
        b_PA2 = [[Buf(), Buf()], [Buf(), Buf()]]
        SX2 = [[sb("SX%d_%d" % (bf_, tab), [128, 4, 256], BF16) for tab in range(2)] for bf_ in range(2)]
        b_SX2 = [[Buf(), Buf()], [Buf(), Buf()]]
        CK = [sb("CK%d" % tab, [128, 7, 256], BF16) for tab in range(2)]; b_CK = [Buf(), Buf()]
        XN = sb("XN", [128, 4, 256], BF16); b_XN = Buf()
        T1 = [sb("T1_%d" % i, [128, 4, 256]) for i in range(2)]; b_T1 = [Buf(), Buf()]
        TB = [sb("TB%d" % i, [128, 4, 256], BF16) for i in range(2)]; b_TB = [Buf() for _ in range(2)]
        identb = sb("identb", [128, 128], BF16); b_idb = Buf()
        S.op("act", lambda e: e.activation(out=identb[:], in_=K.identt[:], func=AF.Copy), reads=[K.b_ident], writes=[b_idb])
        xg = [sb("xg%d" % i, [128, 256]) for i in range(2)]; b_xg = [Buf(), Buf()]
        yo = [sb("yo%d" % i, [128, 256]) for i in range(2)]; b_yo = [Buf(), Buf()]
        ld, b_ld = yo, b_yo
        K.sc_yhy = K.scratch("sc_yhy", [L, 512])
        K.sc_yhy2 = K.scratch("sc_yhy2", [L, 512])
        K.b_yhy2 = [Buf(), Buf()]
        st_bufs = [[], []]
        rmask = sb("rmask", [128, 2]); b_rm = Buf()
        S.dma(lambda q: q.dma_start(out=rmask[:], in_=I["rmask"]), writes=[b_rm])
        for hh in range(1):
            cs = slice(0, 256)
            for ti in range(NT):
                k = ti % 2
                S.dma(lambda q: q.dma_start(out=ld[k][:], in_=K.sc_u[ti * 128:(ti + 1) * 128, 0:256]), reads=[K.b_scr], writes=[b_ld[k]])
                S.op("act", lambda e: e.activation(out=U[:, ti, :], in_=ld[k][:], func=AF.Copy), reads=[b_ld[k]], writes=[b_U])
            for o in range(2):
                for g4 in range(4):
                    gs = slice(g4 * 8, (g4 + 1) * 8)
                    S.dma(lambda q: q.dma_start(out=HF[:, gs, :], in_=K.sc_h[2 * o].rearrange("(t p) c -> p t c", p=128)[:, gs, cs]),
                          reads=[K.b_scr], writes=[b_HF])
                    gr = slice((3 - g4) * 8, (4 - g4) * 8)
                    S.dma(lambda q: q.dma_start(out=HB[:, gr, :], in_=K.sc_h[2 * o + 1].rearrange("(t p) c -> p t c", p=128)[:, gs, cs]),
                          reads=[K.b_scr], writes=[b_HB])
                sigs = [(HF, b_HF, n) for n in range(4)] + [(HB, b_HB, n) for n in range(4)] + [(U, b_U, n) for n in range(4)]
                def emit_tr(fc):
                    fs_ = slice(fc * 128, (fc + 1) * 128)
                    PA, RR, SX = PA2[fc % 2], RR2[fc % 2], SX2[fc % 2]
                    b_PA, b_RR, b_SX = b_PA2[fc % 2], b_RR2[fc % 2], b_SX2[fc % 2]
                    for tab in range(2):
                        banks = [K.next_ps() for _ in range(6)]
                        for a in range(PT):
                            for s2 in range(6):
                                src, bsrc, n = sigs[2 * s2]
                                ps, pb = banks[s2]
                                S.op("pe", lambda e: e.matmul(ps[:].rearrange("p (n c) -> p n c", n=2), lhsT=TAB[:, a, tab, fs_],
                                                              rhs=src[:].rearrange("p (n a) c -> p a n c", a=PT)[:, a, n:n + 2, :],
                                                              start=(a == 0), stop=(a == PT - 1)),
                                     reads=[b_TAB, bsrc], writes=[pb])
                        for s2 in range(6):
                            ps, pb = banks[s2]
                            pv = ps[:].rearrange("p (n c) -> p n c", n=2)
                            if s2 < 2:
                                S.op("act", lambda e: e.activation(out=PA[tab][:, 4 + 2 * s2:6 + 2 * s2, :], in_=pv, func=AF.Copy, scale=cf[:, 0, fc:fc + 1]),
                                     reads=[pb, b_cf], writes=[b_PA[tab]])
                            elif s2 < 4:
                                S.op("act", lambda e: e.activation(out=RR[tab][:, 2 * (s2 - 2):2 * (s2 - 2) + 2, :], in_=pv, func=AF.Copy,
                                                                   scale=cf[:, 0, fc:fc + 1]), reads=[pb, b_cf], writes=[b_RR[tab]])
                            else:
                                S.op("act", lambda e: e.activation(out=SX[tab][:, 2 * (s2 - 4):2 * (s2 - 4) + 2, :], in_=pv, func=AF.Copy),
                                     reads=[pb], writes=[b_SX[tab]])
                def emit_mt(fc):
                    PA, RR, SX = PA2[fc % 2], RR2[fc % 2], SX2[fc % 2]
                    b_PA, b_RR, b_SX = b_PA2[fc % 2], b_RR2[fc % 2], b_SX2[fc % 2]
                    for (tab, ia, ib, op_) in ((0, 3, 2, ALU.subtract), (1, 5, 4, ALU.add)):
                        S.op("pool", lambda e: e.tensor_scalar(out=T1[0][:], in0=RR[1][:], scalar1=cf[:, ia, fc:fc + 1], scalar2=0.0, op0=ALU.mult, op1=ALU.add),
                             reads=[b_RR[1], b_cf], writes=[b_T1[0]])
                        S.op("pool", lambda e: e.tensor_scalar(out=T1[1][:], in0=RR[0][:], scalar1=cf[:, ib, fc:fc + 1], scalar2=0.0, op0=ALU.mult, op1=ALU.add),
                             reads=[b_RR[0], b_cf], writes=[b_T1[1]])
                        S.op("pool", lambda e: e.tensor_tensor(out=PA[tab][:, 0:4, :], in0=T1[1][:], in1=T1[0][:], op=op_),
                             reads=[b_T1[0], b_T1[1]], writes=[b_PA[tab]])
                    for tab in range(2):
                        S.op("dve", lambda e: e.scalar_tensor_tensor(out=CK[tab][:], in0=PA[tab][:, 0:7, :], scalar=cf[:, 1, fc:fc + 1],
                                                                     in1=PA[tab][:, 1:8, :], op0=ALU.mult, op1=ALU.add),
                             reads=[b_PA[tab], b_cf], writes=[b_CK[tab]])
                    S.op("pool", lambda e: e.tensor_scalar(out=XN[:], in0=SX[1][:], scalar1=-1.0, scalar2=0.0, op0=ALU.mult, op1=ALU.add),
                         reads=[b_SX[1]], writes=[b_XN])
                    for tab in range(2):
                        pA, pbA = K.next_ps()
                        pB, pbB = K.next_ps()
                        n_ = 0
                        for j in range(4):
                            if tab == 0:
                                terms = ((CK[0], b_CK[0], SX[0], b_SX[0]), (CK[1], b_CK[1], XN, b_XN))
                            else:
                                terms = ((CK[0], b_CK[0], SX[1], b_SX[1]), (CK[1], b_CK[1], SX[0], b_SX[0]))
                            for (ca, bca, xa, bxa) in terms:
                                cav = ca[:, 3 - j:7 - j, :]
                                xav = xa[:, j:j + 1, :].to_broadcast([128, 4, 256])
                                tb_, btb_ = TB[n_ % 2], b_TB[n_ % 2]
                                e_ = "pool" if n_ == 7 else "dve"
                                S.op(e_, lambda e: e.tensor_tensor(out=tb_[:], in0=cav, in1=xav, op=ALU.mult), reads=[bca, bxa], writes=[btb_])
                                S.op("pe", lambda e: e.matmul(pA[:], lhsT=identb[:], rhs=tb_[:, 0:2, :].rearrange("p a b -> p (a b)"),
                                                              start=(n_ == 0), stop=(n_ == 7)), reads=[b_idb, btb_], writes=[pbA])
                                S.op("pe", lambda e: e.matmul(pB[:], lhsT=identb[:], rhs=tb_[:, 2:4, :].rearrange("p a b -> p (a b)"),
                                                              start=(n_ == 0), stop=(n_ == 7)), reads=[b_idb, btb_], writes=[pbB])
                                n_ += 1
                        S.op("act", lambda e: e.activation(out=Y[:, 0:2, fc, tab, :], in_=pA[:].rearrange("p (a b) -> p a b", a=2), func=AF.Copy),
                             reads=[pbA], writes=[b_Y])
                        S.op("act", lambda e: e.activation(out=Y[:, 2:4, fc, tab, :], in_=pB[:].rearrange("p (a b) -> p a b", a=2), func=AF.Copy),
                             reads=[pbB], writes=[b_Y])
                for fc in range(FC):
                    emit_tr(fc)
                    if fc > 0:
                        emit_mt(fc - 1)
                emit_mt(FC - 1)
                for i2 in range(2):
                    for a in range(PT):
                        ps, pb = K.next_ps()
                        n = 0
                        for fc in range(FC):
                            for tab in range(2):
                                S.op("pe", lambda e: e.matmul(ps[:].rearrange("p (n c) -> p n c", n=2), lhsT=TAB[:, fc, tab, a * 128:(a + 1) * 128],
                                                              rhs=Y[:, 2 * i2:2 * i2 + 2, fc, tab, :], start=(n == 0), stop=(n == 2 * FC - 1)),
                                     reads=[b_TAB, b_Y], writes=[pb])
                                n += 1
                        for i in (2 * i2, 2 * i2 + 1):
                            ti = i * PT + a
                            k = i % 2
                            S.dma(lambda q: q.dma_start(out=xg[k][:], in_=K.sc_u[ti * 128:(ti + 1) * 128, 256 * (1 + o):256 * (2 + o)]),
                                  reads=[K.b_scr], writes=[b_xg[k]])
                            if o == 0:
                                S.op("dve", lambda e: e.tensor_tensor(out=U[:, ti, :], in0=ps[:, (i % 2) * 256:(i % 2 + 1) * 256], in1=xg[k][:], op=ALU.mult),
                                     reads=[pb, b_xg[k]], writes=[b_U])
                            else:
                                for m_ in range(2):
                                    S.op("dve", lambda e: e.scalar_tensor_tensor(out=yo[m_][:], in0=ps[:, (i % 2) * 256:(i % 2 + 1) * 256],
                                                                                 scalar=rmask[:, m_:m_ + 1], in1=xg[k][:], op0=ALU.mult, op1=ALU.mult),
                                         reads=[pb, b_xg[k], b_rm], writes=[b_yo[m_]])
                                    bst = Buf()
                                    st_bufs[i2].append(bst)
                                    S.dma(lambda q: q.dma_start(out=K.sc_yhy[ti * 128:(ti + 1) * 128, m_ * 256:(m_ + 1) * 256], in_=yo[m_][:]),
                                          reads=[b_yo[m_]], writes=[K.b_scr, bst])
                    if o == 1:
                        S.coll(lambda g: g.collective_compute("AllReduce", ALU.add, replica_groups=[[0, 1], [2, 3], [4, 5], [6, 7]],
                                                              ins=[K.sc_yhy[i2 * 2048:(i2 + 1) * 2048, :]], outs=[K.sc_yhy2[i2 * 2048:(i2 + 1) * 2048, :]]),
                               reads=st_bufs[i2], writes=[K.b_yhy2[i2]])
        S.barrier()


def phase_e(K):
    nc, S, I = K.nc, K.S, K.I
    K.sc_x1 = K.scratch("sc_x1", [L, D])
    K.sc_hn2 = K.scratch("sc_hn2", [L, D], BF16)
    K.aff = K.sb("aff", [128, NT, NE]); K.b_aff = Buf()
    with ExitStack() as ph:
        sb = lambda name, shape, dt=F32: K.sb("e_" + name, shape, dt, stack=ph)
        wo = sb("wo", [128, 8, D], BF16); b_wo = Buf()
        S.dma(lambda q: q.dma_start(out=wo[:], in_=I["w_out"].rearrange("(kc p) j -> p kc j", p=128)), writes=[b_wo], q="pool")
        wr = sb("wr", [128, 8, NE]); b_wr = Buf()
        S.dma(lambda q: q.dma_start(out=wr[:], in_=I["w_router"].rearrange("(kc p) j -> p kc j", p=128)), writes=[b_wr])
        md = sb("md", [128, 3, D]); b_md = Buf()
        S.dma(lambda q: q.dma_start(out=md[:], in_=K.sc_mod[:, 2 * D:5 * D].rearrange("p (a d) -> p a d", a=3)),
              reads=[K.b_scmod], writes=[b_md])
        ym = [sb("ym%d" % i, [128, D]) for i in range(2)]; b_ym = [Buf(), Buf()]
        yT = [sb("yT%d" % i, [128, 8, 128], BF16) for i in range(2)]; b_yT = [Buf(), Buf()]
        xt = [sb("xt%d" % i, [128, D]) for i in range(2)]; b_xt = [Buf(), Buf()]
        pt = [sb("pt%d" % i, [128, D]) for i in range(2)]; b_pt = [Buf(), Buf()]
        hn = [sb("hn%d" % i, [128, D]) for i in range(2)]; b_hn = [Buf(), Buf()]
        hb = [sb("hb%d" % i, [128, D], BF16) for i in range(2)]; b_hb = [Buf(), Buf()]
        hT = [sb("hT%d" % i, [128, 8, 128]) for i in range(2)]; b_hT = [Buf(), Buf()]
        junk = sb("junk", [128, D]); b_junk = Buf()
        st = sb("st", [128, 2 * NT]); b_st = Buf()
        lg = sb("lg", [128, NT, NE]); b_lg = Buf()
        def e_load(ti):
            k = ti % 2
            rows = slice(ti * 128, (ti + 1) * 128)
            S.dma(lambda q: q.dma_start(out=ym[k][:, 0:512], in_=K.sc_ygla2[rows, :]), reads=[K.b_ygla2[ti // 16]], writes=[b_ym[k]])
            S.dma(lambda q: q.dma_start(out=ym[k][:, 512:1024], in_=K.sc_yhy2[rows, :]), reads=[K.b_yhy2[ti // 16]], writes=[b_ym[k]])
            S.dma(lambda q: q.dma_start(out=xt[k][:], in_=I["x"][rows, :]), writes=[b_xt[k]])
            S.dma(lambda q: q.dma_start(out=pt[k][:], in_=I["pos"][rows, :]), writes=[b_pt[k]])

        e_load(0)
        for ti in range(NT):
            k = ti % 2
            rows = slice(ti * 128, (ti + 1) * 128)
            if ti + 1 < NT:
                e_load(ti + 1)
            S.op("pool", lambda e: e.tensor_tensor(out=xt[k][:], in0=xt[k][:], in1=pt[k][:], op=ALU.add), reads=[b_pt[k]], writes=[b_xt[k]])
            for hh in range(2):
                ps, pb = K.next_ps()
                for j in range(4):
                    kc = hh * 4 + j
                    S.op("pe", lambda e: e.transpose(out=ps[:, j * 128:(j + 1) * 128], in_=ym[k][:, kc * 128:(kc + 1) * 128],
                                                     identity=K.identt[:]), reads=[b_ym[k], K.b_ident], writes=[pb])
                S.op("act", lambda e: e.activation(out=yT[k][:, hh * 4:hh * 4 + 4, :], in_=ps[:].rearrange("p (a b) -> p a b", a=4),
                                                   func=AF.Copy), reads=[pb], writes=[b_yT[k]])
            for half in range(2):
                hs = slice(half * 512, (half + 1) * 512)
                ps, pb = K.next_ps()
                for kc in range(8):
                    S.op("pe", lambda e: e.matmul(ps[:], lhsT=yT[k][:, kc, :], rhs=wo[:, kc, hs], start=(kc == 0), stop=(kc == 7)),
                         reads=[b_yT[k], b_wo], writes=[pb])
                S.op("dve", lambda e: e.tensor_tensor(out=hn[k][:, hs], in0=ps[:], in1=md[:, 0, hs], op=ALU.mult),
                     reads=[pb, b_md], writes=[b_hn[k]])
                S.op("pool", lambda e: e.tensor_tensor(out=xt[k][:, hs], in0=xt[k][:, hs], in1=hn[k][:, hs], op=ALU.add),
                     reads=[b_hn[k]], writes=[b_xt[k]])
            S.dma(lambda q: q.dma_start(out=K.sc_x1[rows, :], in_=xt[k][:]), reads=[b_xt[k]], writes=[K.b_scr])
            ss = st[:, 2 * ti:2 * ti + 1]
            rs = st[:, 2 * ti + 1:2 * ti + 2]
            S.op("act", lambda e: e.activation(out=junk[:], in_=xt[k][:], func=AF.Square, accum_out=ss), reads=[b_xt[k]], writes=[b_junk, b_st])
            S.op("act", lambda e: e.activation(out=rs, in_=ss, func=AF.Sqrt, scale=1.0 / D, bias=EPS), reads=[b_st], writes=[b_st])
            S.op("dve", lambda e: e.reciprocal(out=rs, in_=rs), reads=[b_st], writes=[b_st])
            S.op("dve", lambda e: e.scalar_tensor_tensor(out=hn[k][:], in0=xt[k][:], scalar=rs, in1=md[:, 2, :], op0=ALU.mult, op1=ALU.mult),
                 reads=[b_xt[k], b_st, b_md], writes=[b_hn[k]])
            S.op("pool", lambda e: e.tensor_tensor(out=hn[k][:], in0=hn[k][:], in1=md[:, 1, :], op=ALU.add), reads=[b_md], writes=[b_hn[k]])
            S.op("act", lambda e: e.activation(out=hb[k][:], in_=hn[k][:], func=AF.Copy), reads=[b_hn[k]], writes=[b_hb[k]])
            S.dma(lambda q: q.dma_start(out=K.sc_hn2[rows, :], in_=hb[k][:]), reads=[b_hb[k]], writes=[K.b_scr])
            for hh in range(2):
                ps, pb = K.next_ps()
                for j in range(4):
                    kc = hh * 4 + j
                    S.op("pe", lambda e: e.transpose(out=ps[:, j * 128:(j + 1) * 128], in_=hn[k][:, kc * 128:(kc + 1) * 128],
                                                     identity=K.identt[:]), reads=[b_hn[k], K.b_ident], writes=[pb])
                S.op("act", lambda e: e.activation(out=hT[k][:, hh * 4:hh * 4 + 4, :], in_=ps[:].rearrange("p (a b) -> p a b", a=4),
                                                   func=AF.Copy), reads=[pb], writes=[b_hT[k]])
            ps, pb = K.next_ps()
            for kc in range(8):
                S.op("pe", lambda e: e.matmul(ps[:, 0:NE], lhsT=hT[k][:, kc, :], rhs=wr[:, kc, :], start=(kc == 0), stop=(kc == 7)),
                     reads=[b_hT[k], b_wr], writes=[pb])
            S.op("act", lambda e: e.activation(out=lg[:, ti, :], in_=ps[:, 0:NE], func=AF.Copy), reads=[pb], writes=[b_lg])
        mx = sb("mx", [128, NT]); sm = sb("sm", [128, NT]); b_mx = Buf()
        S.op("dve", lambda e: e.tensor_reduce(out=mx[:], in_=lg[:], axis=AX.X, op=ALU.max), reads=[b_lg], writes=[b_mx])
        S.op("dve", lambda e: e.tensor_tensor(out=lg[:], in0=lg[:], in1=mx[:].unsqueeze(2).to_broadcast([128, NT, NE]), op=ALU.subtract),
             reads=[b_mx], writes=[b_lg])
        S.op("act", lambda e: e.activation(out=lg[:], in_=lg[:], func=AF.Exp), reads=[], writes=[b_lg])
        S.op("dve", lambda e: e.tensor_reduce(out=sm[:], in_=lg[:], axis=AX.X, op=ALU.add), reads=[b_lg], writes=[b_mx])
        S.op("dve", lambda e: e.reciprocal(out=sm[:], in_=sm[:]), reads=[], writes=[b_mx])
        S.op("dve", lambda e: e.tensor_tensor(out=K.aff[:], in0=lg[:], in1=sm[:].unsqueeze(2).to_broadcast([128, NT, NE]), op=ALU.mult),
             reads=[b_lg, b_mx], writes=[K.b_aff])
        K.tap("aff", K.aff[:], K.b_aff, [128, NT, NE])
        S.barrier()


def phase_f(K):
    nc, S, I = K.nc, K.S, K.I
    U32 = mybir.dt.uint32
    aff, b_aff = K.aff, K.b_aff
    ZR = NE * CAP
    b_R = Buf()
    K.sc_moe = K.scratch("sc_moe", [L, D], BF16)
    K.sc_moe2 = K.scratch("sc_moe2", [L, D], BF16)
    b_moe2 = [Buf() for _ in range(4)]
    with ExitStack() as ph:
        sb = lambda name, shape, dt=F32: K.sb("f_" + name, shape, dt, stack=ph)
        ones = sb("ones", [128, 128]); onesb = sb("onesb", [128, 128], BF16); b_one = Buf()
        S.op("dve", lambda e: e.memset(ones[:], 1.0), writes=[b_one])
        S.op("dve", lambda e: e.memset(onesb[:], 1.0), writes=[b_one])
        identb = sb("identb", [128, 128], BF16); b_idb = Buf()
        S.op("act", lambda e: e.activation(out=identb[:], in_=K.identt[:], func=AF.Copy), reads=[K.b_ident], writes=[b_idb])
        zt = sb("zt", [128, 8, D], BF16); b_zt = Buf()
        S.op("pool", lambda e: e.memset(zt[:], 0.0), writes=[b_zt])
        for zi in range(4):
            S.dma(lambda q: q.dma_start(out=K.sc_moe[zi * 1024:(zi + 1) * 1024, :].rearrange("(a p) d -> p a d", p=128), in_=zt[:]),
                  reads=[b_zt], writes=[b_R])
        ltri = sb("ltri", [128, 128], BF16); iotaJ = sb("iotaJ", [128, 512]); eoff = sb("eoff", [128, NE]); b_cst = Buf()
        S.dma(lambda q: q.dma_start(out=ltri[:], in_=I["ltri"]), writes=[b_cst])
        S.dma(lambda q: q.dma_start(out=iotaJ[:], in_=I["iotaJ"]), writes=[b_cst])
        S.dma(lambda q: q.dma_start(out=eoff[:], in_=I["eoff"]), writes=[b_cst])
        rhs5 = sb("rhs5", [128, NT, NE, 5], BF16); b_r5 = Buf()
        tidxt = sb("tidxt", [128, NT, 2], BF16); b_tidx = Buf()
        S.dma(lambda q: q.dma_start(out=tidxt[:], in_=I["tidx"]), writes=[b_tidx])
        S.op("dve", lambda e: e.tensor_copy(out=rhs5[:, :, :, 0:2], in_=tidxt[:].unsqueeze(2).to_broadcast([128, NT, NE, 2])),
             reads=[b_tidx], writes=[b_r5])
        lo = sb("lo", [128, NE]); posm = sb("posm", [128, NT, NE])
        g5 = sb("g5", [128, D]); nfg = sb("nfg", [128, D]); b_g5 = Buf()
        S.dma(lambda q: q.dma_start(out=g5[:], in_=K.sc_mod[:, 5 * D:6 * D]), reads=[K.b_scmod], writes=[b_g5])
        S.dma(lambda q: q.dma_start(out=nfg[:], in_=I["norm_final_g"].partition_broadcast(128)), writes=[b_g5])
        pht = ExitStack()
        sb_main = sb
        sb = lambda name, shape, dt=F32: K.sb("ft_" + name, shape, dt, stack=pht)
        r1 = sb("r1", [128, NT, NE]); r2 = sb("r2", [128, NT, NE]); b_r = Buf()
        S.op("act", lambda e: e.activation(out=rhs5[:, :, :, 2], in_=aff[:], func=AF.Copy), reads=[b_aff], writes=[b_r5])
        S.op("dve", lambda e: e.tensor_tensor(out=r1[:], in0=aff[:], in1=rhs5[:, :, :, 2], op=ALU.subtract), reads=[b_aff, b_r5], writes=[b_r])
        S.op("act", lambda e: e.activation(out=rhs5[:, :, :, 3], in_=r1[:], func=AF.Copy), reads=[b_r], writes=[b_r5])
        S.op("dve", lambda e: e.tensor_tensor(out=r2[:], in0=r1[:], in1=rhs5[:, :, :, 3], op=ALU.subtract), reads=[b_r, b_r5], writes=[b_r])
        S.op("act", lambda e: e.activation(out=rhs5[:, :, :, 4], in_=r2[:], func=AF.Copy), reads=[b_r], writes=[b_r5])
        hi = sb("hi", [128, NE]); mid = sb("mid", [128, NE]); cntp = sb("cntp", [128, NE])
        cond = sb("cond", [128, NE], U32); ncond = sb("ncond", [128, NE], U32)
        cmp_ = sb("cmp", [128, NT, NE])
        b_lo, b_hi, b_mid, b_cnt, b_cond, b_cmp = Buf(), Buf(), Buf(), Buf(), Buf(), Buf()
        S.op("dve", lambda e: e.memset(lo[:], 0.0), writes=[b_lo])
        S.op("dve", lambda e: e.memset(hi[:], 1.0), writes=[b_hi])
        for it in range(36):
            S.op("dve", lambda e: e.tensor_tensor(out=mid[:], in0=lo[:], in1=hi[:], op=ALU.add), reads=[b_lo, b_hi], writes=[b_mid])
            S.op("dve", lambda e: e.tensor_scalar(out=mid[:], in0=mid[:], scalar1=0.5, scalar2=None, op0=ALU.mult), reads=[], writes=[b_mid])
            S.op("dve", lambda e: e.tensor_tensor(out=cmp_[:], in0=aff[:], in1=mid[:].unsqueeze(1).to_broadcast([128, NT, NE]), op=ALU.is_ge),
                 reads=[b_aff, b_mid], writes=[b_cmp])
            S.op("dve", lambda e: e.tensor_reduce(out=cntp[:], in_=cmp_[:].rearrange("p t e -> p e t"), axis=AX.X, op=ALU.add),
                 reads=[b_cmp], writes=[b_cnt])
            ps, pb = K.next_ps()
            S.op("pe", lambda e: e.matmul(ps[:, 0:NE], lhsT=ones[:], rhs=cntp[:], start=True, stop=True), reads=[b_one, b_cnt], writes=[pb])
            S.op("dve", lambda e: e.tensor_scalar(out=cond[:], in0=ps[:, 0:NE], scalar1=float(CAP), scalar2=None, op0=ALU.is_ge),
                 reads=[pb], writes=[b_cond])
            S.op("dve", lambda e: e.tensor_scalar(out=ncond[:], in0=ps[:, 0:NE], scalar1=float(CAP), scalar2=None, op0=ALU.is_lt),
                 reads=[pb], writes=[b_cond])
            S.op("dve", lambda e: e.copy_predicated(out=lo[:], mask=cond[:], data=mid[:]), reads=[b_cond, b_mid], writes=[b_lo])
            S.op("dve", lambda e: e.copy_predicated(out=hi[:], mask=ncond[:], data=mid[:]), reads=[b_cond, b_mid], writes=[b_hi])
        msk = sb("msk", [128, NT, NE]); mskb = sb("mskb", [128, NT, NE], BF16)
        off = sb("off", [128, NT, NE]); b_m, b_pos, b_off = Buf(), Buf(), Buf()
        S.op("dve", lambda e: e.tensor_tensor(out=msk[:], in0=aff[:], in1=lo[:].unsqueeze(1).to_broadcast([128, NT, NE]), op=ALU.is_ge),
             reads=[b_aff, b_lo], writes=[b_m])
        S.op("act", lambda e: e.activation(out=mskb[:], in_=msk[:], func=AF.Copy), reads=[b_m], writes=[b_m])
        psw, pbw = K.next_ps()
        S.op("pe", lambda e: e.matmul(psw[:], lhsT=ltri[:], rhs=mskb[:].rearrange("p t e -> p (t e)"), start=True, stop=True),
             reads=[b_cst, b_m], writes=[pbw])
        pst, pbt = K.next_ps()
        S.op("pe", lambda e: e.matmul(pst[:], lhsT=onesb[:], rhs=mskb[:].rearrange("p t e -> p (t e)"), start=True, stop=True),
             reads=[b_one, b_m], writes=[pbt])
        tot = sb("tot", [128, NT, NE])
        S.op("act", lambda e: e.activation(out=tot[:].rearrange("p t e -> p (t e)"), in_=pst[:], func=AF.Copy), reads=[pbt], writes=[b_off])
        S.op("dve", lambda e: e.memset(off[:, 0, :], 0.0), writes=[b_off])
        for ti in range(NT - 1):
            S.op("dve", lambda e: e.tensor_tensor(out=off[:, ti + 1, :], in0=off[:, ti, :], in1=tot[:, ti, :], op=ALU.add),
                 reads=[], writes=[b_off])
        S.op("dve", lambda e: e.tensor_tensor(out=posm[:].rearrange("p t e -> p (t e)"), in0=psw[:], in1=off[:].rearrange("p t e -> p (t e)"),
                                              op=ALU.add), reads=[pbw, b_off], writes=[b_pos])
        S.op("dve", lambda e: e.scalar_tensor_tensor(out=posm[:], in0=posm[:], scalar=1.0, in1=msk[:], op0=ALU.add, op1=ALU.mult),
             reads=[b_m], writes=[b_pos])
        S.op("dve", lambda e: e.tensor_scalar(out=posm[:], in0=posm[:], scalar1=-1.0, scalar2=None, op0=ALU.add), reads=[], writes=[b_pos])
        K.tap("posm", posm[:], b_pos, [128, NT, NE])
        S.barrier()
        pht.close()
        phx = ExitStack()
        sb = lambda name, shape, dt=F32: K.sb("fx_" + name, shape, dt, stack=phx)
        xs = sb("xs", [128, 4, D], BF16); b_xsr = [Buf() for _ in range(4)]
        idx8 = sb("idx8", [128, 4, 5]); idxf = sb("idxf", [128, 4]); idxu = sb("idxu", [128, 4], U32); valj = sb("valj", [128, 4]); b_idx = Buf()
        idxu2 = [sb("idxu2_%d" % i, [128, 4], U32) for i in range(2)]; b_idx2 = [Buf(), Buf()]
        Sg = [sb("Sg%d" % i, [128, 512], BF16) for i in range(3)]; b_Sg = [Buf() for _ in range(3)]
        xsT = sb("xsT", [128, 8, 512], BF16); b_xs = Buf()
        hidT = sb("hidT", [128, 8, 512], BF16); b_hid = Buf()
        Yw = [sb("Yw%d" % i, [128, 4, D], BF16) for i in range(2)]; b_Yw = [Buf(), Buf()]
        wg = [sb("wg%d" % i, [128, 8, 128], BF16) for i in range(2)]; b_wg = [Buf(), Buf()]
        wu = [sb("wu%d" % i, [128, 8, 128], BF16) for i in range(2)]; b_wu = [Buf(), Buf()]
        wd = [sb("wd%d" % i, [128, D], BF16) for i in range(2)]; b_wd = [Buf(), Buf()]
        sgt = [sb("sgt%d" % i, [128, 512]) for i in range(2)]; b_sgt = [Buf(), Buf()]
        for ex in range(NE // 2):
            psi, pbi = K.next_ps()
            for ti in range(NT):
                k = ti % 3
                S.op("dve", lambda e: e.tensor_scalar(out=Sg[k][:], in0=iotaJ[:], scalar1=posm[:, ti, ex:ex + 1], scalar2=None, op0=ALU.is_equal),
                     reads=[b_cst, b_pos], writes=[b_Sg[k]])
                for jc in range(4):
                    S.op("pe", lambda e: e.matmul(psi[:, 5 * jc:5 * jc + 5], lhsT=Sg[k][:, jc * 128:(jc + 1) * 128], rhs=rhs5[:, ti, ex, :],
                                                  start=(ti == 0 and jc == 0), stop=(ti == NT - 1 and jc == 3)),
                         reads=[b_Sg[k], b_r5], writes=[pbi])
            S.op("act", lambda e: e.activation(out=idx8[:].rearrange("p a b -> p (a b)"), in_=psi[:, 0:20], func=AF.Copy), reads=[pbi], writes=[b_idx])
            S.op("dve", lambda e: e.scalar_tensor_tensor(out=idxf[:].unsqueeze(2), in0=idx8[:, :, 0:1], scalar=64.0, in1=idx8[:, :, 1:2],
                                                         op0=ALU.mult, op1=ALU.add), reads=[], writes=[b_idx])
            S.op("dve", lambda e: e.tensor_copy(out=idxu[:], in_=idxf[:]), reads=[], writes=[b_idx])
            S.op("dve", lambda e: e.tensor_copy(out=idxu2[ex % 2][:], in_=idxf[:]), reads=[b_idx], writes=[b_idx2[ex % 2]])
            S.op("dve", lambda e: e.tensor_tensor(out=valj[:].unsqueeze(2), in0=idx8[:, :, 3:4], in1=idx8[:, :, 4:5], op=ALU.add), reads=[], writes=[b_idx])
            S.op("dve", lambda e: e.tensor_tensor(out=valj[:].unsqueeze(2), in0=valj[:].unsqueeze(2), in1=idx8[:, :, 2:3], op=ALU.add), reads=[], writes=[b_idx])
            for jc in range(4):
                S.dma(lambda q: q.indirect_dma_start(out=xs[:, jc, :], out_offset=None, in_=K.sc_hn2,
                                                     in_offset=bass.IndirectOffsetOnAxis(ap=idxu[:, jc:jc + 1], axis=0)),
                      reads=[b_idx, K.b_scr], writes=[b_xsr[jc]], q="pool")
            for jc in range(4):
                ps, pb = K.next_ps()
                pv = ps[:].bitcast(BF16)
                for dc in range(8):
                    S.op("pe", lambda e: e.transpose(out=pv[:, dc * 128:(dc + 1) * 128], in_=xs[:, jc, dc * 128:(dc + 1) * 128], identity=identb[:]),
                         reads=[b_xsr[jc], b_idb], writes=[pb])
                S.op("act" if jc % 2 else "dve",
                     (lambda e: e.activation(out=xsT[:, :, jc * 128:(jc + 1) * 128], in_=pv.rearrange("p (a b) -> p a b", a=8), func=AF.Copy)) if jc % 2 else
                     (lambda e: e.tensor_copy(out=xsT[:, :, jc * 128:(jc + 1) * 128], in_=pv.rearrange("p (a b) -> p a b", a=8))),
                     reads=[pb], writes=[b_xs])
            for fc in range(8):
                k = fc % 2
                S.dma(lambda q: q.dma_start(out=wg[k][:], in_=I["w_gate"][ex].rearrange("(kc p) f -> p kc f", p=128)[:, :, fc * 128:(fc + 1) * 128]),
                      writes=[b_wg[k]], q="pool")
                S.dma(lambda q: q.dma_start(out=wu[k][:], in_=I["w_up"][ex].rearrange("(kc p) f -> p kc f", p=128)[:, :, fc * 128:(fc + 1) * 128]),
                      writes=[b_wu[k]], q="pool")
                psg, pbg = K.next_ps()
                for dc in range(8):
                    S.op("pe", lambda e: e.matmul(psg[:], lhsT=wg[k][:, dc, :], rhs=xsT[:, dc, :], start=(dc == 0), stop=(dc == 7)),
                         reads=[b_wg[k], b_xs], writes=[pbg])
                psu, pbu = K.next_ps()
                for dc in range(8):
                    S.op("pe", lambda e: e.matmul(psu[:], lhsT=wu[k][:, dc, :], rhs=xsT[:, dc, :], start=(dc == 0), stop=(dc == 7)),
                         reads=[b_wu[k], b_xs], writes=[pbu])
                S.op("act", lambda e: e.activation(out=sgt[k][:], in_=psg[:], func=AF.Silu), reads=[pbg], writes=[b_sgt[k]])
                S.op("dve", lambda e: e.tensor_tensor(out=hidT[:, fc, :], in0=psu[:], in1=sgt[k][:], op=ALU.mult), reads=[pbu, b_sgt[k]], writes=[b_hid])
            banks = [K.next_ps() for _ in range(8)]
            for fc in range(8):
                k = fc % 2
                S.dma(lambda q: q.dma_start(out=wd[k][:], in_=I["w_down"][ex, fc * 128:(fc + 1) * 128, :]), writes=[b_wd[k]], q="pool")
                for jc in range(4):
                    for half in range(2):
                        ps, pb = banks[jc * 2 + half]
                        S.op("pe", lambda e: e.matmul(ps[:], lhsT=hidT[:, fc, jc * 128:(jc + 1) * 128], rhs=wd[k][:, half * 512:(half + 1) * 512],
                                                      start=(fc == 0), stop=(fc == 7)), reads=[b_hid, b_wd[k]], writes=[pb])
            yw, byw = Yw[ex % 2], b_Yw[ex % 2]
            for jc in range(4):
                for half in range(2):
                    ps, pb = banks[jc * 2 + half]
                    S.op("dve", lambda e: e.scalar_tensor_tensor(out=yw[:, jc, half * 512:(half + 1) * 512], in0=ps[:], scalar=valj[:, jc:jc + 1],
                                                                 in1=g5[:, half * 512:(half + 1) * 512], op0=ALU.mult, op1=ALU.mult),
                         reads=[pb, b_idx, b_g5], writes=[byw])
            for jc in range(4):
                S.dma(lambda q: q.indirect_dma_start(out=K.sc_moe, out_offset=bass.IndirectOffsetOnAxis(ap=idxu2[ex % 2][:, jc:jc + 1], axis=0),
                                                     in_=yw[:, jc, :], in_offset=None, compute_op=ALU.add),
                      reads=[byw, b_idx2[ex % 2]], writes=[b_R], q="pool")
        S.barrier()
        phx.close()
        sb = sb_main
        for ch in range(4):
            S.coll(lambda g: g.collective_compute("AllReduce", ALU.add, replica_groups=[[0, 1], [2, 3], [4, 5], [6, 7]],
                                                  ins=[K.sc_moe[ch * 1024:(ch + 1) * 1024, :]], outs=[K.sc_moe2[ch * 1024:(ch + 1) * 1024, :]]),
                   reads=[b_R], writes=[b_moe2[ch]])
        x1t = [sb("x1t%d" % i, [128, D]) for i in range(3)]; b_x1t = [Buf(), Buf(), Buf()]
        mt = [sb("mt%d" % i, [128, D], BF16) for i in range(3)]; b_mt = [Buf(), Buf(), Buf()]
        junk = sb("junk", [128, D]); b_junk = Buf()
        st = sb("st", [128, 2 * NT]); b_st = Buf()
        def f_load(ti):
            k = ti % 3
            rows = slice(ti * 128, (ti + 1) * 128)
            S.dma(lambda q: q.dma_start(out=x1t[k][:], in_=K.sc_x1[rows, :]), reads=[K.b_scr], writes=[b_x1t[k]])
            S.dma(lambda q: q.dma_start(out=mt[k][:], in_=K.sc_moe2[rows, :]), reads=[b_moe2[ti // 8]], writes=[b_mt[k]])

        f_load(0)
        f_load(1)
        for ti in range(NT):
            k = ti % 3
            rows = slice(ti * 128, (ti + 1) * 128)
            if ti + 2 < NT:
                f_load(ti + 2)
            S.op("pool", lambda e: e.tensor_tensor(out=x1t[k][:], in0=x1t[k][:], in1=mt[k][:], op=ALU.add), reads=[b_mt[k]], writes=[b_x1t[k]])
            ss = st[:, 2 * ti:2 * ti + 1]
            rs = st[:, 2 * ti + 1:2 * ti + 2]
            S.op("act", lambda e: e.activation(out=junk[:], in_=x1t[k][:], func=AF.Square, accum_out=ss), reads=[b_x1t[k]], writes=[b_junk, b_st])
            S.op("act", lambda e: e.activation(out=rs, in_=ss, func=AF.Sqrt, scale=1.0 / D, bias=EPS), reads=[b_st], writes=[b_st])
            S.op("dve", lambda e: e.reciprocal(out=rs, in_=rs), reads=[b_st], writes=[b_st])
            S.op("dve", lambda e: e.scalar_tensor_tensor(out=x1t[k][:], in0=x1t[k][:], scalar=rs, in1=nfg[:], op0=ALU.mult, op1=ALU.mult),
                 reads=[b_st, b_g5], writes=[b_x1t[k]])
            S.dma(lambda q: q.dma_start(out=K.out[rows, :], in_=x1t[k][:]), reads=[b_x1t[k]])
        S.barrier()
```
